# Optimizing a Trainium2 kernel written in Bass

```python
import math
import jax
import jax.numpy as jnp
from jax import lax
import numpy as np

D_MODEL = 1024
BATCH = 2
SEQ = 16384
DEPTH = 4

N_EVEN = (DEPTH + 1) // 2
N_ODD = DEPTH // 2
NORM_EPS = 1e-6

SSD_HEAD_DIM = 64
SSD_HEADS = 16
SSD_GROUPS = 2
SSD_HEADS_PER_GROUP = SSD_HEADS // SSD_GROUPS
SSD_STATE = 128
SSD_CHUNK = 128
CONV_WIDTH = 4
D_SSM = SSD_HEADS * SSD_HEAD_DIM
CONV_DIM = D_SSM + 2 * SSD_GROUPS * SSD_STATE

GLA_HEADS = 4
GLA_KEY_DIM = D_MODEL // 2
GLA_VAL_DIM = D_MODEL
GLA_HEAD_K = GLA_KEY_DIM // GLA_HEADS
GLA_HEAD_V = GLA_VAL_DIM // GLA_HEADS
GLA_GATE_RANK = 16
GLA_GATE_NORM = 16.0
GLA_CHUNK = 64

IN_SIZES = (D_SSM, CONV_DIM, SSD_HEADS, GLA_KEY_DIM, GLA_KEY_DIM, GLA_VAL_DIM, GLA_VAL_DIM, GLA_GATE_RANK)
IN_COLS = sum(IN_SIZES)
MIX_WIDTH = D_SSM + GLA_VAL_DIM

MLA_HEADS = 16
MLA_NOPE = 64
MLA_ROPE = 32
MLA_V = 64
MLA_Q_RANK = 384
MLA_KV_RANK = 256
ROPE_THETA = 10000.0
ATTN_BLOCK = 128

FFN_HIDDEN = -(-(8 * D_MODEL) // (3 * 256)) * 256

kernel_name = 'hybrid_ssd_gla_mla_trunk'


def _offsets(sizes):
    out, acc = [], 0
    for s in sizes[:-1]:
        acc += s
        out.append(acc)
    return out


def rms_norm(x, w):
    xf = x.astype(jnp.float32)
    y = xf * lax.rsqrt(jnp.mean(xf * xf, axis=-1, keepdims=True) + NORM_EPS)
    return (y * w.astype(jnp.float32)).astype(x.dtype)


def causal_depthwise_conv(x, w, b):
    c = x.shape[-1]
    y = lax.conv_general_dilated(x, w[:, None, :].astype(x.dtype), window_strides=(1,),
                                 padding=[(CONV_WIDTH - 1, 0)],
                                 dimension_numbers=('NWC', 'WIO', 'NWC'),
                                 feature_group_count=c)
    return y + b


def ssd_mixer(z, xbc, dt_raw, conv_w, conv_b, dt_bias, a_log, d_skip, norm_w):
    bsz, seqlen, _ = z.shape
    G, K, P, N, Q = SSD_GROUPS, SSD_HEADS_PER_GROUP, SSD_HEAD_DIM, SSD_STATE, SSD_CHUNK
    nc = seqlen // Q
    xbc = jax.nn.silu(causal_depthwise_conv(xbc, conv_w, conv_b))
    xs, bm, cm = jnp.split(xbc, [D_SSM, D_SSM + G * N], axis=-1)
    xs = xs.reshape(bsz, nc, Q, G, K, P)
    bm = bm.reshape(bsz, nc, Q, G, N)
    cm = cm.reshape(bsz, nc, Q, G, N)
    dt = jax.nn.softplus(dt_raw + dt_bias).reshape(bsz, nc, Q, G, K)
    a = dt * (-jnp.exp(a_log)).reshape(G, K)
    a_cum = jnp.cumsum(a, axis=2)
    xdt = xs * dt[..., None]
    causal = jnp.tril(jnp.ones((Q, Q), dtype=bool))
    seg = a_cum[:, :, :, None] - a_cum[:, :, None, :]
    decay = jnp.exp(jnp.where(causal[:, :, None, None], seg, -jnp.inf))
    cb = jnp.einsum('bclgn,bcsgn->bclsg', cm, bm)
    y_diag = jnp.einsum('bclsg,bclsgk,bcsgkp->bclgkp', cb, decay, xdt)
    to_end = jnp.exp(a_cum[:, :, -1:] - a_cum)
    states = jnp.einsum('bcsgn,bcsgk,bcsgkp->bcgkpn', bm, to_end, xdt)
    chunk_decay = jnp.exp(a_cum[:, :, -1])

    def step(h, inp):
        s_c, d_c = inp
        return h * d_c[..., None, None] + s_c, h

    h0 = jnp.zeros((bsz, G, K, P, N), states.dtype)
    _, prev = lax.scan(step, h0, (jnp.moveaxis(states, 1, 0), jnp.moveaxis(chunk_decay, 1, 0)))
    prev = jnp.moveaxis(prev, 0, 1)
    y_off = jnp.einsum('bclgn,bcgkpn,bclgk->bclgkp', cm, prev, jnp.exp(a_cum))
    y = y_diag + y_off + xs * d_skip.reshape(G, K)[:, :, None]
    y = y.reshape(bsz, seqlen, D_SSM) * jax.nn.silu(z)
    gsz = D_SSM // G
    y = rms_norm(y.reshape(bsz, seqlen, G, gsz), norm_w.reshape(G, gsz))
    return y.reshape(bsz, seqlen, D_SSM)


def gla_mixer(q, k, v, g, gk_low, gk_w2, gk_b, norm_w):
    bsz, seqlen, _ = q.shape
    H, DK, DV, Q = GLA_HEADS, GLA_HEAD_K, GLA_HEAD_V, GLA_CHUNK
    nc = seqlen // Q
    dtype = q.dtype
    gk = jax.nn.log_sigmoid((gk_low @ gk_w2 + gk_b).astype(jnp.float32)) / GLA_GATE_NORM
    gcum = jnp.cumsum(gk.reshape(bsz, nc, Q, H, DK), axis=2)
    g_last = gcum[:, :, -1]
    q = q.reshape(bsz, nc, Q, H, DK) * (DK ** -0.5)
    k = k.reshape(bsz, nc, Q, H, DK)
    v = v.reshape(bsz, nc, Q, H, DV)
    q_dec = q * jnp.exp(gcum).astype(dtype)
    k_inv = k * jnp.exp(-gcum).astype(dtype)
    k_end = k * jnp.exp(g_last[:, :, None] - gcum).astype(dtype)
    causal = jnp.tril(jnp.ones((Q, Q), dtype=bool))
    scores = jnp.where(causal, jnp.einsum('bcihd,bcjhd->bchij', q_dec, k_inv), 0)
    o_intra = jnp.einsum('bchij,bcjhv->bcihv', scores, v)
    chunk_kv = jnp.einsum('bcjhd,bcjhv->bchdv', k_end, v)
    chunk_decay = jnp.exp(g_last).astype(dtype)

    def step(s, inp):
        kv_c, d_c = inp
        return s * d_c[..., None] + kv_c, s

    s0 = jnp.zeros((bsz, H, DK, DV), chunk_kv.dtype)
    _, prev = lax.scan(step, s0, (jnp.moveaxis(chunk_kv, 1, 0), jnp.moveaxis(chunk_decay, 1, 0)))
    prev = jnp.moveaxis(prev, 0, 1)
    o_inter = jnp.einsum('bcihd,bchdv->bcihv', q_dec, prev)
    o = (o_intra + o_inter).reshape(bsz, seqlen, H, DV)
    o = rms_norm(o, norm_w) * jax.nn.silu(g.reshape(bsz, seqlen, H, DV))
    return o.reshape(bsz, seqlen, GLA_VAL_DIM)


def ssd_gla_layer(h, w_in, conv_w, conv_b, dt_bias, a_log, d_skip, ssd_norm,
                  gla_gk_w2, gla_gk_b, gla_norm, w_out):
    bsz, seqlen, _ = h.shape
    proj = h @ w_in
    z, xbc, dt_raw, q, k, v, g, gk_low = jnp.split(proj, _offsets(IN_SIZES), axis=-1)
    y_ssd = ssd_mixer(z, xbc, dt_raw, conv_w, conv_b, dt_bias, a_log, d_skip, ssd_norm)
    y_gla = gla_mixer(q, k, v, g, gk_low, gla_gk_w2, gla_gk_b, gla_norm)
    return jnp.concatenate([y_ssd, y_gla], axis=-1) @ w_out


def rope_tables(seqlen):
    pos = jnp.arange(seqlen, dtype=jnp.float32)
    inv = 1.0 / (ROPE_THETA ** (jnp.arange(0, MLA_ROPE, 2, dtype=jnp.float32) / MLA_ROPE))
    ang = pos[:, None] * inv[None, :]
    return jnp.cos(ang), jnp.sin(ang)


def apply_rope(x, cos, sin):
    x1, x2 = jnp.split(x, 2, axis=-1)
    cos = cos.astype(x.dtype)
    sin = sin.astype(x.dtype)
    return jnp.concatenate([x1 * cos - x2 * sin, x2 * cos + x1 * sin], axis=-1)


def mla_layer(h, w_dqkv, q_lora_norm, w_uq, kv_lora_norm, w_ukv,
              q_nope_norm, q_rope_norm, k_nope_norm, k_rope_norm, w_o):
    bsz, seqlen, _ = h.shape
    H = MLA_HEADS
    cq, ckv, k_rope = jnp.split(h @ w_dqkv, [MLA_Q_RANK, MLA_Q_RANK + MLA_KV_RANK], axis=-1)
    q = (rms_norm(cq, q_lora_norm) @ w_uq).reshape(bsz, seqlen, H, MLA_NOPE + MLA_ROPE)
    q_nope, q_rope = jnp.split(q, [MLA_NOPE], axis=-1)
    kv = (rms_norm(ckv, kv_lora_norm) @ w_ukv).reshape(bsz, seqlen, H, MLA_NOPE + MLA_V)
    k_nope, v = jnp.split(kv, [MLA_NOPE], axis=-1)
    cos, sin = rope_tables(seqlen)
    q_nope = rms_norm(q_nope, q_nope_norm)
    q_rope = apply_rope(rms_norm(q_rope, q_rope_norm), cos[:, None, :], sin[:, None, :])
    k_nope = rms_norm(k_nope, k_nope_norm)
    k_rope = apply_rope(rms_norm(k_rope, k_rope_norm), cos, sin)
    scale = (MLA_NOPE + MLA_ROPE) ** -0.5
    kpos = jnp.arange(seqlen)

    def block(i):
        start = i * ATTN_BLOCK
        qn = lax.dynamic_slice_in_dim(q_nope, start, ATTN_BLOCK, axis=1)
        qr = lax.dynamic_slice_in_dim(q_rope, start, ATTN_BLOCK, axis=1)
        s = jnp.einsum('bqhd,bkhd->bhqk', qn, k_nope) + jnp.einsum('bqhr,bkr->bhqk', qr, k_rope)
        s = s.astype(jnp.float32) * scale
        qpos = start + jnp.arange(ATTN_BLOCK)
        mask = kpos[None, :] <= qpos[:, None]
        p = jax.nn.softmax(jnp.where(mask, s, -jnp.inf), axis=-1).astype(v.dtype)
        return jnp.einsum('bhqk,bkhv->bqhv', p, v)

    o = lax.map(block, jnp.arange(seqlen // ATTN_BLOCK))
    o = jnp.moveaxis(o, 0, 1).reshape(bsz, seqlen, H * MLA_V)
    return o @ w_o


def swiglu(h, w_gate, w_up, w_down):
    return (jax.nn.silu(h @ w_gate) * (h @ w_up)) @ w_down


def setup_inputs(seed: int = 0) -> dict:
    key = jax.random.key(seed)
    keys = iter(jax.random.split(key, 40))

    def nrm(shape, scale):
        return jax.random.normal(next(keys), shape, jnp.float32) * scale

    def gain(shape):
        return 1.0 + nrm(shape, 0.02)

    NE, NO = N_EVEN, N_ODD
    x = nrm((BATCH, SEQ, D_MODEL), 1.0)
    u = jax.random.uniform(next(keys), (NE, SSD_HEADS), jnp.float32)
    dt0 = jnp.exp(u * (math.log(0.1) - math.log(0.001)) + math.log(0.001))
    dt_bias = dt0 + jnp.log(-jnp.expm1(-dt0))
    a_log = jnp.log(jax.random.uniform(next(keys), (NE, SSD_HEADS), jnp.float32, 1.0, 16.0))
    return {
        'x': x,
        'mix_norm_even': gain((NE, D_MODEL)),
        'w_in_even': nrm((NE, D_MODEL, IN_COLS), D_MODEL ** -0.5),
        'conv_w': nrm((NE, CONV_WIDTH, CONV_DIM), CONV_WIDTH ** -0.5),
        'conv_b': nrm((NE, CONV_DIM), 0.02),
        'dt_bias': dt_bias,
        'a_log': a_log,
        'd_skip': 1.0 + nrm((NE, SSD_HEADS), 0.1),
        'ssd_norm': gain((NE, D_SSM)),
        'gla_gk_w2': nrm((NE, GLA_GATE_RANK, GLA_KEY_DIM), GLA_GATE_RANK ** -0.5),
        'gla_gk_b': nrm((NE, GLA_KEY_DIM), 0.1),
        'gla_norm': gain((NE, GLA_HEAD_V)),
        'w_out_even': nrm((NE, MIX_WIDTH, D_MODEL), MIX_WIDTH ** -0.5),
        'mix_norm_odd': gain((NO, D_MODEL)),
        'w_dqkv': nrm((NO, D_MODEL, MLA_Q_RANK + MLA_KV_RANK + MLA_ROPE), D_MODEL ** -0.5),
        'q_lora_norm': gain((NO, MLA_Q_RANK)),
        'w_uq': nrm((NO, MLA_Q_RANK, MLA_HEADS * (MLA_NOPE + MLA_ROPE)), MLA_Q_RANK ** -0.5),
        'kv_lora_norm': gain((NO, MLA_KV_RANK)),
        'w_ukv': nrm((NO, MLA_KV_RANK, MLA_HEADS * (MLA_NOPE + MLA_V)), MLA_KV_RANK ** -0.5),
        'q_nope_norm': gain((NO, MLA_NOPE)),
        'q_rope_norm': gain((NO, MLA_ROPE)),
        'k_nope_norm': gain((NO, MLA_NOPE)),
        'k_rope_norm': gain((NO, MLA_ROPE)),
        'w_o_mla': nrm((NO, MLA_HEADS * MLA_V, D_MODEL), (MLA_HEADS * MLA_V) ** -0.5),
        'ffn_norm': gain((DEPTH, D_MODEL)),
        'w_gate': nrm((DEPTH, D_MODEL, FFN_HIDDEN), D_MODEL ** -0.5),
        'w_up': nrm((DEPTH, D_MODEL, FFN_HIDDEN), D_MODEL ** -0.5),
        'w_down': nrm((DEPTH, FFN_HIDDEN, D_MODEL), FFN_HIDDEN ** -0.5),
    }


def reference(x, mix_norm_even, w_in_even, conv_w, conv_b, dt_bias, a_log, d_skip, ssd_norm,
              gla_gk_w2, gla_gk_b, gla_norm, w_out_even,
              mix_norm_odd, w_dqkv, q_lora_norm, w_uq, kv_lora_norm, w_ukv,
              q_nope_norm, q_rope_norm, k_nope_norm, k_rope_norm, w_o_mla,
              ffn_norm, w_gate, w_up, w_down):
    for i in range(DEPTH):
        j = i // 2
        if i % 2 == 0:
            h = rms_norm(x, mix_norm_even[j])
            x = x + ssd_gla_layer(h, w_in_even[j], conv_w[j], conv_b[j], dt_bias[j], a_log[j],
                                  d_skip[j], ssd_norm[j], gla_gk_w2[j], gla_gk_b[j], gla_norm[j],
                                  w_out_even[j])
        else:
            h = rms_norm(x, mix_norm_odd[j])
            x = x + mla_layer(h, w_dqkv[j], q_lora_norm[j], w_uq[j], kv_lora_norm[j], w_ukv[j],
                              q_nope_norm[j], q_rope_norm[j], k_nope_norm[j], k_rope_norm[j],
                              w_o_mla[j])
        x = x + swiglu(rms_norm(x, ffn_norm[i]), w_gate[i], w_up[i], w_down[i])
    return x
```

```python
import numpy as np
from contextlib import ExitStack
import concourse.bass as bass
import concourse.mybir as mybir
from concourse.bass_utils import run_bass_kernel_spmd

F32 = mybir.dt.float32
BF16 = mybir.dt.bfloat16
ALU = mybir.AluOpType
AF = mybir.ActivationFunctionType
AX = mybir.AxisListType

D = 1024
FH = 2816
EPS = 1e-6
ENGS = ("pe", "act", "dve", "pool", "sp")


class Res:
    __slots__ = ("w", "r")

    def __init__(self):
        self.w = None
        self.r = {}


class Dom:
    __slots__ = ("sem", "inc", "count", "waitall")

    def __init__(self, sem, inc):
        self.sem, self.inc, self.count = sem, inc, 0
        self.waitall = False


class Buf:
    def __init__(self, t, nres=1):
        self.t = t
        self.res = [Res() for _ in range(nres)]

    def __getitem__(self, idx):
        return self.t[idx]

    @property
    def r(self):
        return self.res[0]


class Ctx:
    def __init__(self, nc, stack):
        self.nc, self.stack = nc, stack
        self.root = stack
        self.prog = {e: [] for e in ENGS}
        self.dom = {}
        for e in ("pe", "act", "dve", "pool"):
            self.dom[e] = Dom(stack.enter_context(nc.semaphore("s_" + e)), 1)
        self.waited = {e: {} for e in ENGS}
        self.dma_doms = {}
        self.nbuf = 0
        self.psum_banks = []
        self.psum_i = 0

    def sb(self, shape, dtype, nres=1, name=None):
        self.nbuf += 1
        t = self.stack.enter_context(self.nc.sbuf_tensor(name or ("b%d" % self.nbuf), list(shape), dtype))
        return Buf(t, nres)

    def init_psum(self):
        for i in range(8):
            t = self.stack.enter_context(self.nc.psum_tensor("ps%d" % i, [128, 512], F32))
            self.psum_banks.append(Buf(t))

    def psum(self):
        b = self.psum_banks[self.psum_i % 6]
        self.psum_i += 1
        return b

    def psum_acc(self, i):
        return self.psum_banks[6 + i]

    def dma_dom(self, key):
        if key not in self.dma_doms:
            sem = self.root.enter_context(self.nc.semaphore("d%d" % len(self.dma_doms)))
            self.dma_doms[key] = Dom(sem, 16)
            self.dma_doms[key].waitall = isinstance(key, str)
        return self.dma_doms[key]

    def op(self, e, fn, reads=(), writes=(), dom=None):
        deps = {}

        def add(dm, v):
            if dm.waitall:
                v = dm.count
            if deps.get(dm, 0) < v:
                deps[dm] = v

        for r in reads:
            if r.w is not None:
                add(*r.w)
        for w in writes:
            if w.w is not None:
                add(*w.w)
            for dm, v in w.r.items():
                add(dm, v)
        own = self.dom.get(e)
        for dm, v in deps.items():
            if e == "pe" and dm is own:
                continue
            if self.waited[e].get(dm, 0) >= v:
                continue
            self.waited[e][dm] = v
            self.prog[e].append(lambda eng, s=dm.sem, v=v: eng.wait_ge(s, v))
        dm = dom if dom is not None else own
        dm.count += dm.inc
        self.prog[e].append(lambda eng, s=dm.sem, i=dm.inc: fn(eng).then_inc(s, i))
        for r in reads:
            if r.r.get(dm, 0) < dm.count:
                r.r[dm] = dm.count
        for w in writes:
            w.w = (dm, dm.count)
            w.r = {}

    def dma(self, e, out, in_, reads, writes, key):
        self.dma_op(e, lambda eng: eng.dma_start(out=out, in_=in_), reads, writes, key)

    def dma_op(self, e, fn, reads, writes, key):
        dm = self.dma_dom(key)
        if dm.waitall and dm.count > 0 and self.waited[e].get(dm, 0) < dm.count:
            self.waited[e][dm] = dm.count
            self.prog[e].append(lambda eng, s=dm.sem, v=dm.count: eng.wait_ge(s, v))
        self.op(e, fn, reads, writes, dom=dm)

    def wait_all(self, e, ress):
        for r in ress:
            if r.w is not None:
                dm, v = r.w
                if self.waited[e].get(dm, 0) < v:
                    self.waited[e][dm] = v
                    self.prog[e].append(lambda eng, s=dm.sem, v=v: eng.wait_ge(s, v))

    def barrier(self):
        doms = list(self.dom.values()) + list(self.dma_doms.values())
        for e in ENGS:
            for dm in doms:
                if dm.count > 0 and self.waited[e].get(dm, 0) < dm.count:
                    self.waited[e][dm] = dm.count
                    self.prog[e].append(lambda eng, s=dm.sem, v=dm.count: eng.wait_ge(s, v))

    def flush(self):
        nc = self.nc
        prog = self.prog
        self.prog = {e: [] for e in ENGS}
        with nc.Block() as block:
            @block.tensor
            def _(eng):
                for t in prog["pe"]:
                    t(eng)

            @block.scalar
            def _(eng):
                for t in prog["act"]:
                    t(eng)

            @block.vector
            def _(eng):
                for t in prog["dve"]:
                    t(eng)

            @block.gpsimd
            def _(eng):
                for t in prog["pool"]:
                    t(eng)

            @block.sync
            def _(eng):
                for t in prog["sp"]:
                    t(eng)


class Phase:
    def __init__(self, c):
        self.c = c

    def __enter__(self):
        self.outer = self.c.stack
        self.st = ExitStack()
        self.c.stack = self.st
        return self

    def __exit__(self, *a):
        self.c.barrier()
        self.c.flush()
        self.c.stack = self.outer
        self.st.close()
        return False


def mm(c, ps_ap, lhsT, rhs, start, stop, reads, writes):
    c.op("pe", lambda pe: pe.matmul(ps_ap, lhsT, rhs, start=start, stop=stop), reads, writes)


def tr(c, ps_ap, in_ap, ident_ap, reads, writes):
    c.op("pe", lambda pe: pe.transpose(ps_ap, in_ap, ident_ap), reads, writes)


def v_tt(c, out, in0, in1, op, reads, writes, e="dve"):
    c.op(e, lambda v: v.tensor_tensor(out=out, in0=in0, in1=in1, op=op), reads, writes)


def v_ts(c, out, in0, s1, s2, op0, op1, reads, writes, e="dve"):
    if op1 is None:
        c.op(e, lambda v: v.tensor_scalar(out=out, in0=in0, scalar1=s1, scalar2=None, op0=op0), reads, writes)
    else:
        c.op(e, lambda v: v.tensor_scalar(out=out, in0=in0, scalar1=s1, scalar2=s2, op0=op0, op1=op1), reads, writes)


def v_stt(c, out, in0, scalar, in1, op0, op1, reads, writes, e="dve"):
    c.op(e, lambda v: v.scalar_tensor_tensor(out=out, in0=in0, scalar=scalar, in1=in1, op0=op0, op1=op1),
         reads, writes)


def v_copy(c, out, in_, reads, writes, e="dve"):
    c.op(e, lambda v: v.tensor_copy(out=out, in_=in_), reads, writes)


def v_red(c, out, in_, reads, writes):
    c.op("dve", lambda v: v.reduce_sum(out=out, in_=in_, axis=AX.X), reads, writes)


def v_recip(c, out, in_, reads, writes):
    c.op("dve", lambda v: v.reciprocal(out=out, in_=in_), reads, writes)


def a_act(c, out, in_, func, reads, writes, **kw):
    c.op("act", lambda a: a.activation(out=out, in_=in_, func=func, **kw), reads, writes)


def rstd_inplace(c, ss_ap, n, res):
    a_act(c, ss_ap, ss_ap, AF.Sqrt, [res], [res], scale=1.0 / n, bias=EPS)
    v_recip(c, ss_ap, ss_ap, [res], [res])


def rmsnorm(c, xt_ap, P, n, wb_ap, out_ap, junk, ss, reads, writes, wres):
    c.op("act", lambda a: a.activation(out=junk[0:P, 0:n], in_=xt_ap, func=AF.Square, accum_out=ss[0:P, 0:1]),
         reads, [junk.r, ss.r])
    c.op("act", lambda a: a.activation(out=ss[0:P, 0:1], in_=ss[0:P, 0:1], func=AF.Sqrt, scale=1.0 / n, bias=EPS),
         [ss.r], [ss.r])
    c.op("dve", lambda v: v.reciprocal(out=ss[0:P, 0:1], in_=ss[0:P, 0:1]), [ss.r], [ss.r])
    c.op("dve", lambda v: v.scalar_tensor_tensor(out=out_ap, in0=xt_ap, scalar=ss[0:P, 0:1], in1=wb_ap,
                                                 op0=ALU.mult, op1=ALU.mult),
         list(reads) + [ss.r, wres], writes)


class Prog:
    def __init__(self, T, layers, test=None):
        self.T = T
        self.layers = layers
        nc = self.nc = bass.Bass("TRN2", target_bir_lowering=False)
        self.stack = ExitStack()
        self.c = Ctx(nc, self.stack)
        self.c.init_psum()
        self.ext = {}

    def inp(self, name, shape, dtype=F32):
        t = self.nc.dram_tensor(name, list(shape), dtype, kind="ExternalInput")
        self.ext[name] = t
        return t

    def setup_consts(self):
        c = self.c
        ident_d = self.inp("c_ident", [128, 128])
        idf = c.sb([128, 128], F32)
        self.ident = c.sb([128, 128], BF16)
        c.dma("sp", idf[:], ident_d.ap()[:, :], [], [idf.r], "const")
        c.op("dve", lambda v: v.tensor_copy(out=self.ident[:], in_=idf[:]), [idf.r], [self.ident.r])
        self.junk = c.sb([128, 1024], BF16)
        self.ss = c.sb([128, 4], F32)

    def ffn_phase(self, L, src, src_res, dst, dst_res, wg_d, wu_d, wd_d, nw_d):
        c, T = self.c, self.T
        NT = 256
        wg = c.sb([128, 8, FH], BF16, nres=8)
        wu = c.sb([128, 8, FH], BF16, nres=8)
        wd = c.sb([128, 22, D], BF16, nres=22)
        nw = c.sb([128, D], F32)
        c.dma("sp", nw[:], nw_d[L, :].partition_broadcast(128), [], [nw.r], "ffn_nw")
        for k in range(8):
            c.dma("pool", wg[:, k, :], wg_d[L, k * 128:(k + 1) * 128, :], [], [wg.res[k]], "w")
            c.dma("pool", wu[:, k, :], wu_d[L, k * 128:(k + 1) * 128, :], [], [wu.res[k]], "w")
        for h in range(22):
            c.dma("pool", wd[:, h, :], wd_d[L, h * 128:(h + 1) * 128, :], [], [wd.res[h]], "w")
        xt = [c.sb([128, D], F32) for _ in range(2)]
        xo = [c.sb([128, D], F32) for _ in range(2)]
        hb = c.sb([128, D], BF16)
        hT = c.sb([128, 8, NT], BF16)
        sg = c.sb([128, NT], F32)
        hid = c.sb([128, 22, NT], BF16, nres=22)
        for st in range(T // NT):
            for j in range(2):
                ti = st * 2 + j
                c.dma("sp", xt[j][:], src[ti * 128:(ti + 1) * 128, :], [], [xt[j].r], ("ffn_x", j))
                rmsnorm(c, xt[j][:], 128, D, nw[:], hb[:], self.junk, self.ss, [xt[j].r], [hb.r], nw.r)
                pT = c.psum()
                pTv = pT[:].bitcast(BF16)
                for k in range(8):
                    tr(c, pTv[:, k * 128:(k + 1) * 128], hb[:, k * 128:(k + 1) * 128], self.ident[:],
                       [hb.r, self.ident.r], [pT.r])
                c.op("act", lambda a, j=j, pTv=pTv: a.copy(out=hT[:, :, j * 128:(j + 1) * 128],
                                                         in_=pTv.rearrange("p (k t) -> p k t", k=8)),
                     [pT.r], [hT.r])
            for h in range(22):
                pg = c.psum()
                pu = c.psum()
                for k in range(8):
                    mm(c, pg[:, 0:NT], wg[:, k, h * 128:(h + 1) * 128], hT[:, k, :], k == 0, k == 7,
                       [wg.res[k], hT.r], [pg.r])
                for k in range(8):
                    mm(c, pu[:, 0:NT], wu[:, k, h * 128:(h + 1) * 128], hT[:, k, :], k == 0, k == 7,
                       [wu.res[k], hT.r], [pu.r])
                c.op("act", lambda a, pg=pg: a.activation(out=sg[:], in_=pg[:, 0:NT], func=AF.Silu),
                     [pg.r], [sg.r])
                c.op("dve", lambda v, pu=pu, h=h: v.tensor_tensor(out=hid[:, h, :], in0=sg[:], in1=pu[:, 0:NT],
                                                                  op=ALU.mult),
                     [sg.r, pu.r], [hid.res[h]])
            for j in range(2):
                ti = st * 2 + j
                for n in range(2):
                    po = c.psum()
                    for h in range(22):
                        mm(c, po[:, :], hid[:, h, j * 128:(j + 1) * 128], wd[:, h, n * 512:(n + 1) * 512],
                           h == 0, h == 21, [hid.res[h], wd.res[h]], [po.r])
                    c.op("dve", lambda v, po=po, j=j, n=n: v.tensor_tensor(
                        out=xo[j][:, n * 512:(n + 1) * 512], in0=po[:, :], in1=xt[j][:, n * 512:(n + 1) * 512],
                        op=ALU.add), [po.r, xt[j].r], [xo[j].r])
                c.dma("sp", dst[ti * 128:(ti + 1) * 128, :], xo[j][:], [xo[j].r], [], ("ffn_o", j))


    def declare(self, depth):
        ne, no = (depth + 1) // 2, depth // 2
        T = self.T
        shapes = {
            "x": [T, D], "pos": [T], "rmask": [128, 4], "hsel": [128, 4],
            "c_ident": [128, 128], "c_inv": [16], "c_kpos": [128, 4 * T // 128], "c_triu": [128, 128],
            "mix_norm_even": [ne, D], "w_in_even": [ne, D, 5664], "conv_w": [ne, 4, 1536], "conv_b": [ne, 1536],
            "dt_bias": [ne, 16], "a_log": [ne, 16], "d_skip": [ne, 16], "ssd_norm": [ne, 1024],
            "gla_gk_w2": [ne, 16, 512], "gla_gk_b": [ne, 512], "gla_norm": [ne, 256], "w_out_even": [ne, 2048, D],
            "mix_norm_odd": [max(no, 1), D], "w_dqkv": [max(no, 1), D, 672], "q_lora_norm": [max(no, 1), 384],
            "w_uq": [max(no, 1), 384, 1536], "kv_lora_norm": [max(no, 1), 256], "w_ukv": [max(no, 1), 256, 2048],
            "q_nope_norm": [max(no, 1), 64], "q_rope_norm": [max(no, 1), 32], "k_nope_norm": [max(no, 1), 64],
            "k_rope_norm": [max(no, 1), 32], "w_o_mla": [max(no, 1), 1024, D],
            "ffn_norm": [depth, D], "w_gate": [depth, D, FH], "w_up": [depth, D, FH], "w_down": [depth, FH, D],
        }
        self.d = {k: self.inp(k, v).ap() for k, v in shapes.items()}
        self.y = self.nc.dram_tensor("y", [T, D], F32, kind="ExternalOutput").ap()
        nc = self.nc
        NB = 4 * T
        self.xs = nc.dram_tensor("xs", [T, D], F32).ap()
        self.NCH = max(1, (288 * T * 2 + 786431) // 786432)
        while T % (self.NCH * 512) != 0:
            self.NCH += 1
        self.TC = T // self.NCH
        self.lat_l = [nc.dram_tensor("lat_l%d" % q, [288, self.TC], BF16) for q in range(self.NCH)]
        self.lat_g = [nc.dram_tensor("lat_g%d" % q, [4 * 288, self.TC], BF16) for q in range(self.NCH)]
        self.qT_d = nc.dram_tensor("qT_d", [16, 96, T], BF16).ap()
        self.aT_d = nc.dram_tensor("aT_d", [16, 64, T], BF16).ap()
        self.KT_d = nc.dram_tensor("KT_d", [16, 96, NB], BF16).ap()
        self.V_d = nc.dram_tensor("V_d", [NB // 128, 128, 16, 65], BF16).ap()
        self.st_l = nc.dram_tensor("st_l", [128, 1040], F32)
        self.st_g = nc.dram_tensor("st_g", [4 * 128, 1040], F32)
        self.sg_l = nc.dram_tensor("sg_l", [128, 1040], F32)
        self.sg_g = nc.dram_tensor("sg_g", [4 * 128, 1040], F32)
        self.hl_l = nc.dram_tensor("hl_l", [16, D], F32)
        self.hl_g = nc.dram_tensor("hl_g", [4 * 16, D], F32)
        self.fn_l = nc.dram_tensor("fn_l", [16, 64], F32)
        self.fn_g = nc.dram_tensor("fn_g", [64, 64], F32)
        self.dly_a = nc.dram_tensor("dly_a", [128, 2048], F32).ap()
        self.dly_b = nc.dram_tensor("dly_b", [128, 2048], F32).ap()
        self.cc_dom = Dom(self.stack.enter_context(nc.semaphore("cc")), 1)
        self.groups = [[0, 1, 2, 3], [4, 5, 6, 7]]
        self.dres = Res()

    def bload(self, src_row_ap, n, key="bl"):
        b = self.c.sb([128, n], F32)
        self.c.dma("sp", b[:], src_row_ap.partition_broadcast(128), [], [b.r], key)
        return b

    def allgather(self, src_t, dst_t):
        c = self.c
        c.barrier()
        c.op("pool", lambda g: g.collective_compute("AllGather", ALU.bypass, replica_groups=self.groups,
                                                      ins=[src_t.ap().opt()], outs=[dst_t.ap().opt()]),
             [], [], dom=self.cc_dom)
        c.barrier()
        c.op("pool", lambda g: g.collective_compute("AllGather", ALU.bypass, replica_groups=self.groups,
                                                      ins=[self.fn_l.ap().opt()], outs=[self.fn_g.ap().opt()]),
             [], [], dom=self.cc_dom)
        c.barrier()
        for i in range(16):
            a, b = (self.dly_a, self.dly_b) if i % 2 == 0 else (self.dly_b, self.dly_a)
            c.dma("sp", b[:, :], a[:, :], [], [], "dly")
        c.barrier()

    def norm_tile_T(self, xt, nw, hb, hT, col0, P=128):
        c = self.c
        rmsnorm(c, xt[0:P, :], P, D, nw[0:P, :], hb[0:P, :], self.junk, self.ss, [xt.r], [hb.r], nw.r)
        pT = c.psum()
        pTv = pT[:].bitcast(BF16)
        for k in range(8):
            tr(c, pTv[:, k * P:(k + 1) * P], hb[0:P, k * 128:(k + 1) * 128], self.ident[0:P, 0:P],
               [hb.r, self.ident.r], [pT.r])
        c.op("act", lambda a: a.copy(out=hT[:, :, col0:col0 + P],
                                     in_=pTv[:, 0:8 * P].rearrange("p (k t) -> p k t", k=8)), [pT.r], [hT.r])

    def rope(self, x, Hn, cos_ap, sin_ap, out, ta, tb, xres, ores):
        c = self.c
        x1, x2 = x[:, :, 0:16], x[:, :, 16:32]
        cb = cos_ap.unsqueeze(1).to_broadcast([128, Hn, 16])
        sb_ = sin_ap.unsqueeze(1).to_broadcast([128, Hn, 16])
        a, b = ta[:, 0:Hn, :], tb[:, 0:Hn, :]
        v_tt(c, a, x1, cb, ALU.mult, [xres, self.cs.r, self.sn.r], [ta.r])
        v_tt(c, b, x2, sb_, ALU.mult, [xres, self.cs.r, self.sn.r], [tb.r])
        v_tt(c, out[:, :, 0:16], a, b, ALU.subtract, [ta.r, tb.r], [ores])
        v_tt(c, a, x2, cb, ALU.mult, [xres, self.cs.r, self.sn.r], [ta.r])
        v_tt(c, b, x1, sb_, ALU.mult, [xres, self.cs.r, self.sn.r], [tb.r])
        v_tt(c, out[:, :, 16:32], a, b, ALU.add, [ta.r, tb.r], [ores])

    def mla_m1(self, j, xs):
        c, T, d = self.c, self.T, self.d
        NTL = T // 128
        PI = float(np.pi)
        wdq = c.sb([128, 8, 672], BF16, nres=8)
        for k in range(8):
            c.dma("pool", wdq[:, k, :], d["w_dqkv"][j, k * 128:(k + 1) * 128, :], [], [wdq.res[k]], "w")
        wuq = c.sb([128, 3, 1536], BF16, nres=3)
        for k in range(3):
            c.dma("pool", wuq[:, k, :], d["w_uq"][j, k * 128:(k + 1) * 128, :], [], [wuq.res[k]], "w")
        nw = self.bload(d["mix_norm_odd"][j, :], D)
        qln = self.bload(d["q_lora_norm"][j, :], 384)
        kvln = self.bload(d["kv_lora_norm"][j, :], 256)
        gq = c.sb([128, 96], F32)
        c.dma("sp", gq[:, 0:64], d["q_nope_norm"][j, :].partition_broadcast(128), [], [gq.r], "bl")
        c.dma("sp", gq[:, 64:96], d["q_rope_norm"][j, :].partition_broadcast(128), [], [gq.r], "bl")
        v_ts(c, gq[:], gq[:], 96.0 ** -0.5, None, ALU.mult, None, [gq.r], [gq.r])
        gkr = self.bload(d["k_rope_norm"][j, :], 32)
        inv = self.bload(d["c_inv"], 16)
        posc = self.tload(d["pos"].rearrange("(n p) -> n p", p=128), NTL)
        ang = c.sb([128, NTL, 16], F32)
        self.cs = c.sb([128, NTL, 16], F32)
        self.sn = c.sb([128, NTL, 16], F32)
        v_tt(c, ang[:], inv[:].unsqueeze(1).to_broadcast([128, NTL, 16]),
             posc[:].unsqueeze(2).to_broadcast([128, NTL, 16]), ALU.mult, [inv.r, posc.r], [ang.r])
        kf = c.sb([128, NTL, 16], F32)
        msk = c.sb([128, NTL, 16], F32)
        MAGIC = 12582912.0
        C1 = 6.28125
        C2 = 2.0 * PI - C1
        PIC = 3.141592
        v_ts(c, kf[:], ang[:], 1.0 / (2.0 * PI), None, ALU.mult, None, [ang.r], [kf.r])
        v_ts(c, kf[:], kf[:], MAGIC, None, ALU.add, None, [kf.r], [kf.r])
        v_ts(c, kf[:], kf[:], -MAGIC, None, ALU.add, None, [kf.r], [kf.r])
        v_stt(c, ang[:], kf[:], -C1, ang[:], ALU.mult, ALU.add, [kf.r, ang.r], [ang.r])
        v_stt(c, ang[:], kf[:], -C2, ang[:], ALU.mult, ALU.add, [kf.r, ang.r], [ang.r])
        v_ts(c, self.sn[:], ang[:], PIC, -PIC, ALU.min, ALU.max, [ang.r], [self.sn.r])
        a_act(c, self.sn[:], self.sn[:], AF.Sin, [self.sn.r], [self.sn.r])
        v_ts(c, self.cs[:], ang[:], PI / 2, None, ALU.add, None, [ang.r], [self.cs.r])
        v_ts(c, msk[:], self.cs[:], PI, None, ALU.is_gt, None, [self.cs.r], [msk.r])
        v_stt(c, self.cs[:], msk[:], -2.0 * PI, self.cs[:], ALU.mult, ALU.add, [msk.r, self.cs.r], [self.cs.r])
        v_ts(c, self.cs[:], self.cs[:], PIC, -PIC, ALU.min, ALU.max, [self.cs.r], [self.cs.r])
        a_act(c, self.cs[:], self.cs[:], AF.Sin, [self.cs.r], [self.cs.r])
        xt = c.sb([128, D], F32)
        hb = c.sb([128, D], BF16)
        hT = c.sb([128, 8, 128], BF16)
        lat = c.sb([128, 672], F32)
        cqn = c.sb([128, 384], BF16)
        cqT = c.sb([128, 3, 128], BF16)
        ckvn = c.sb([128, 256], BF16)
        latT = c.sb([128, 2, 128], BF16)
        kr = c.sb([128, 1, 32], F32)
        krr = c.sb([128, 1, 32], BF16)
        krT = c.sb([32, 128], BF16)
        q = c.sb([128, 16, 96], F32)
        sq = c.sb([128, 16, 96], F32)
        ssn = c.sb([128, 16], F32)
        ssr = c.sb([128, 16], F32)
        qr = c.sb([128, 16, 32], F32)
        qf = c.sb([128, 16, 96], BF16)
        ta = c.sb([128, 16, 16], F32)
        tb = c.sb([128, 16, 16], F32)
        qT = c.sb([96, 16, 128], BF16)
        for ti in range(NTL):
            tsl = slice(ti * 128, (ti + 1) * 128)
            lq = (ti * 128) // self.TC
            lat_l = self.lat_l[lq].ap()
            lsl = slice(ti * 128 - lq * self.TC, ti * 128 - lq * self.TC + 128)
            c.dma("sp", xt[:], xs[tsl, :], [], [xt.r], "m1x")
            self.norm_tile_T(xt, nw, hb, hT, 0)
            pl0, pl1 = c.psum(), c.psum()
            for k in range(8):
                mm(c, pl0[:, :], hT[:, k, :], wdq[:, k, 0:512], k == 0, k == 7, [hT.r, wdq.res[k]], [pl0.r])
            for k in range(8):
                mm(c, pl1[:, 0:160], hT[:, k, :], wdq[:, k, 512:672], k == 0, k == 7, [hT.r, wdq.res[k]], [pl1.r])
            c.op("act", lambda a, p=pl0: a.copy(out=lat[:, 0:512], in_=p[:, :]), [pl0.r], [lat.r])
            c.op("act", lambda a, p=pl1: a.copy(out=lat[:, 512:672], in_=p[:, 0:160]), [pl1.r], [lat.r])
            rmsnorm(c, lat[:, 0:384], 128, 384, qln[:], cqn[:], self.junk, self.ss, [lat.r], [cqn.r], qln.r)
            pT = c.psum()
            pTv = pT[:].bitcast(BF16)
            for k in range(3):
                tr(c, pTv[:, k * 128:(k + 1) * 128], cqn[:, k * 128:(k + 1) * 128], self.ident[:],
                   [cqn.r, self.ident.r], [pT.r])
            c.op("act", lambda a, pTv=pTv: a.copy(out=cqT[:], in_=pTv[:, 0:384].rearrange("p (k t) -> p k t", k=3)),
                 [pT.r], [cqT.r])
            rmsnorm(c, lat[:, 384:640], 128, 256, kvln[:], ckvn[:], self.junk, self.ss, [lat.r], [ckvn.r], kvln.r)
            pT2 = c.psum()
            pT2v = pT2[:].bitcast(BF16)
            for k in range(2):
                tr(c, pT2v[:, k * 128:(k + 1) * 128], ckvn[:, k * 128:(k + 1) * 128], self.ident[:],
                   [ckvn.r, self.ident.r], [pT2.r])
            c.op("act", lambda a, p=pT2v: a.copy(out=latT[:], in_=p[:, 0:256].rearrange("p (k t) -> p k t", k=2)),
                 [pT2.r], [latT.r])
            c.dma("sp", lat_l[0:256, lsl].rearrange("(k p) t -> p k t", p=128), latT[:], [latT.r], [], "m1l")
            rmsnorm(c, lat[:, 640:672], 128, 32, gkr[:], kr[:, 0, :], self.junk, self.ss, [lat.r], [kr.r], gkr.r)
            self.rope(kr, 1, self.cs[:, ti, :], self.sn[:, ti, :], krr, ta, tb, kr.r, krr.r)
            pT3 = c.psum()
            pT3v = pT3[:].bitcast(BF16)
            tr(c, pT3v[0:32, 0:128], krr[:, 0, :], self.ident[:], [krr.r, self.ident.r], [pT3.r])
            c.op("act", lambda a, p=pT3v: a.copy(out=krT[:], in_=p[0:32, 0:128]), [pT3.r], [krT.r])
            c.dma("sp", lat_l[256:288, lsl], krT[:], [krT.r], [], "m1k")
            qfl = q[:].rearrange("p h d -> p (h d)")
            for n in range(3):
                pq = c.psum()
                for k in range(3):
                    mm(c, pq[:, :], cqT[:, k, :], wuq[:, k, n * 512:(n + 1) * 512], k == 0, k == 2,
                       [cqT.r, wuq.res[k]], [pq.r])
                c.op("act", lambda a, p=pq, n=n: a.copy(out=qfl[:, n * 512:(n + 1) * 512], in_=p[:, :]),
                     [pq.r], [q.r])
            a_act(c, sq[:], q[:], AF.Square, [q.r], [sq.r])
            v_red(c, ssn[:], sq[:, :, 0:64], [sq.r], [ssn.r])
            v_red(c, ssr[:], sq[:, :, 64:96], [sq.r], [ssr.r])
            rstd_inplace(c, ssn[:], 64, ssn.r)
            rstd_inplace(c, ssr[:], 32, ssr.r)
            v_tt(c, sq[:, :, 0:64], q[:, :, 0:64], ssn[:].unsqueeze(2).to_broadcast([128, 16, 64]), ALU.mult,
                 [q.r, ssn.r], [sq.r])
            v_tt(c, qf[:, :, 0:64], sq[:, :, 0:64], gq[:, 0:64].unsqueeze(1).to_broadcast([128, 16, 64]), ALU.mult,
                 [sq.r, gq.r], [qf.r])
            v_tt(c, sq[:, :, 64:96], q[:, :, 64:96], ssr[:].unsqueeze(2).to_broadcast([128, 16, 32]), ALU.mult,
                 [q.r, ssr.r], [sq.r])
            v_tt(c, qr[:], sq[:, :, 64:96], gq[:, 64:96].unsqueeze(1).to_broadcast([128, 16, 32]), ALU.mult,
                 [sq.r, gq.r], [qr.r])
            self.rope(qr, 16, self.cs[:, ti, :], self.sn[:, ti, :], qf[:, :, 64:96], ta, tb, qr.r, qf.r)
            for half in range(2):
                pt = c.psum()
                ptv = pt[:].bitcast(BF16)
                for hh in range(8):
                    tr(c, ptv[0:96, hh * 128:(hh + 1) * 128], qf[:, half * 8 + hh, :], self.ident[:],
                       [qf.r, self.ident.r], [pt.r])
                c.op("act", lambda a, p=ptv, half=half: a.copy(
                    out=qT[:, half * 8:half * 8 + 8, :], in_=p[0:96, :].rearrange("p (h t) -> p h t", h=8)),
                    [pt.r], [qT.r])
            c.dma("sp", self.qT_d[:, :, tsl].rearrange("h p t -> p h t"), qT[:], [qT.r], [], "m1q")

    def mla_m2(self, j):
        c, T, d = self.c, self.T, self.d
        NB = 4 * T
        wukv = c.sb([128, 2, 2048], BF16, nres=2)
        for k in range(2):
            c.dma("pool", wukv[:, k, :], d["w_ukv"][j, k * 128:(k + 1) * 128, :], [], [wukv.res[k]], "w")
        gk = self.bload(d["k_nope_norm"][j, :], 64)
        ckvT = [c.sb([128, 2, 512], BF16) for _ in range(2)]
        KTs = [c.sb([96, 16, 512], BF16) for _ in range(2)]
        vaug = [c.sb([128, 4, 16, 65], BF16) for _ in range(2)]
        for b in range(2):
            c.op("dve", lambda v, b=b: v.memset(vaug[b][:, :, :, 64:65], 1.0), [], [vaug[b].r])
        sq = c.sb([128, 4, 64], F32)
        ssk = c.sb([128, 16], F32)
        tmp = c.sb([128, 4, 64], F32)
        kn = c.sb([128, 16, 64], BF16)
        for st in range(NB // 512):
            b = st % 2
            r = (st * 512) // T
            t0 = st * 512 - r * T
            lq = t0 // self.TC
            lat_g = self.lat_g[lq].ap()
            t0 = t0 - lq * self.TC
            base = r * 288
            c.dma("sp", ckvT[b][:], lat_g[base:base + 256, t0:t0 + 512].rearrange("(k p) t -> p k t", p=128),
                  [], [ckvT[b].r], ("m2c", b))
            for h in range(16):
                c.dma("sp", KTs[b][64:96, h, :], lat_g[base + 256:base + 288, t0:t0 + 512], [], [KTs[b].r],
                      ("m2r", b))
            for blk in range(4):
                bs = slice(blk * 128, (blk + 1) * 128)
                pk = [c.psum() for _ in range(4)]
                for n in range(4):
                    for k in range(2):
                        mm(c, pk[n][:, :], ckvT[b][:, k, bs], wukv[:, k, n * 512:(n + 1) * 512], k == 0, k == 1,
                           [ckvT[b].r, wukv.res[k]], [pk[n].r])
                for n in range(4):
                    pv = pk[n][:, :].rearrange("p (h d) -> p h d", h=4)
                    a_act(c, sq[:], pv[:, :, 0:64], AF.Square, [pk[n].r], [sq.r])
                    v_red(c, ssk[:, 4 * n:4 * n + 4], sq[:], [sq.r], [ssk.r])
                    c.op("act", lambda a, pv=pv, n=n, blk=blk, b=b: a.copy(
                        out=vaug[b][:, blk, 4 * n:4 * n + 4, 0:64], in_=pv[:, :, 64:128]), [pk[n].r], [vaug[b].r])
                rstd_inplace(c, ssk[:], 64, ssk.r)
                for n in range(4):
                    pv = pk[n][:, :].rearrange("p (h d) -> p h d", h=4)
                    v_tt(c, tmp[:], pv[:, :, 0:64], ssk[:, 4 * n:4 * n + 4].unsqueeze(2).to_broadcast([128, 4, 64]),
                         ALU.mult, [pk[n].r, ssk.r], [tmp.r])
                    v_tt(c, kn[:, 4 * n:4 * n + 4, :], tmp[:], gk[:].unsqueeze(1).to_broadcast([128, 4, 64]),
                         ALU.mult, [tmp.r, gk.r], [kn.r])
                for half in range(2):
                    pt = c.psum()
                    ptv = pt[:].bitcast(BF16)
                    for hh in range(8):
                        tr(c, ptv[0:64, hh * 128:(hh + 1) * 128], kn[:, half * 8 + hh, :], self.ident[:],
                           [kn.r, self.ident.r], [pt.r])
                    c.op("act", lambda a, p=ptv, half=half, b=b, bs=bs: a.copy(
                        out=KTs[b][0:64, half * 8:half * 8 + 8, bs],
                        in_=p[0:64, :].rearrange("p (h t) -> p h t", h=8)), [pt.r], [KTs[b].r])
            c.dma("sp", self.KT_d[:, :, st * 512:(st + 1) * 512].rearrange("h p t -> p h t"), KTs[b][:],
                  [KTs[b].r], [], ("m2k", b))
            c.dma("sp", self.V_d[st * 4:(st + 1) * 4].rearrange("n p h c -> p n h c"), vaug[b][:],
                  [vaug[b].r], [], ("m2v", b))

    def mla_m3(self, j):
        c, T, d = self.c, self.T, self.d
        NB = 4 * T
        NBLK = NB // 128
        NG = T // 512
        KT = [c.sb([96, NB], BF16) for _ in range(2)]
        Vh = [c.sb([128, NBLK, 65], BF16) for _ in range(2)]
        qrow = self.bload(d["pos"], T)
        kcols = c.sb([128, NBLK], F32)
        c.dma("sp", kcols[:], d["c_kpos"][:, :], [], [kcols.r], "bl")
        ones = c.sb([128, 64], F32)
        c.op("dve", lambda v: v.memset(ones[:], 1.0), [], [ones.r])
        qT = [c.sb([96, 512], BF16) for _ in range(2)]
        pTs = [c.sb([128, 512], BF16) for _ in range(3)]
        rs = c.sb([128, 512], F32)
        rb = c.sb([64, 512], F32)
        oT = [c.sb([64, 512], BF16) for _ in range(2)]
        it = 0
        ig = 0
        for h in range(16):
            b = h % 2
            c.dma("sp", KT[b][:], self.KT_d[h], [], [KT[b].r], ("m3k", b))
            c.dma("sp", Vh[b][:], self.V_d[:, :, h, :].rearrange("n p c -> p n c"), [], [Vh[b].r], ("m3v", b))
            for g in range(NG):
                qb = qT[ig % 2]
                ob = oT[ig % 2]
                gs = slice(g * 512, (g + 1) * 512)
                c.dma("sp", qb[:], self.qT_d[h, :, gs], [], [qb.r], ("m3q", ig % 2))
                kmax = (3 * T + (g + 1) * 512) // 128
                po = c.psum_acc(ig % 2)
                for kb in range(kmax):
                    ps = c.psum()
                    mm(c, ps[:, :], KT[b][:, kb * 128:(kb + 1) * 128], qb[:], True, True, [KT[b].r, qb.r], [ps.r])
                    pT = pTs[it % 3]
                    it += 1
                    a_act(c, pT[:], ps[:, :], AF.Exp, [ps.r], [pT.r])
                    if kb * 128 + 127 > g * 512:
                        v_stt(c, pT[:], qrow[:, gs], kcols[:, kb:kb + 1], pT[:], ALU.is_ge, ALU.mult,
                              [pT.r, qrow.r, kcols.r], [pT.r])
                    mm(c, po[0:65, :], Vh[b][:, kb, :], pT[:], kb == 0, kb == kmax - 1, [Vh[b].r, pT.r], [po.r])
                c.op("act", lambda a, po=po: a.copy(out=rs[64:65, :], in_=po[64:65, :]), [po.r], [rs.r])
                v_recip(c, rs[64:65, :], rs[64:65, :], [rs.r], [rs.r])
                pb = c.psum()
                mm(c, pb[0:64, :], ones[64:65, 0:64], rs[64:65, :], True, True, [ones.r, rs.r], [pb.r])
                c.op("act", lambda a, pb=pb: a.copy(out=rb[:], in_=pb[0:64, :]), [pb.r], [rb.r])
                v_tt(c, ob[:], po[0:64, :], rb[:], ALU.mult, [po.r, rb.r], [ob.r])
                c.dma("sp", self.aT_d[h, :, gs], ob[:], [ob.r], [], ("m3o", ig % 2))
                ig += 1

    def mla_m4(self, j, xs, dst):
        c, T, d = self.c, self.T, self.d
        wo = c.sb([64, 16, D], BF16, nres=16)
        for h in range(16):
            c.dma("pool", wo[:, h, :], d["w_o_mla"][j, h * 64:(h + 1) * 64, :], [], [wo.res[h]], "w")
        aT = [c.sb([64, 16, 128], BF16) for _ in range(2)]
        xt = [c.sb([128, D], F32) for _ in range(2)]
        xo = [c.sb([128, D], F32) for _ in range(2)]
        for ti in range(T // 128):
            b = ti % 2
            tsl = slice(ti * 128, (ti + 1) * 128)
            c.dma("sp", aT[b][:], self.aT_d[:, :, tsl].rearrange("h p t -> p h t"), [], [aT[b].r], ("m4a", b))
            c.dma("sp", xt[b][:], xs[tsl, :], [], [xt[b].r], ("m4x", b))
            for n in range(2):
                po = c.psum()
                for h in range(16):
                    mm(c, po[:, :], aT[b][:, h, :], wo[:, h, n * 512:(n + 1) * 512], h == 0, h == 15,
                       [aT[b].r, wo.res[h]], [po.r])
                v_tt(c, xo[b][:, n * 512:(n + 1) * 512], po[:, :], xt[b][:, n * 512:(n + 1) * 512], ALU.add,
                     [po.r, xt[b].r], [xo[b].r])
            c.dma("sp", dst[tsl, :], xo[b][:], [xo[b].r], [], ("m4o", b))

    def mla_layer(self, j, xs):
        c = self.c
        with Phase(c):
            self.mla_m1(j, xs)
        with Phase(c):
            for q in range(self.NCH):
                self.allgather(self.lat_l[q], self.lat_g[q])
        import os
        with Phase(c):
            if os.environ.get("KDBG") == "2":
                dl0 = self.nc.dram_tensor("dbg_l0", [4 * 288, self.TC], BF16, kind="ExternalOutput").ap()
                c.dma("sp", dl0[:, :], self.lat_g[0].ap()[:, :], [], [], "dbg")
            self.mla_m2(j)
        with Phase(c):
            self.mla_m3(j)
        import os
        if os.environ.get("KDBG") == "2":
            with Phase(c):
                dbg = self.nc.dram_tensor("dbg_a", [16, 64, self.T], BF16, kind="ExternalOutput").ap()
                c.dma("sp", dbg[:, :, :], self.aT_d[:, :, :], [], [], "dbg")
                dbq = self.nc.dram_tensor("dbg_q", [16, 96, self.T], BF16, kind="ExternalOutput").ap()
                c.dma("sp", dbq[:, :, :], self.qT_d[:, :, :], [], [], "dbg")
                dbk = self.nc.dram_tensor("dbg_k", [16, 96, 4 * self.T], BF16, kind="ExternalOutput").ap()
                c.dma("sp", dbk[:, :, :], self.KT_d[:, :, :], [], [], "dbg")
                dbv = self.nc.dram_tensor("dbg_v", [4 * self.T // 128, 128, 16, 65], BF16, kind="ExternalOutput").ap()
                c.dma("sp", dbv[:, :, :, :], self.V_d[:, :, :, :], [], [], "dbg")
        with Phase(c):
            self.mla_m4(j, xs, xs)


    def ncload(self, shape, src_ap, key="bl"):
        b = self.c.sb(shape, F32)
        self.c.dma_op("sp", lambda e: e.dma_start(out=b[:], in_=src_ap, allow_slow_non_contiguous=True), [], [b.r], key)
        return b

    def tload(self, src_rows_ap, n):
        c = self.c
        rows = c.sb([128, 128], F32)
        c.dma("sp", rows[0:n, :], src_rows_ap, [], [rows.r], "bl")
        pt = c.psum()
        tr(c, pt[:, 0:n], rows[0:n, :], self.identf[0:n, 0:n], [rows.r, self.identf.r], [pt.r])
        out = c.sb([128, n], F32)
        c.op("act", lambda a: a.copy(out=out[:], in_=pt[:, 0:n]), [pt.r], [out.r])
        return out

    def wload(self, j, c0, c1, nk=8):
        c = self.c
        w = c.sb([128, nk, c1 - c0], BF16, nres=nk)
        for k in range(nk):
            c.dma("pool", w[:, k, :], self.d["w_in_even"][j, k * 128:(k + 1) * 128, c0:c1], [], [w.res[k]], "w")
        return w

    def halo_phase(self, xs):
        c, T = self.c, self.T
        with Phase(c):
            c.dma("sp", self.hl_l.ap()[0:16, :], xs[T - 16:T, :], [], [], "hl")
        with Phase(c):
            self.allgather(self.hl_l, self.hl_g)

    def halo_x(self):
        c, d = self.c, self.d
        g4 = c.sb([3, 4, D], F32)
        c.dma("sp", g4[:], self.hl_g.ap().rearrange("(r p) n -> p r n", p=16)[13:16], [], [g4.r], "hl")
        hs = c.sb([128, 4], F32)
        c.dma("sp", hs[:], d["hsel"][:, :], [], [hs.r], "bl")
        hx = c.sb([128, D], F32)
        c.op("dve", lambda v: v.memset(hx[0:4, :], 0.0), [], [hx.r])
        v_ts(c, hx[0:3, :], g4[:, 0, :], hs[0:3, 0:1], None, ALU.mult, None, [g4.r, hs.r], [hx.r])
        for r in range(1, 4):
            v_stt(c, hx[0:3, :], g4[:, r, :], hs[0:3, r:r + 1], hx[0:3, :], ALU.mult, ALU.add, [g4.r, hs.r, hx.r],
                  [hx.r])
        return hx

    def ssd_pass(self, j, xs, final):
        c, T, d = self.c, self.T, self.d
        NT = 256
        wx = self.wload(j, 1024, 2560)
        wdt = self.wload(j, 2560, 2576)
        wz = self.wload(j, 0, 1024) if final else None
        nw = self.bload(d["mix_norm_even"][j, :], D)
        cw = self.tload(d["conv_w"][j].rearrange("t (c p) -> (t c) p", p=128), 48)
        cb = self.tload(d["conv_b"][j].rearrange("(c p) -> c p", p=128), 12)
        dtb = self.bload(d["dt_bias"][j, :], 16)
        negA = self.bload(d["a_log"][j, :], 16)
        a_act(c, negA[:], negA[:], AF.Exp, [negA.r], [negA.r])
        v_ts(c, negA[:], negA[:], -1.0, None, ALU.mult, None, [negA.r], [negA.r])
        dsk = self.bload(d["d_skip"][j, :], 16)
        snw = self.bload(d["ssd_norm"][j, :], 1024)
        triu = c.sb([128, 128], F32)
        c.dma("sp", triu[:], d["c_triu"][:, :], [], [triu.r], "bl")
        ones = c.sb([128, 128], F32)
        c.op("dve", lambda v: v.memset(ones[:], 1.0), [], [ones.r])
        Hs = c.sb([128, 16, 64], F32)
        Hsb = c.sb([128, 16, 64], BF16)
        Dtot = c.sb([128, 16], F32)
        c.op("dve", lambda v: v.memset(Hs[:], 0.0), [], [Hs.r])
        c.op("dve", lambda v: v.memset(Dtot[:], 1.0), [], [Dtot.r])
        if final:
            rm = c.sb([128, 4], F32)
            c.dma("sp", rm[:], d["rmask"][:, :], [], [rm.r], "bl")
            G = c.sb([128, 1040], F32)
            Dp = c.sb([128, 16], F32)
            for r in range(4):
                c.dma("sp", G[:, 0:1024], self.st_g.ap()[r * 128:(r + 1) * 128, 0:1024], [], [G.r], "stg")
                c.dma("sp", G[:, 1024:1040], self.st_g.ap()[r * 128:(r + 1) * 128, 1024:1040], [], [G.r], "stg")
                v_ts(c, Dp[:], G[:, 1024:1040], -1.0, rm[:, r:r + 1], ALU.add, ALU.mult, [G.r, rm.r], [Dp.r])
                v_ts(c, Dp[:], Dp[:], 1.0, None, ALU.add, None, [Dp.r], [Dp.r])
                v_tt(c, Hs[:], Hs[:], Dp[:].unsqueeze(2).to_broadcast([128, 16, 64]), ALU.mult, [Hs.r, Dp.r], [Hs.r])
                v_stt(c, Hs[:], G[:, 0:1024].rearrange("p (h d) -> p h d", h=16), rm[:, r:r + 1], Hs[:],
                      ALU.mult, ALU.add, [G.r, rm.r, Hs.r], [Hs.r])
        v_copy(c, Hsb[:], Hs[:], [Hs.r], [Hsb.r])
        xt = c.sb([128, D], F32)
        hb = c.sb([128, D], BF16)
        hT = c.sb([128, 8, NT], BF16)
        xbc = c.sb([128, 12, NT + 3], F32, nres=12)
        acc = [c.sb([128, NT], F32) for _ in range(2)]
        xc = c.sb([128, 12, NT], BF16, nres=12)
        hx = self.halo_x()
        hTh = c.sb([128, 8, 4], BF16)
        self.norm_tile_T(hx, nw, hb, hTh, 0, P=4)
        for cc in range(12):
            pc = c.psum()
            for k in range(8):
                mm(c, pc[:, 0:4], wx[:, k, cc * 128:(cc + 1) * 128], hTh[:, k, 0:4], k == 0, k == 7,
                   [wx.res[k], hTh.r], [pc.r])
            c.op("act", lambda a, pc=pc, cc=cc: a.copy(out=xbc[:, cc, 0:3], in_=pc[:, 0:3]), [pc.r], [xbc.res[cc]])
        sz = c.sb([128, 1024], F32) if final else None
        dt = c.sb([128, 16], F32)
        av = c.sb([128, 16], F32)
        acol = c.sb([128, 16], F32)
        arhs = c.sb([128, 16, 128], F32)
        arow = c.sb([128, 16, 128], F32)
        cdrow = c.sb([128, 16], F32)
        te = c.sb([128, 16], F32)
        Btok = c.sb([128, 2, 128], BF16)
        xdt = c.sb([128, 16, 64], F32)
        xdtw = c.sb([128, 16, 64], BF16)
        if final:
            xdtb = c.sb([128, 16, 64], BF16)
            xsk = c.sb([128, 16, 64], F32)
            CBm = c.sb([128, 2, 128], F32)
            erow = c.sb([128, 16, 128], BF16)
            CeT = c.sb([128, 16, 128], BF16)
            MT = c.sb([128, 16, 128], BF16)
            ysb = c.sb([128, 1024], F32)
            ssg = c.sb([128, 4], F32)
            yo = [c.sb([128, 1024], BF16) for _ in range(2)]
        arf = arhs[:].rearrange("p h l -> p (h l)")
        awf = arow[:].rearrange("p h l -> p (h l)")
        for st in range(T // NT):
            for jt in range(2):
                ti = st * 2 + jt
                c.dma("sp", xt[:], xs[ti * 128:(ti + 1) * 128, :], [], [xt.r], "ex")
                self.norm_tile_T(xt, nw, hb, hT, jt * 128)
            for cc in range(12):
                pc = c.psum()
                for k in range(8):
                    mm(c, pc[:, 0:NT], wx[:, k, cc * 128:(cc + 1) * 128], hT[:, k, :], k == 0, k == 7,
                       [wx.res[k], hT.r], [pc.r])
                c.op("act", lambda a, pc=pc, cc=cc: a.copy(out=xbc[:, cc, 3:NT + 3], in_=pc[:, 0:NT]),
                     [pc.r], [xbc.res[cc]])
                e = "dve"
                ac = acc[cc % 2]
                v_ts(c, ac[:], xbc[:, cc, 0:NT], cw[:, cc:cc + 1], None, ALU.mult, None,
                     [xbc.res[cc], cw.r], [ac.r], e=e)
                for tap in range(1, 4):
                    v_stt(c, ac[:], xbc[:, cc, tap:tap + NT], cw[:, tap * 12 + cc:tap * 12 + cc + 1], ac[:], ALU.mult, ALU.add,
                          [xbc.res[cc], cw.r, ac.r], [ac.r], e=e)
                a_act(c, xc[:, cc, :], ac[:], AF.Silu, [ac.r, cb.r], [xc.res[cc]], bias=cb[:, cc:cc + 1])
            v_copy(c, xbc[:, :, 0:3], xbc[:, :, NT:NT + 3], xbc.res, xbc.res)
            for jt in range(2):
                ti = st * 2 + jt
                ts = slice(jt * 128, (jt + 1) * 128)
                pd = c.psum()
                for k in range(8):
                    mm(c, pd[:, 0:16], hT[:, k, ts], wdt[:, k, :], k == 0, k == 7, [hT.r, wdt.res[k]], [pd.r])
                if final:
                    for n in range(2):
                        pz = c.psum()
                        for k in range(8):
                            mm(c, pz[:, :], hT[:, k, ts], wz[:, k, n * 512:(n + 1) * 512], k == 0, k == 7,
                               [hT.r, wz.res[k]], [pz.r])
                        a_act(c, sz[:, n * 512:(n + 1) * 512], pz[:, :], AF.Silu, [pz.r], [sz.r])
                v_tt(c, dt[:], pd[:, 0:16], dtb[:], ALU.add, [pd.r, dtb.r], [dt.r])
                a_act(c, dt[:], dt[:], AF.Exp, [dt.r], [dt.r])
                a_act(c, dt[:], dt[:], AF.Ln, [dt.r], [dt.r], bias=1.0)
                v_tt(c, av[:], dt[:], negA[:], ALU.mult, [dt.r, negA.r], [av.r])
                pa = c.psum()
                mm(c, pa[:, 0:16], triu[:], av[:], True, True, [triu.r, av.r], [pa.r])
                v_copy(c, acol[:], pa[:, 0:16], [pa.r], [acol.r])
                v_tt(c, arhs[:], triu[:].unsqueeze(1).to_broadcast([128, 16, 128]),
                     av[:].unsqueeze(2).to_broadcast([128, 16, 128]), ALU.mult, [triu.r, av.r], [arhs.r])
                for n in range(4):
                    pr = c.psum()
                    mm(c, pr[:, :], ones[:], arf[:, n * 512:(n + 1) * 512], True, True, [ones.r, arhs.r], [pr.r])
                    c.op("act", lambda a, pr=pr, n=n: a.copy(out=awf[:, n * 512:(n + 1) * 512], in_=pr[:, :]),
                         [pr.r], [arow.r])
                a_act(c, cdrow[:], arow[:, :, 127], AF.Exp, [arow.r], [cdrow.r])
                v_tt(c, te[:], arow[:, :, 127], acol[:], ALU.subtract, [arow.r, acol.r], [te.r])
                a_act(c, te[:], te[:], AF.Exp, [te.r], [te.r])
                pxs = c.psum()
                pxv = pxs[:].bitcast(BF16)
                for cc in range(8):
                    tr(c, pxv[:, cc * 128:(cc + 1) * 128], xc[:, cc, ts], self.ident[:], [xc.res[cc], self.ident.r],
                       [pxs.r])
                pB = c.psum()
                pBv = pB[:].bitcast(BF16)
                for g in range(2):
                    tr(c, pBv[:, g * 128:(g + 1) * 128], xc[:, 8 + g, ts], self.ident[:],
                       [xc.res[8 + g], self.ident.r], [pB.r])
                c.op("act", lambda a, pBv=pBv: a.copy(out=Btok[:], in_=pBv[:, 0:256].rearrange("p (g n) -> p g n", g=2)),
                     [pB.r], [Btok.r])
                xv = pxv[:, 0:1024].rearrange("p (h d) -> p h d", h=16)
                v_tt(c, xdt[:], xv, dt[:].unsqueeze(2).to_broadcast([128, 16, 64]), ALU.mult, [pxs.r, dt.r], [xdt.r])
                v_tt(c, xdtw[:], xdt[:], te[:].unsqueeze(2).to_broadcast([128, 16, 64]), ALU.mult, [xdt.r, te.r],
                     [xdtw.r])
                if final:
                    v_copy(c, xdtb[:], xdt[:], [xdt.r], [xdtb.r], e="pool")
                    v_tt(c, xsk[:], xv, dsk[:].unsqueeze(2).to_broadcast([128, 16, 64]), ALU.mult, [pxs.r, dsk.r],
                         [xsk.r])
                    pcb = c.psum()
                    for g in range(2):
                        mm(c, pcb[:, g * 128:(g + 1) * 128], xc[:, 8 + g, ts], xc[:, 10 + g, ts], True, True,
                           [xc.res[8 + g], xc.res[10 + g]], [pcb.r])
                    v_tt(c, CBm[:], pcb[:, 0:256].rearrange("p (g l) -> p g l", g=2),
                         triu[:].unsqueeze(1).to_broadcast([128, 2, 128]), ALU.mult, [pcb.r, triu.r], [CBm.r])
                    a_act(c, erow[:], arow[:], AF.Exp, [arow.r], [erow.r])
                    for g in range(2):
                        v_tt(c, CeT[:, 8 * g:8 * g + 8, :], erow[:, 8 * g:8 * g + 8, :],
                             xc[:, 10 + g, ts].unsqueeze(1).to_broadcast([128, 8, 128]), ALU.mult,
                             [erow.r, xc.res[10 + g]], [CeT.r], e="pool")
                    v_tt(c, arhs[:], arow[:], acol[:].unsqueeze(2).to_broadcast([128, 16, 128]), ALU.subtract,
                         [arow.r, acol.r], [arhs.r])
                    v_ts(c, arhs[:], arhs[:], 0.0, None, ALU.min, None, [arhs.r], [arhs.r])
                    a_act(c, arhs[:], arhs[:], AF.Exp, [arhs.r], [arhs.r])
                    for g in range(2):
                        v_tt(c, MT[:, 8 * g:8 * g + 8, :], arhs[:, 8 * g:8 * g + 8, :],
                             CBm[:, g, :].unsqueeze(1).to_broadcast([128, 8, 128]), ALU.mult, [arhs.r, CBm.r], [MT.r])
                    py = [c.psum(), c.psum()]
                    for h in range(16):
                        p_ = py[h // 8]
                        col = (h % 8) * 64
                        mm(c, p_[:, col:col + 64], MT[:, h, :], xdtb[:, h, :], True, False, [MT.r, xdtb.r], [p_.r])
                        mm(c, p_[:, col:col + 64], CeT[:, h, :], Hsb[:, h, :], False, True, [CeT.r, Hsb.r], [p_.r])
                    xskf = xsk[:].rearrange("p h d -> p (h d)")
                    for n in range(2):
                        v_tt(c, ysb[:, n * 512:(n + 1) * 512], py[n][:, :], xskf[:, n * 512:(n + 1) * 512], ALU.add,
                             [py[n].r, xsk.r], [ysb.r])
                    v_tt(c, ysb[:], ysb[:], sz[:], ALU.mult, [ysb.r, sz.r], [ysb.r])
                    for g in range(2):
                        a_act(c, self.junk[:, 0:512], ysb[:, g * 512:(g + 1) * 512], AF.Square, [ysb.r],
                              [self.junk.r, ssg.r], accum_out=ssg[:, g:g + 1])
                    rstd_inplace(c, ssg[:, 0:2], 512, ssg.r)
                    yb = yo[ti % 2]
                    for g in range(2):
                        v_stt(c, yb[:, g * 512:(g + 1) * 512], ysb[:, g * 512:(g + 1) * 512], ssg[:, g:g + 1],
                              snw[:, g * 512:(g + 1) * 512], ALU.mult, ALU.mult, [ysb.r, ssg.r, snw.r], [yb.r])
                    c.dma("sp", self.ycat_d[ti * 128:(ti + 1) * 128, 0:1024], yb[:], [yb.r], [], ("yo", ti % 2))
                xwf = xdtw[:].rearrange("p h d -> p (h d)")
                for g in range(2):
                    pst = c.psum()
                    mm(c, pst[:, :], Btok[:, g, :], xwf[:, g * 512:(g + 1) * 512], True, True, [Btok.r, xdtw.r], [pst.r])
                    hg = Hs[:, 8 * g:8 * g + 8, :]
                    v_tt(c, hg, hg, cdrow[:, 8 * g:8 * g + 8].unsqueeze(2).to_broadcast([128, 8, 64]), ALU.mult,
                         [Hs.r, cdrow.r, Hsb.r], [Hs.r])
                    v_tt(c, hg, hg, pst[:, :].rearrange("p (h d) -> p h d", h=8), ALU.add, [Hs.r, pst.r], [Hs.r])
                v_copy(c, Hsb[:], Hs[:], [Hs.r], [Hsb.r])
                if not final:
                    v_tt(c, Dtot[:], Dtot[:], cdrow[:], ALU.mult, [Dtot.r, cdrow.r], [Dtot.r])
        if not final:
            c.dma("sp", self.st_l.ap()[:, 0:1024], Hs[:].rearrange("p h d -> p (h d)"), [Hs.r], [], "stl")
            c.dma("sp", self.st_l.ap()[:, 1024:1040], Dtot[:], [Dtot.r], [], "stl2")

    def gla_pass(self, j, xs, final):
        c, T, d = self.c, self.T, self.d
        NT = 256
        wq = self.wload(j, 2576, 3088)
        wk = self.wload(j, 3088, 3600)
        wv = self.wload(j, 3600, 4624)
        wg = self.wload(j, 4624, 5648) if final else None
        wgl = self.wload(j, 5648, 5664)
        nw = self.bload(d["mix_norm_even"][j, :], D)
        gkb = self.bload(d["gla_gk_b"][j, :], 512)
        gnw = self.bload(d["gla_norm"][j, :], 256)
        w2 = c.sb([16, 512], BF16)
        c.dma("pool", w2[:], d["gla_gk_w2"][j, :, :], [], [w2.r], "w")
        triu = c.sb([128, 128], F32)
        c.dma("sp", triu[:], d["c_triu"][:, :], [], [triu.r], "bl")
        Sg = c.sb([128, 4, 256], F32)
        Sgb = c.sb([128, 4, 256], BF16)
        Dg = c.sb([128, 4], F32)
        c.op("dve", lambda v: v.memset(Sg[:], 0.0), [], [Sg.r])
        c.op("dve", lambda v: v.memset(Dg[:], 1.0), [], [Dg.r])
        if final:
            rm = c.sb([128, 4], F32)
            c.dma("sp", rm[:], d["rmask"][:, :], [], [rm.r], "bl")
            G = c.sb([128, 1028], F32)
            Dp = c.sb([128, 4], F32)
            for r in range(4):
                c.dma("sp", G[:, 0:1024], self.sg_g.ap()[r * 128:(r + 1) * 128, 0:1024], [], [G.r], "stg")
                c.dma("sp", G[:, 1024:1028], self.sg_g.ap()[r * 128:(r + 1) * 128, 1024:1028], [], [G.r], "stg")
                v_ts(c, Dp[:], G[:, 1024:1028], -1.0, rm[:, r:r + 1], ALU.add, ALU.mult, [G.r, rm.r], [Dp.r])
                v_ts(c, Dp[:], Dp[:], 1.0, None, ALU.add, None, [Dp.r], [Dp.r])
                v_tt(c, Sg[:], Sg[:], Dp[:].unsqueeze(2).to_broadcast([128, 4, 256]), ALU.mult, [Sg.r, Dp.r], [Sg.r])
                v_stt(c, Sg[:], G[:, 0:1024].rearrange("p (h d) -> p h d", h=4), rm[:, r:r + 1], Sg[:],
                      ALU.mult, ALU.add, [G.r, rm.r, Sg.r], [Sg.r])
        v_copy(c, Sgb[:], Sg[:], [Sg.r], [Sgb.r])
        xt = c.sb([128, D], F32)
        hb = c.sb([128, D], BF16)
        hT = c.sb([128, 8, NT], BF16)
        qTs = c.sb([128, 4, NT], F32)
        kTs = c.sb([128, 4, NT], F32)
        vsb = c.sb([128, 1024], BF16)
        sgt = c.sb([128, 1024], F32) if final else None
        glow = c.sb([128, 16], BF16)
        glT = c.sb([16, 128], BF16)
        gk = c.sb([128, 512], F32)
        eg = c.sb([128, 4, 128], F32)
        eng = c.sb([128, 4, 128], F32)
        qd = c.sb([128, 4, 128], BF16)
        kif = c.sb([128, 4, 128], F32)
        ki = c.sb([128, 4, 128], BF16)
        ke = c.sb([128, 4, 128], BF16)
        ketok = c.sb([128, 4, 128], BF16)
        if final:
            scm = c.sb([128, 4, 128], BF16)
            ssg = c.sb([128, 4], F32)
            otmp = c.sb([128, 256], F32)
            og = [c.sb([128, 1024], BF16) for _ in range(2)]
        for st in range(T // NT):
            for jt in range(2):
                ti = st * 2 + jt
                c.dma("sp", xt[:], xs[ti * 128:(ti + 1) * 128, :], [], [xt.r], "ex")
                self.norm_tile_T(xt, nw, hb, hT, jt * 128)
            for w_, dst_ in ((wq, qTs), (wk, kTs)):
                for hh in range(4):
                    pq = c.psum()
                    for k in range(8):
                        mm(c, pq[:, 0:NT], w_[:, k, hh * 128:(hh + 1) * 128], hT[:, k, :], k == 0, k == 7,
                           [w_.res[k], hT.r], [pq.r])
                    c.op("act", lambda a, pq=pq, hh=hh, dst_=dst_: a.copy(out=dst_[:, hh, :], in_=pq[:, 0:NT]),
                         [pq.r], [dst_.r])
            for jt in range(2):
                ti = st * 2 + jt
                ts = slice(jt * 128, (jt + 1) * 128)
                for n in range(2):
                    pv = c.psum()
                    for k in range(8):
                        mm(c, pv[:, :], hT[:, k, ts], wv[:, k, n * 512:(n + 1) * 512], k == 0, k == 7,
                           [hT.r, wv.res[k]], [pv.r])
                    c.op("act", lambda a, pv=pv, n=n: a.copy(out=vsb[:, n * 512:(n + 1) * 512], in_=pv[:, :]),
                         [pv.r], [vsb.r])
                    if final:
                        pg = c.psum()
                        for k in range(8):
                            mm(c, pg[:, :], hT[:, k, ts], wg[:, k, n * 512:(n + 1) * 512], k == 0, k == 7,
                               [hT.r, wg.res[k]], [pg.r])
                        a_act(c, sgt[:, n * 512:(n + 1) * 512], pg[:, :], AF.Silu, [pg.r], [sgt.r])
                pgl = c.psum()
                for k in range(8):
                    mm(c, pgl[:, 0:16], hT[:, k, ts], wgl[:, k, :], k == 0, k == 7, [hT.r, wgl.res[k]], [pgl.r])
                c.op("act", lambda a, pgl=pgl: a.copy(out=glow[:], in_=pgl[:, 0:16]), [pgl.r], [glow.r])
                ptg = c.psum()
                ptgv = ptg[:].bitcast(BF16)
                tr(c, ptgv[0:16, 0:128], glow[:], self.ident[:], [glow.r, self.ident.r], [ptg.r])
                c.op("act", lambda a, ptgv=ptgv: a.copy(out=glT[:], in_=ptgv[0:16, 0:128]), [ptg.r], [glT.r])
                pgk = c.psum()
                mm(c, pgk[:, :], glT[:], w2[:], True, True, [glT.r, w2.r], [pgk.r])
                v_tt(c, gk[:], pgk[:, :], gkb[:], ALU.add, [pgk.r, gkb.r], [gk.r])
                a_act(c, gk[:], gk[:], AF.Exp, [gk.r], [gk.r], scale=-1.0)
                a_act(c, gk[:], gk[:], AF.Ln, [gk.r], [gk.r], bias=1.0)
                v_ts(c, gk[:], gk[:], -1.0 / 16.0, None, ALU.mult, None, [gk.r], [gk.r])
                pgc = c.psum()
                for hh in range(4):
                    mm(c, pgc[:, hh * 128:(hh + 1) * 128], gk[:, hh * 128:(hh + 1) * 128], triu[:], True, True,
                       [gk.r, triu.r], [pgc.r])
                pgv = pgc[:, :].rearrange("p (h t) -> p h t", h=4)
                a_act(c, eg[:], pgv, AF.Exp, [pgc.r], [eg.r])
                a_act(c, eng[:], pgv, AF.Exp, [pgc.r], [eng.r], scale=-1.0)
                v_stt(c, qd[:], qTs[:, :, ts], 128.0 ** -0.5, eg[:], ALU.mult, ALU.mult, [qTs.r, eg.r], [qd.r])
                v_tt(c, kif[:], kTs[:, :, ts], eng[:], ALU.mult, [kTs.r, eng.r], [kif.r])
                v_copy(c, ki[:], kif[:], [kif.r], [ki.r], e="pool")
                v_tt(c, ke[:], kif[:], eg[:, :, 127:128].to_broadcast([128, 4, 128]), ALU.mult, [kif.r, eg.r], [ke.r])
                pke = c.psum()
                pkev = pke[:].bitcast(BF16)
                for hh in range(4):
                    tr(c, pkev[:, hh * 128:(hh + 1) * 128], ke[:, hh, :], self.ident[:], [ke.r, self.ident.r], [pke.r])
                c.op("act", lambda a, pkev=pkev: a.copy(out=ketok[:], in_=pkev[:, 0:512].rearrange("p (h t) -> p h t", h=4)),
                     [pke.r], [ketok.r])
                if final:
                    psc = c.psum()
                    for hh in range(4):
                        mm(c, psc[:, hh * 128:(hh + 1) * 128], ki[:, hh, :], qd[:, hh, :], True, True, [ki.r, qd.r],
                           [psc.r])
                    v_tt(c, scm[:], psc[:, :].rearrange("p (h t) -> p h t", h=4),
                         triu[:].unsqueeze(1).to_broadcast([128, 4, 128]), ALU.mult, [psc.r, triu.r], [scm.r])
                    po = [c.psum(), c.psum()]
                    for hh in range(4):
                        p_ = po[hh // 2]
                        col = (hh % 2) * 256
                        mm(c, p_[:, col:col + 256], scm[:, hh, :], vsb[:, hh * 256:(hh + 1) * 256], True, False,
                           [scm.r, vsb.r], [p_.r])
                        mm(c, p_[:, col:col + 256], qd[:, hh, :], Sgb[:, hh, :], False, True, [qd.r, Sgb.r], [p_.r])
                    for hh in range(4):
                        p_ = po[hh // 2]
                        col = (hh % 2) * 256
                        a_act(c, self.junk[:, 0:256], p_[:, col:col + 256], AF.Square, [p_.r], [self.junk.r, ssg.r],
                              accum_out=ssg[:, hh:hh + 1])
                    rstd_inplace(c, ssg[:], 256, ssg.r)
                    ob = og[ti % 2]
                    for hh in range(4):
                        p_ = po[hh // 2]
                        col = (hh % 2) * 256
                        v_stt(c, otmp[:], p_[:, col:col + 256], ssg[:, hh:hh + 1], gnw[:], ALU.mult, ALU.mult,
                              [p_.r, ssg.r, gnw.r], [otmp.r])
                        v_tt(c, ob[:, hh * 256:(hh + 1) * 256], otmp[:], sgt[:, hh * 256:(hh + 1) * 256], ALU.mult,
                             [otmp.r, sgt.r], [ob.r])
                    c.dma("sp", self.ycat_d[ti * 128:(ti + 1) * 128, 1024:2048], ob[:], [ob.r], [], ("yo", ti % 2))
                pkv = [c.psum(), c.psum()]
                for hh in range(4):
                    p_ = pkv[hh // 2]
                    col = (hh % 2) * 256
                    mm(c, p_[:, col:col + 256], ketok[:, hh, :], vsb[:, hh * 256:(hh + 1) * 256], True, True,
                       [ketok.r, vsb.r], [p_.r])
                    v_stt(c, Sg[:, hh, :], Sg[:, hh, :], eg[:, hh, 127:128], p_[:, col:col + 256], ALU.mult, ALU.add,
                          [Sg.r, eg.r, p_.r, Sgb.r], [Sg.r])
                v_copy(c, Sgb[:], Sg[:], [Sg.r], [Sgb.r])
                if not final:
                    v_tt(c, Dg[:], Dg[:], eg[:, :, 127], ALU.mult, [Dg.r, eg.r], [Dg.r])
        if not final:
            c.dma("sp", self.sg_l.ap()[:, 0:1024], Sg[:].rearrange("p h d -> p (h d)"), [Sg.r], [], "stl")
            c.dma("sp", self.sg_l.ap()[:, 1024:1028], Dg[:], [Dg.r], [], "stl2")

    def outproj_phase(self, j, xs):
        c, T, d = self.c, self.T, self.d
        wo = c.sb([128, 16, D], BF16, nres=16)
        for k in range(16):
            c.dma("pool", wo[:, k, :], d["w_out_even"][j, k * 128:(k + 1) * 128, :], [], [wo.res[k]], "w")
        yc = [c.sb([128, 2048], BF16) for _ in range(2)]
        ycT = c.sb([128, 16, 128], BF16)
        xt = [c.sb([128, D], F32) for _ in range(2)]
        xo = [c.sb([128, D], F32) for _ in range(2)]
        for ti in range(T // 128):
            b = ti % 2
            tsl = slice(ti * 128, (ti + 1) * 128)
            c.dma("sp", yc[b][:], self.ycat_d[tsl, :], [], [yc[b].r], ("opy", b))
            c.dma("sp", xt[b][:], xs[tsl, :], [], [xt[b].r], ("opx", b))
            for half in range(2):
                pt = c.psum()
                ptv = pt[:].bitcast(BF16)
                for kk in range(8):
                    k = half * 8 + kk
                    tr(c, ptv[:, kk * 128:(kk + 1) * 128], yc[b][:, k * 128:(k + 1) * 128], self.ident[:],
                       [yc[b].r, self.ident.r], [pt.r])
                c.op("act", lambda a, ptv=ptv, half=half: a.copy(
                    out=ycT[:, half * 8:half * 8 + 8, :], in_=ptv.rearrange("p (k t) -> p k t", k=8)), [pt.r], [ycT.r])
            for n in range(2):
                po = c.psum()
                for k in range(16):
                    mm(c, po[:, :], ycT[:, k, :], wo[:, k, n * 512:(n + 1) * 512], k == 0, k == 15,
                       [ycT.r, wo.res[k]], [po.r])
                v_tt(c, xo[b][:, n * 512:(n + 1) * 512], po[:, :], xt[b][:, n * 512:(n + 1) * 512], ALU.add,
                     [po.r, xt[b].r], [xo[b].r])
            c.dma("sp", xs[tsl, :], xo[b][:], [xo[b].r], [], ("opo", b))

    def even_layer(self, j, xs):
        import os
        c = self.c
        nph = int(os.environ.get("KPH", "99"))
        if nph >= 1:
            self.halo_phase(xs)
        steps = [lambda: self.ssd_pass(j, xs, False), lambda: self.gla_pass(j, xs, False),
                 lambda: (self.allgather(self.st_l, self.st_g), self.allgather(self.sg_l, self.sg_g)), lambda: self.ssd_pass(j, xs, True),
                 lambda: self.gla_pass(j, xs, True), lambda: self.outproj_phase(j, xs)]
        for i, f in enumerate(steps):
            if nph >= i + 2:
                with Phase(c):
                    f()

    def build_all(self, depth):
        c = self.c
        self.declare(depth)
        self.ycat_d = self.nc.dram_tensor("ycat_d", [self.T, 2048], BF16).ap()
        with Phase(c):
            self.setup_consts_inner()
            c.dma("sp", self.xs[:, :], self.d["x"][:, :], [], [], "xcp")
        for i in range(depth):
            if i % 2 == 0:
                self.even_layer(i // 2, self.xs)
            else:
                self.mla_layer(i // 2, self.xs)
            with Phase(c):
                dst = self.y if i == depth - 1 else self.xs
                self.ffn_phase(i, self.xs, None, dst, None, self.d["w_gate"], self.d["w_up"], self.d["w_down"],
                               self.d["ffn_norm"])
        return self.finish()

    def setup_consts_inner(self):
        c = self.c
        outer = c.stack
        c.stack = self.stack
        idf = c.sb([128, 128], F32)
        self.identf = idf
        self.ident = c.sb([128, 128], BF16)
        c.dma("sp", idf[:], self.d["c_ident"][:, :], [], [idf.r], "const")
        c.op("dve", lambda v: v.tensor_copy(out=self.ident[:], in_=idf[:]), [idf.r], [self.ident.r])
        self.junk = c.sb([128, 1024], BF16)
        self.ss = c.sb([128, 4], F32)
        c.stack = outer

    def finish(self):
        self.c.barrier()
        self.c.flush()
        self.stack.close()
        return self.nc


def _core_inputs(inputs, SEQ, depth):
    T = SEQ // 4
    NB = SEQ
    consts = {
        "c_ident": np.eye(128, dtype=np.float32),
        "c_inv": (1.0 / (np.float32(10000.0) ** (np.arange(0, 32, 2, dtype=np.float32) / np.float32(32)))).astype(np.float32),
        "c_kpos": (np.arange(NB // 128, dtype=np.float32)[None, :] * 128 + np.arange(128, dtype=np.float32)[:, None]).astype(np.float32),
        "c_triu": np.triu(np.ones((128, 128), dtype=np.float32)),
    }
    maps = []
    for core in range(8):
        b, r = core // 4, core % 4
        m = {}
        for k, v in inputs.items():
            if k == "x":
                continue
            m[k] = np.ascontiguousarray(np.asarray(v, dtype=np.float32))
        m["x"] = np.ascontiguousarray(np.asarray(inputs["x"])[b, r * T:(r + 1) * T, :], dtype=np.float32)
        m["pos"] = np.arange(r * T, (r + 1) * T, dtype=np.float32)
        rm = np.zeros((128, 4), np.float32)
        rm[:, :r] = 1.0
        hs = np.zeros((128, 4), np.float32)
        if r > 0:
            hs[:, r - 1] = 1.0
        m["rmask"], m["hsel"] = rm, hs
        m.update(consts)
        maps.append(m)
    return maps


_CACHE = {}


def kernel(**inputs):
    x = np.asarray(inputs["x"])
    B, SEQ, _ = x.shape
    depth = int(np.asarray(inputs["ffn_norm"]).shape[0])
    T = SEQ // 4
    key = (SEQ, depth)
    if key not in _CACHE:
        _CACHE[key] = Prog(T, None).build_all(depth)
    nc = _CACHE[key]
    maps = _core_inputs(inputs, SEQ, depth)
    res = run_bass_kernel_spmd(nc, maps, core_ids=list(range(8)))
    global _LAST
    _LAST = res
    out = np.empty((B, SEQ, D), dtype=np.float32)
    for core in range(8):
        b, r = core // 4, core % 4
        out[b, r * T:(r + 1) * T, :] = res.results[core]["y"]
    return out
```

```python
import numpy as np
from contextlib import ExitStack
import concourse.bass as bass
import concourse.mybir as mybir
from concourse.bass_utils import run_bass_kernel_spmd

F32 = mybir.dt.float32
BF16 = mybir.dt.bfloat16
ALU = mybir.AluOpType
AF = mybir.ActivationFunctionType
AX = mybir.AxisListType

D = 1024
FH = 2816
EPS = 1e-6
ENGS = ("pe", "act", "dve", "pool", "sp")


class Res:
    __slots__ = ("w", "r")

    def __init__(self):
        self.w = None
        self.r = {}


class Dom:
    __slots__ = ("sem", "inc", "count", "waitall")

    def __init__(self, sem, inc):
        self.sem, self.inc, self.count = sem, inc, 0
        self.waitall = False


class Buf:
    def __init__(self, t, nres=1):
        self.t = t
        self.res = [Res() for _ in range(nres)]

    def __getitem__(self, idx):
        return self.t[idx]

    @property
    def r(self):
        return self.res[0]


class Ctx:
    def __init__(self, nc, stack):
        self.nc, self.stack = nc, stack
        self.root = stack
        self.prog = {e: [] for e in ENGS}
        self.dom = {}
        for e in ("pe", "act", "dve", "pool"):
            self.dom[e] = Dom(stack.enter_context(nc.semaphore("s_" + e)), 1)
        self.waited = {e: {} for e in ENGS}
        self.dma_doms = {}
        self.nbuf = 0
        self.psum_banks = []
        self.psum_i = 0

    def sb(self, shape, dtype, nres=1, name=None):
        self.nbuf += 1
        t = self.stack.enter_context(self.nc.sbuf_tensor(name or ("b%d" % self.nbuf), list(shape), dtype))
        return Buf(t, nres)

    def init_psum(self):
        for i in range(8):
            t = self.stack.enter_context(self.nc.psum_tensor("ps%d" % i, [128, 512], F32))
            self.psum_banks.append(Buf(t))

    def psum(self):
        b = self.psum_banks[self.psum_i % 6]
        self.psum_i += 1
        return b

    def psum_acc(self, i):
        return self.psum_banks[6 + i]

    def dma_dom(self, key):
        if key not in self.dma_doms:
            sem = self.root.enter_context(self.nc.semaphore("d%d" % len(self.dma_doms)))
            self.dma_doms[key] = Dom(sem, 16)
            self.dma_doms[key].waitall = isinstance(key, str)
        return self.dma_doms[key]

    def op(self, e, fn, reads=(), writes=(), dom=None):
        deps = {}

        def add(dm, v):
            if dm.waitall:
                v = dm.count
            if deps.get(dm, 0) < v:
                deps[dm] = v

        for r in reads:
            if r.w is not None:
                add(*r.w)
        for w in writes:
            if w.w is not None:
                add(*w.w)
            for dm, v in w.r.items():
                add(dm, v)
        own = self.dom.get(e)
        for dm, v in deps.items():
            if e == "pe" and dm is own:
                continue
            if self.waited[e].get(dm, 0) >= v:
                continue
            self.waited[e][dm] = v
            self.prog[e].append(lambda eng, s=dm.sem, v=v: eng.wait_ge(s, v))
        dm = dom if dom is not None else own
        dm.count += dm.inc
        self.prog[e].append(lambda eng, s=dm.sem, i=dm.inc: fn(eng).then_inc(s, i))
        for r in reads:
            if r.r.get(dm, 0) < dm.count:
                r.r[dm] = dm.count
        for w in writes:
            w.w = (dm, dm.count)
            w.r = {}

    def dma(self, e, out, in_, reads, writes, key):
        self.dma_op(e, lambda eng: eng.dma_start(out=out, in_=in_), reads, writes, key)

    def dma_op(self, e, fn, reads, writes, key):
        dm = self.dma_dom(key)
        if dm.waitall and dm.count > 0 and self.waited[e].get(dm, 0) < dm.count:
            self.waited[e][dm] = dm.count
            self.prog[e].append(lambda eng, s=dm.sem, v=dm.count: eng.wait_ge(s, v))
        self.op(e, fn, reads, writes, dom=dm)

    def wait_all(self, e, ress):
        for r in ress:
            if r.w is not None:
                dm, v = r.w
                if self.waited[e].get(dm, 0) < v:
                    self.waited[e][dm] = v
                    self.prog[e].append(lambda eng, s=dm.sem, v=v: eng.wait_ge(s, v))

    def barrier(self):
        doms = list(self.dom.values()) + list(self.dma_doms.values())
        for e in ENGS:
            for dm in doms:
                if dm.count > 0 and self.waited[e].get(dm, 0) < dm.count:
                    self.waited[e][dm] = dm.count
                    self.prog[e].append(lambda eng, s=dm.sem, v=dm.count: eng.wait_ge(s, v))

    def flush(self):
        nc = self.nc
        prog = self.prog
        self.prog = {e: [] for e in ENGS}
        with nc.Block() as block:
            @block.tensor
            def _(eng):
                for t in prog["pe"]:
                    t(eng)

            @block.scalar
            def _(eng):
                for t in prog["act"]:
                    t(eng)

            @block.vector
            def _(eng):
                for t in prog["dve"]:
                    t(eng)

            @block.gpsimd
            def _(eng):
                for t in prog["pool"]:
                    t(eng)

            @block.sync
            def _(eng):
                for t in prog["sp"]:
                    t(eng)


class Phase:
    def __init__(self, c):
        self.c = c

    def __enter__(self):
        self.outer = self.c.stack
        self.st = ExitStack()
        self.c.stack = self.st
        return self

    def __exit__(self, *a):
        self.c.barrier()
        self.c.flush()
        self.c.stack = self.outer
        self.st.close()
        return False


def mm(c, ps_ap, lhsT, rhs, start, stop, reads, writes):
    c.op("pe", lambda pe: pe.matmul(ps_ap, lhsT, rhs, start=start, stop=stop), reads, writes)


def tr(c, ps_ap, in_ap, ident_ap, reads, writes):
    c.op("pe", lambda pe: pe.transpose(ps_ap, in_ap, ident_ap), reads, writes)


def v_tt(c, out, in0, in1, op, reads, writes, e="dve"):
    c.op(e, lambda v: v.tensor_tensor(out=out, in0=in0, in1=in1, op=op), reads, writes)


def v_ts(c, out, in0, s1, s2, op0, op1, reads, writes, e="dve"):
    if op1 is None:
        c.op(e, lambda v: v.tensor_scalar(out=out, in0=in0, scalar1=s1, scalar2=None, op0=op0), reads, writes)
    else:
        c.op(e, lambda v: v.tensor_scalar(out=out, in0=in0, scalar1=s1, scalar2=s2, op0=op0, op1=op1), reads, writes)


def v_stt(c, out, in0, scalar, in1, op0, op1, reads, writes, e="dve"):
    c.op(e, lambda v: v.scalar_tensor_tensor(out=out, in0=in0, scalar=scalar, in1=in1, op0=op0, op1=op1),
         reads, writes)


def v_copy(c, out, in_, reads, writes, e="dve"):
    c.op(e, lambda v: v.tensor_copy(out=out, in_=in_), reads, writes)


def v_red(c, out, in_, reads, writes):
    c.op("dve", lambda v: v.reduce_sum(out=out, in_=in_, axis=AX.X), reads, writes)


def v_recip(c, out, in_, reads, writes):
    c.op("dve", lambda v: v.reciprocal(out=out, in_=in_), reads, writes)


def a_act(c, out, in_, func, reads, writes, **kw):
    c.op("act", lambda a: a.activation(out=out, in_=in_, func=func, **kw), reads, writes)


def rstd_inplace(c, ss_ap, n, res):
    a_act(c, ss_ap, ss_ap, AF.Sqrt, [res], [res], scale=1.0 / n, bias=EPS)
    v_recip(c, ss_ap, ss_ap, [res], [res])


def rmsnorm(c, xt_ap, P, n, wb_ap, out_ap, junk, ss, reads, writes, wres):
    c.op("act", lambda a: a.activation(out=junk[0:P, 0:n], in_=xt_ap, func=AF.Square, accum_out=ss[0:P, 0:1]),
         reads, [junk.r, ss.r])
    c.op("act", lambda a: a.activation(out=ss[0:P, 0:1], in_=ss[0:P, 0:1], func=AF.Sqrt, scale=1.0 / n, bias=EPS),
         [ss.r], [ss.r])
    c.op("dve", lambda v: v.reciprocal(out=ss[0:P, 0:1], in_=ss[0:P, 0:1]), [ss.r], [ss.r])
    c.op("dve", lambda v: v.scalar_tensor_tensor(out=out_ap, in0=xt_ap, scalar=ss[0:P, 0:1], in1=wb_ap,
                                                 op0=ALU.mult, op1=ALU.mult),
         list(reads) + [ss.r, wres], writes)


class Prog:
    def __init__(self, T, layers, test=None):
        self.T = T
        self.layers = layers
        nc = self.nc = bass.Bass("TRN2", target_bir_lowering=False)
        self.stack = ExitStack()
        self.c = Ctx(nc, self.stack)
        self.c.init_psum()
        self.ext = {}

    def inp(self, name, shape, dtype=F32):
        t = self.nc.dram_tensor(name, list(shape), dtype, kind="ExternalInput")
        self.ext[name] = t
        return t

    def setup_consts(self):
        c = self.c
        ident_d = self.inp("c_ident", [128, 128])
        idf = c.sb([128, 128], F32)
        self.ident = c.sb([128, 128], BF16)
        c.dma("sp", idf[:], ident_d.ap()[:, :], [], [idf.r], "const")
        c.op("dve", lambda v: v.tensor_copy(out=self.ident[:], in_=idf[:]), [idf.r], [self.ident.r])
        self.junk = c.sb([128, 1024], BF16)
        self.ss = c.sb([128, 4], F32)

    def ffn_phase(self, L, src, src_res, dst, dst_res, wg_d, wu_d, wd_d, nw_d):
        c, T = self.c, self.T
        NT = 256
        wg = c.sb([128, 8, FH], BF16, nres=8)
        wu = c.sb([128, 8, FH], BF16, nres=8)
        wd = c.sb([128, 22, D], BF16, nres=22)
        nw = c.sb([128, D], F32)
        c.dma("sp", nw[:], nw_d[L, :].partition_broadcast(128), [], [nw.r], "ffn_nw")
        for k in range(8):
            c.dma("pool", wg[:, k, :], wg_d[L, k * 128:(k + 1) * 128, :], [], [wg.res[k]], "w")
            c.dma("pool", wu[:, k, :], wu_d[L, k * 128:(k + 1) * 128, :], [], [wu.res[k]], "w")
        for h in range(22):
            c.dma("pool", wd[:, h, :], wd_d[L, h * 128:(h + 1) * 128, :], [], [wd.res[h]], "w")
        xt = [c.sb([128, D], F32) for _ in range(2)]
        xo = [c.sb([128, D], F32) for _ in range(2)]
        hb = c.sb([128, D], BF16)
        hT = c.sb([128, 8, NT], BF16)
        sg = c.sb([128, NT], F32)
        hid = c.sb([128, 22, NT], BF16, nres=22)
        for st in range(T // NT):
            for j in range(2):
                ti = st * 2 + j
                c.dma("sp", xt[j][:], src[ti * 128:(ti + 1) * 128, :], [], [xt[j].r], ("ffn_x", j))
                rmsnorm(c, xt[j][:], 128, D, nw[:], hb[:], self.junk, self.ss, [xt[j].r], [hb.r], nw.r)
                pT = c.psum()
                pTv = pT[:].bitcast(BF16)
                for k in range(8):
                    tr(c, pTv[:, k * 128:(k + 1) * 128], hb[:, k * 128:(k + 1) * 128], self.ident[:],
                       [hb.r, self.ident.r], [pT.r])
                c.op("act", lambda a, j=j, pTv=pTv: a.copy(out=hT[:, :, j * 128:(j + 1) * 128],
                                                         in_=pTv.rearrange("p (k t) -> p k t", k=8)),
                     [pT.r], [hT.r])
            for h in range(22):
                pg = c.psum()
                pu = c.psum()
                for k in range(8):
                    mm(c, pg[:, 0:NT], wg[:, k, h * 128:(h + 1) * 128], hT[:, k, :], k == 0, k == 7,
                       [wg.res[k], hT.r], [pg.r])
                for k in range(8):
                    mm(c, pu[:, 0:NT], wu[:, k, h * 128:(h + 1) * 128], hT[:, k, :], k == 0, k == 7,
                       [wu.res[k], hT.r], [pu.r])
                c.op("act", lambda a, pg=pg: a.activation(out=sg[:], in_=pg[:, 0:NT], func=AF.Silu),
                     [pg.r], [sg.r])
                c.op("dve", lambda v, pu=pu, h=h: v.tensor_tensor(out=hid[:, h, :], in0=sg[:], in1=pu[:, 0:NT],
                                                                  op=ALU.mult),
                     [sg.r, pu.r], [hid.res[h]])
            for j in range(2):
                ti = st * 2 + j
                for n in range(2):
                    po = c.psum()
                    for h in range(22):
                        mm(c, po[:, :], hid[:, h, j * 128:(j + 1) * 128], wd[:, h, n * 512:(n + 1) * 512],
                           h == 0, h == 21, [hid.res[h], wd.res[h]], [po.r])
                    c.op("dve", lambda v, po=po, j=j, n=n: v.tensor_tensor(
                        out=xo[j][:, n * 512:(n + 1) * 512], in0=po[:, :], in1=xt[j][:, n * 512:(n + 1) * 512],
                        op=ALU.add), [po.r, xt[j].r], [xo[j].r])
                c.dma("sp", dst[ti * 128:(ti + 1) * 128, :], xo[j][:], [xo[j].r], [], ("ffn_o", j))


    def declare(self, depth):
        ne, no = (depth + 1) // 2, depth // 2
        T = self.T
        shapes = {
            "x": [T, D], "pos": [T], "rmask": [128, 4], "hsel": [128, 4],
            "c_ident": [128, 128], "c_inv": [16], "c_kpos": [128, 4 * T // 128], "c_triu": [128, 128],
            "mix_norm_even": [ne, D], "w_in_even": [ne, D, 5664], "conv_w": [ne, 4, 1536], "conv_b": [ne, 1536],
            "dt_bias": [ne, 16], "a_log": [ne, 16], "d_skip": [ne, 16], "ssd_norm": [ne, 1024],
            "gla_gk_w2": [ne, 16, 512], "gla_gk_b": [ne, 512], "gla_norm": [ne, 256], "w_out_even": [ne, 2048, D],
            "mix_norm_odd": [max(no, 1), D], "w_dqkv": [max(no, 1), D, 672], "q_lora_norm": [max(no, 1), 384],
            "w_uq": [max(no, 1), 384, 1536], "kv_lora_norm": [max(no, 1), 256], "w_ukv": [max(no, 1), 256, 2048],
            "q_nope_norm": [max(no, 1), 64], "q_rope_norm": [max(no, 1), 32], "k_nope_norm": [max(no, 1), 64],
            "k_rope_norm": [max(no, 1), 32], "w_o_mla": [max(no, 1), 1024, D],
            "ffn_norm": [depth, D], "w_gate": [depth, D, FH], "w_up": [depth, D, FH], "w_down": [depth, FH, D],
        }
        self.d = {k: self.inp(k, v).ap() for k, v in shapes.items()}
        self.y = self.nc.dram_tensor("y", [T, D], F32, kind="ExternalOutput").ap()
        nc = self.nc
        NB = 4 * T
        self.xs = nc.dram_tensor("xs", [T, D], F32).ap()
        self.NCH = max(1, (288 * T * 2 + 786431) // 786432)
        while T % (self.NCH * 512) != 0:
            self.NCH += 1
        self.TC = T // self.NCH
        self.lat_l = [nc.dram_tensor("lat_l%d" % q, [288, self.TC], BF16) for q in range(self.NCH)]
        self.lat_g = [nc.dram_tensor("lat_g%d" % q, [4 * 288, self.TC], BF16) for q in range(self.NCH)]
        self.qT_d = nc.dram_tensor("qT_d", [16, 96, T], BF16).ap()
        self.aT_d = nc.dram_tensor("aT_d", [16, 64, T], BF16).ap()
        self.KT_d = nc.dram_tensor("KT_d", [16, 96, NB], BF16).ap()
        self.V_d = nc.dram_tensor("V_d", [NB // 128, 128, 16, 65], BF16).ap()
        self.st_l = nc.dram_tensor("st_l", [128, 1040], F32)
        self.st_g = nc.dram_tensor("st_g", [4 * 128, 1040], F32)
        self.sg_l = nc.dram_tensor("sg_l", [128, 1040], F32)
        self.sg_g = nc.dram_tensor("sg_g", [4 * 128, 1040], F32)
        self.hl_l = nc.dram_tensor("hl_l", [16, D], F32)
        self.hl_g = nc.dram_tensor("hl_g", [4 * 16, D], F32)
        self.fn_l = nc.dram_tensor("fn_l", [16, 64], F32)
        self.fn_g = nc.dram_tensor("fn_g", [64, 64], F32)
        self.dly_a = nc.dram_tensor("dly_a", [128, 2048], F32).ap()
        self.dly_b = nc.dram_tensor("dly_b", [128, 2048], F32).ap()
        self.cc_dom = Dom(self.stack.enter_context(nc.semaphore("cc")), 1)
        self.groups = [[0, 1, 2, 3], [4, 5, 6, 7]]
        self.dres = Res()

    def bload(self, src_row_ap, n, key="bl"):
        b = self.c.sb([128, n], F32)
        self.c.dma("sp", b[:], src_row_ap.partition_broadcast(128), [], [b.r], key)
        return b

    def allgather(self, src_t, dst_t):
        c = self.c
        c.barrier()
        c.op("pool", lambda g: g.collective_compute("AllGather", ALU.bypass, replica_groups=self.groups,
                                                      ins=[src_t.ap().opt()], outs=[dst_t.ap().opt()]),
             [], [], dom=self.cc_dom)
        c.barrier()
        c.op("pool", lambda g: g.collective_compute("AllGather", ALU.bypass, replica_groups=self.groups,
                                                      ins=[self.fn_l.ap().opt()], outs=[self.fn_g.ap().opt()]),
             [], [], dom=self.cc_dom)
        c.barrier()
        for i in range(16):
            a, b = (self.dly_a, self.dly_b) if i % 2 == 0 else (self.dly_b, self.dly_a)
            c.dma("sp", b[:, :], a[:, :], [], [], "dly")
        c.barrier()

    def norm_tile_T(self, xt, nw, hb, hT, col0, P=128):
        c = self.c
        rmsnorm(c, xt[0:P, :], P, D, nw[0:P, :], hb[0:P, :], self.junk, self.ss, [xt.r], [hb.r], nw.r)
        pT = c.psum()
        pTv = pT[:].bitcast(BF16)
        for k in range(8):
            tr(c, pTv[:, k * P:(k + 1) * P], hb[0:P, k * 128:(k + 1) * 128], self.ident[0:P, 0:P],
               [hb.r, self.ident.r], [pT.r])
        c.op("act", lambda a: a.copy(out=hT[:, :, col0:col0 + P],
                                     in_=pTv[:, 0:8 * P].rearrange("p (k t) -> p k t", k=8)), [pT.r], [hT.r])

    def rope(self, x, Hn, cos_ap, sin_ap, out, ta, tb, xres, ores):
        c = self.c
        x1, x2 = x[:, :, 0:16], x[:, :, 16:32]
        cb = cos_ap.unsqueeze(1).to_broadcast([128, Hn, 16])
        sb_ = sin_ap.unsqueeze(1).to_broadcast([128, Hn, 16])
        a, b = ta[:, 0:Hn, :], tb[:, 0:Hn, :]
        v_tt(c, a, x1, cb, ALU.mult, [xres, self.cs.r, self.sn.r], [ta.r])
        v_tt(c, b, x2, sb_, ALU.mult, [xres, self.cs.r, self.sn.r], [tb.r])
        v_tt(c, out[:, :, 0:16], a, b, ALU.subtract, [ta.r, tb.r], [ores])
        v_tt(c, a, x2, cb, ALU.mult, [xres, self.cs.r, self.sn.r], [ta.r])
        v_tt(c, b, x1, sb_, ALU.mult, [xres, self.cs.r, self.sn.r], [tb.r])
        v_tt(c, out[:, :, 16:32], a, b, ALU.add, [ta.r, tb.r], [ores])

    def mla_m1(self, j, xs):
        c, T, d = self.c, self.T, self.d
        NTL = T // 128
        PI = float(np.pi)
        wdq = c.sb([128, 8, 672], BF16, nres=8)
        for k in range(8):
            c.dma("pool", wdq[:, k, :], d["w_dqkv"][j, k * 128:(k + 1) * 128, :], [], [wdq.res[k]], "w")
        wuq = c.sb([128, 3, 1536], BF16, nres=3)
        for k in range(3):
            c.dma("pool", wuq[:, k, :], d["w_uq"][j, k * 128:(k + 1) * 128, :], [], [wuq.res[k]], "w")
        nw = self.bload(d["mix_norm_odd"][j, :], D)
        qln = self.bload(d["q_lora_norm"][j, :], 384)
        kvln = self.bload(d["kv_lora_norm"][j, :], 256)
        gq = c.sb([128, 96], F32)
        c.dma("sp", gq[:, 0:64], d["q_nope_norm"][j, :].partition_broadcast(128), [], [gq.r], "bl")
        c.dma("sp", gq[:, 64:96], d["q_rope_norm"][j, :].partition_broadcast(128), [], [gq.r], "bl")
        v_ts(c, gq[:], gq[:], 96.0 ** -0.5, None, ALU.mult, None, [gq.r], [gq.r])
        gkr = self.bload(d["k_rope_norm"][j, :], 32)
        inv = self.bload(d["c_inv"], 16)
        posc = self.tload(d["pos"].rearrange("(n p) -> n p", p=128), NTL)
        ang = c.sb([128, NTL, 16], F32)
        self.cs = c.sb([128, NTL, 16], F32)
        self.sn = c.sb([128, NTL, 16], F32)
        v_tt(c, ang[:], inv[:].unsqueeze(1).to_broadcast([128, NTL, 16]),
             posc[:].unsqueeze(2).to_broadcast([128, NTL, 16]), ALU.mult, [inv.r, posc.r], [ang.r])
        kf = c.sb([128, NTL, 16], F32)
        msk = c.sb([128, NTL, 16], F32)
        MAGIC = 12582912.0
        C1 = 6.28125
        C2 = 2.0 * PI - C1
        PIC = 3.141592
        v_ts(c, kf[:], ang[:], 1.0 / (2.0 * PI), None, ALU.mult, None, [ang.r], [kf.r])
        v_ts(c, kf[:], kf[:], MAGIC, None, ALU.add, None, [kf.r], [kf.r])
        v_ts(c, kf[:], kf[:], -MAGIC, None, ALU.add, None, [kf.r], [kf.r])
        v_stt(c, ang[:], kf[:], -C1, ang[:], ALU.mult, ALU.add, [kf.r, ang.r], [ang.r])
        v_stt(c, ang[:], kf[:], -C2, ang[:], ALU.mult, ALU.add, [kf.r, ang.r], [ang.r])
        v_ts(c, self.sn[:], ang[:], PIC, -PIC, ALU.min, ALU.max, [ang.r], [self.sn.r])
        a_act(c, self.sn[:], self.sn[:], AF.Sin, [self.sn.r], [self.sn.r])
        v_ts(c, self.cs[:], ang[:], PI / 2, None, ALU.add, None, [ang.r], [self.cs.r])
        v_ts(c, msk[:], self.cs[:], PI, None, ALU.is_gt, None, [self.cs.r], [msk.r])
        v_stt(c, self.cs[:], msk[:], -2.0 * PI, self.cs[:], ALU.mult, ALU.add, [msk.r, self.cs.r], [self.cs.r])
        v_ts(c, self.cs[:], self.cs[:], PIC, -PIC, ALU.min, ALU.max, [self.cs.r], [self.cs.r])
        a_act(c, self.cs[:], self.cs[:], AF.Sin, [self.cs.r], [self.cs.r])
        xt = c.sb([128, D], F32)
        hb = c.sb([128, D], BF16)
        hT = c.sb([128, 8, 128], BF16)
        lat = c.sb([128, 672], F32)
        cqn = c.sb([128, 384], BF16)
        cqT = c.sb([128, 3, 128], BF16)
        ckvn = c.sb([128, 256], BF16)
        latT = c.sb([128, 2, 128], BF16)
        kr = c.sb([128, 1, 32], F32)
        krr = c.sb([128, 1, 32], BF16)
        krT = c.sb([32, 128], BF16)
        q = c.sb([128, 16, 96], F32)
        sq = c.sb([128, 16, 96], F32)
        ssn = c.sb([128, 16], F32)
        ssr = c.sb([128, 16], F32)
        qr = c.sb([128, 16, 32], F32)
        qf = c.sb([128, 16, 96], BF16)
        ta = c.sb([128, 16, 16], F32)
        tb = c.sb([128, 16, 16], F32)
        qT = c.sb([96, 16, 128], BF16)
        for ti in range(NTL):
            tsl = slice(ti * 128, (ti + 1) * 128)
            lq = (ti * 128) // self.TC
            lat_l = self.lat_l[lq].ap()
            lsl = slice(ti * 128 - lq * self.TC, ti * 128 - lq * self.TC + 128)
            c.dma("sp", xt[:], xs[tsl, :], [], [xt.r], "m1x")
            self.norm_tile_T(xt, nw, hb, hT, 0)
            pl0, pl1 = c.psum(), c.psum()
            for k in range(8):
                mm(c, pl0[:, :], hT[:, k, :], wdq[:, k, 0:512], k == 0, k == 7, [hT.r, wdq.res[k]], [pl0.r])
            for k in range(8):
                mm(c, pl1[:, 0:160], hT[:, k, :], wdq[:, k, 512:672], k == 0, k == 7, [hT.r, wdq.res[k]], [pl1.r])
            c.op("act", lambda a, p=pl0: a.copy(out=lat[:, 0:512], in_=p[:, :]), [pl0.r], [lat.r])
            c.op("act", lambda a, p=pl1: a.copy(out=lat[:, 512:672], in_=p[:, 0:160]), [pl1.r], [lat.r])
            rmsnorm(c, lat[:, 0:384], 128, 384, qln[:], cqn[:], self.junk, self.ss, [lat.r], [cqn.r], qln.r)
            pT = c.psum()
            pTv = pT[:].bitcast(BF16)
            for k in range(3):
                tr(c, pTv[:, k * 128:(k + 1) * 128], cqn[:, k * 128:(k + 1) * 128], self.ident[:],
                   [cqn.r, self.ident.r], [pT.r])
            c.op("act", lambda a, pTv=pTv: a.copy(out=cqT[:], in_=pTv[:, 0:384].rearrange("p (k t) -> p k t", k=3)),
                 [pT.r], [cqT.r])
            rmsnorm(c, lat[:, 384:640], 128, 256, kvln[:], ckvn[:], self.junk, self.ss, [lat.r], [ckvn.r], kvln.r)
            pT2 = c.psum()
            pT2v = pT2[:].bitcast(BF16)
            for k in range(2):
                tr(c, pT2v[:, k * 128:(k + 1) * 128], ckvn[:, k * 128:(k + 1) * 128], self.ident[:],
                   [ckvn.r, self.ident.r], [pT2.r])
            c.op("act", lambda a, p=pT2v: a.copy(out=latT[:], in_=p[:, 0:256].rearrange("p (k t) -> p k t", k=2)),
                 [pT2.r], [latT.r])
            c.dma("sp", lat_l[0:256, lsl].rearrange("(k p) t -> p k t", p=128), latT[:], [latT.r], [], "m1l")
            rmsnorm(c, lat[:, 640:672], 128, 32, gkr[:], kr[:, 0, :], self.junk, self.ss, [lat.r], [kr.r], gkr.r)
            self.rope(kr, 1, self.cs[:, ti, :], self.sn[:, ti, :], krr, ta, tb, kr.r, krr.r)
            pT3 = c.psum()
            pT3v = pT3[:].bitcast(BF16)
            tr(c, pT3v[0:32, 0:128], krr[:, 0, :], self.ident[:], [krr.r, self.ident.r], [pT3.r])
            c.op("act", lambda a, p=pT3v: a.copy(out=krT[:], in_=p[0:32, 0:128]), [pT3.r], [krT.r])
            c.dma("sp", lat_l[256:288, lsl], krT[:], [krT.r], [], "m1k")
            qfl = q[:].rearrange("p h d -> p (h d)")
            for n in range(3):
                pq = c.psum()
                for k in range(3):
                    mm(c, pq[:, :], cqT[:, k, :], wuq[:, k, n * 512:(n + 1) * 512], k == 0, k == 2,
                       [cqT.r, wuq.res[k]], [pq.r])
                c.op("act", lambda a, p=pq, n=n: a.copy(out=qfl[:, n * 512:(n + 1) * 512], in_=p[:, :]),
                     [pq.r], [q.r])
            a_act(c, sq[:], q[:], AF.Square, [q.r], [sq.r])
            v_red(c, ssn[:], sq[:, :, 0:64], [sq.r], [ssn.r])
            v_red(c, ssr[:], sq[:, :, 64:96], [sq.r], [ssr.r])
            rstd_inplace(c, ssn[:], 64, ssn.r)
            rstd_inplace(c, ssr[:], 32, ssr.r)
            v_tt(c, sq[:, :, 0:64], q[:, :, 0:64], ssn[:].unsqueeze(2).to_broadcast([128, 16, 64]), ALU.mult,
                 [q.r, ssn.r], [sq.r])
            v_tt(c, qf[:, :, 0:64], sq[:, :, 0:64], gq[:, 0:64].unsqueeze(1).to_broadcast([128, 16, 64]), ALU.mult,
                 [sq.r, gq.r], [qf.r])
            v_tt(c, sq[:, :, 64:96], q[:, :, 64:96], ssr[:].unsqueeze(2).to_broadcast([128, 16, 32]), ALU.mult,
                 [q.r, ssr.r], [sq.r])
            v_tt(c, qr[:], sq[:, :, 64:96], gq[:, 64:96].unsqueeze(1).to_broadcast([128, 16, 32]), ALU.mult,
                 [sq.r, gq.r], [qr.r])
            self.rope(qr, 16, self.cs[:, ti, :], self.sn[:, ti, :], qf[:, :, 64:96], ta, tb, qr.r, qf.r)
            for half in range(2):
                pt = c.psum()
                ptv = pt[:].bitcast(BF16)
                for hh in range(8):
                    tr(c, ptv[0:96, hh * 128:(hh + 1) * 128], qf[:, half * 8 + hh, :], self.ident[:],
                       [qf.r, self.ident.r], [pt.r])
                c.op("act", lambda a, p=ptv, half=half: a.copy(
                    out=qT[:, half * 8:half * 8 + 8, :], in_=p[0:96, :].rearrange("p (h t) -> p h t", h=8)),
                    [pt.r], [qT.r])
            c.dma("sp", self.qT_d[:, :, tsl].rearrange("h p t -> p h t"), qT[:], [qT.r], [], "m1q")

    def mla_m2(self, j):
        c, T, d = self.c, self.T, self.d
        NB = 4 * T
        wukv = c.sb([128, 2, 2048], BF16, nres=2)
        for k in range(2):
            c.dma("pool", wukv[:, k, :], d["w_ukv"][j, k * 128:(k + 1) * 128, :], [], [wukv.res[k]], "w")
        gk = self.bload(d["k_nope_norm"][j, :], 64)
        ckvT = [c.sb([128, 2, 512], BF16) for _ in range(2)]
        KTs = [c.sb([96, 16, 512], BF16) for _ in range(2)]
        vaug = [c.sb([128, 4, 16, 65], BF16) for _ in range(2)]
        for b in range(2):
            c.op("dve", lambda v, b=b: v.memset(vaug[b][:, :, :, 64:65], 1.0), [], [vaug[b].r])
        sq = c.sb([128, 4, 64], F32)
        ssk = c.sb([128, 16], F32)
        tmp = c.sb([128, 4, 64], F32)
        kn = c.sb([128, 16, 64], BF16)
        for st in range(NB // 512):
            b = st % 2
            r = (st * 512) // T
            t0 = st * 512 - r * T
            lq = t0 // self.TC
            lat_g = self.lat_g[lq].ap()
            t0 = t0 - lq * self.TC
            base = r * 288
            c.dma("sp", ckvT[b][:], lat_g[base:base + 256, t0:t0 + 512].rearrange("(k p) t -> p k t", p=128),
                  [], [ckvT[b].r], ("m2c", b))
            for h in range(16):
                c.dma("sp", KTs[b][64:96, h, :], lat_g[base + 256:base + 288, t0:t0 + 512], [], [KTs[b].r],
                      ("m2r", b))
            for blk in range(4):
                bs = slice(blk * 128, (blk + 1) * 128)
                pk = [c.psum() for _ in range(4)]
                for n in range(4):
                    for k in range(2):
                        mm(c, pk[n][:, :], ckvT[b][:, k, bs], wukv[:, k, n * 512:(n + 1) * 512], k == 0, k == 1,
                           [ckvT[b].r, wukv.res[k]], [pk[n].r])
                for n in range(4):
                    pv = pk[n][:, :].rearrange("p (h d) -> p h d", h=4)
                    a_act(c, sq[:], pv[:, :, 0:64], AF.Square, [pk[n].r], [sq.r])
                    v_red(c, ssk[:, 4 * n:4 * n + 4], sq[:], [sq.r], [ssk.r])
                    c.op("act", lambda a, pv=pv, n=n, blk=blk, b=b: a.copy(
                        out=vaug[b][:, blk, 4 * n:4 * n + 4, 0:64], in_=pv[:, :, 64:128]), [pk[n].r], [vaug[b].r])
                rstd_inplace(c, ssk[:], 64, ssk.r)
                for n in range(4):
                    pv = pk[n][:, :].rearrange("p (h d) -> p h d", h=4)
                    v_tt(c, tmp[:], pv[:, :, 0:64], ssk[:, 4 * n:4 * n + 4].unsqueeze(2).to_broadcast([128, 4, 64]),
                         ALU.mult, [pk[n].r, ssk.r], [tmp.r])
                    v_tt(c, kn[:, 4 * n:4 * n + 4, :], tmp[:], gk[:].unsqueeze(1).to_broadcast([128, 4, 64]),
                         ALU.mult, [tmp.r, gk.r], [kn.r])
                for half in range(2):
                    pt = c.psum()
                    ptv = pt[:].bitcast(BF16)
                    for hh in range(8):
                        tr(c, ptv[0:64, hh * 128:(hh + 1) * 128], kn[:, half * 8 + hh, :], self.ident[:],
                           [kn.r, self.ident.r], [pt.r])
                    c.op("act", lambda a, p=ptv, half=half, b=b, bs=bs: a.copy(
                        out=KTs[b][0:64, half * 8:half * 8 + 8, bs],
                        in_=p[0:64, :].rearrange("p (h t) -> p h t", h=8)), [pt.r], [KTs[b].r])
            c.dma("sp", self.KT_d[:, :, st * 512:(st + 1) * 512].rearrange("h p t -> p h t"), KTs[b][:],
                  [KTs[b].r], [], ("m2k", b))
            c.dma("sp", self.V_d[st * 4:(st + 1) * 4].rearrange("n p h c -> p n h c"), vaug[b][:],
                  [vaug[b].r], [], ("m2v", b))

    def mla_m3(self, j):
        c, T, d = self.c, self.T, self.d
        NB = 4 * T
        NBLK = NB // 128
        NG = T // 512
        KT = [c.sb([96, NB], BF16) for _ in range(2)]
        Vh = [c.sb([128, NBLK, 65], BF16) for _ in range(2)]
        qrow = self.bload(d["pos"], T)
        kcols = c.sb([128, NBLK], F32)
        c.dma("sp", kcols[:], d["c_kpos"][:, :], [], [kcols.r], "bl")
        ones = c.sb([128, 64], F32)
        c.op("dve", lambda v: v.memset(ones[:], 1.0), [], [ones.r])
        qT = [c.sb([96, 512], BF16) for _ in range(2)]
        pTs = [c.sb([128, 512], BF16) for _ in range(6)]
        rs = c.sb([128, 512], F32)
        rb = c.sb([64, 512], F32)
        oT = [c.sb([64, 512], BF16) for _ in range(2)]
        it = 0
        ig = 0
        for h in range(16):
            b = h % 2
            c.dma("sp", KT[b][:], self.KT_d[h], [], [KT[b].r], ("m3k", b))
            c.dma("sp", Vh[b][:], self.V_d[:, :, h, :].rearrange("n p c -> p n c"), [], [Vh[b].r], ("m3v", b))
            for g in range(NG):
                qb = qT[ig % 2]
                ob = oT[ig % 2]
                gs = slice(g * 512, (g + 1) * 512)
                c.dma("sp", qb[:], self.qT_d[h, :, gs], [], [qb.r], ("m3q", ig % 2))
                kmax = (3 * T + (g + 1) * 512) // 128
                po = c.psum_acc(ig % 2)
                LA = 3
                pT_of = {}
                for step in range(kmax + LA):
                    if step < kmax:
                        kb = step
                        ps = c.psum()
                        mm(c, ps[:, :], KT[b][:, kb * 128:(kb + 1) * 128], qb[:], True, True, [KT[b].r, qb.r], [ps.r])
                        pT = pTs[it % len(pTs)]
                        it += 1
                        pT_of[kb] = pT
                        a_act(c, pT[:], ps[:, :], AF.Exp, [ps.r], [pT.r])
                        if kb * 128 + 127 > g * 512:
                            v_stt(c, pT[:], qrow[:, gs], kcols[:, kb:kb + 1], pT[:], ALU.is_ge, ALU.mult,
                                  [pT.r, qrow.r, kcols.r], [pT.r])
                    if step >= LA:
                        kb = step - LA
                        pT = pT_of.pop(kb)
                        mm(c, po[0:65, :], Vh[b][:, kb, :], pT[:], kb == 0, kb == kmax - 1, [Vh[b].r, pT.r], [po.r])
                c.op("act", lambda a, po=po: a.copy(out=rs[64:65, :], in_=po[64:65, :]), [po.r], [rs.r])
                v_recip(c, rs[64:65, :], rs[64:65, :], [rs.r], [rs.r])
                pb = c.psum()
                mm(c, pb[0:64, :], ones[64:65, 0:64], rs[64:65, :], True, True, [ones.r, rs.r], [pb.r])
                c.op("act", lambda a, pb=pb: a.copy(out=rb[:], in_=pb[0:64, :]), [pb.r], [rb.r])
                v_tt(c, ob[:], po[0:64, :], rb[:], ALU.mult, [po.r, rb.r], [ob.r])
                c.dma("sp", self.aT_d[h, :, gs], ob[:], [ob.r], [], ("m3o", ig % 2))
                ig += 1

    def mla_m4(self, j, xs, dst):
        c, T, d = self.c, self.T, self.d
        wo = c.sb([64, 16, D], BF16, nres=16)
        for h in range(16):
            c.dma("pool", wo[:, h, :], d["w_o_mla"][j, h * 64:(h + 1) * 64, :], [], [wo.res[h]], "w")
        aT = [c.sb([64, 16, 128], BF16) for _ in range(2)]
        xt = [c.sb([128, D], F32) for _ in range(2)]
        xo = [c.sb([128, D], F32) for _ in range(2)]
        for ti in range(T // 128):
            b = ti % 2
            tsl = slice(ti * 128, (ti + 1) * 128)
            c.dma("sp", aT[b][:], self.aT_d[:, :, tsl].rearrange("h p t -> p h t"), [], [aT[b].r], ("m4a", b))
            c.dma("sp", xt[b][:], xs[tsl, :], [], [xt[b].r], ("m4x", b))
            for n in range(2):
                po = c.psum()
                for h in range(16):
                    mm(c, po[:, :], aT[b][:, h, :], wo[:, h, n * 512:(n + 1) * 512], h == 0, h == 15,
                       [aT[b].r, wo.res[h]], [po.r])
                v_tt(c, xo[b][:, n * 512:(n + 1) * 512], po[:, :], xt[b][:, n * 512:(n + 1) * 512], ALU.add,
                     [po.r, xt[b].r], [xo[b].r])
            c.dma("sp", dst[tsl, :], xo[b][:], [xo[b].r], [], ("m4o", b))

    def mla_layer(self, j, xs):
        c = self.c
        with Phase(c):
            self.mla_m1(j, xs)
        with Phase(c):
            for q in range(self.NCH):
                self.allgather(self.lat_l[q], self.lat_g[q])
        import os
        with Phase(c):
            if os.environ.get("KDBG") == "2":
                dl0 = self.nc.dram_tensor("dbg_l0", [4 * 288, self.TC], BF16, kind="ExternalOutput").ap()
                c.dma("sp", dl0[:, :], self.lat_g[0].ap()[:, :], [], [], "dbg")
            self.mla_m2(j)
        with Phase(c):
            self.mla_m3(j)
        import os
        if os.environ.get("KDBG") == "2":
            with Phase(c):
                dbg = self.nc.dram_tensor("dbg_a", [16, 64, self.T], BF16, kind="ExternalOutput").ap()
                c.dma("sp", dbg[:, :, :], self.aT_d[:, :, :], [], [], "dbg")
                dbq = self.nc.dram_tensor("dbg_q", [16, 96, self.T], BF16, kind="ExternalOutput").ap()
                c.dma("sp", dbq[:, :, :], self.qT_d[:, :, :], [], [], "dbg")
                dbk = self.nc.dram_tensor("dbg_k", [16, 96, 4 * self.T], BF16, kind="ExternalOutput").ap()
                c.dma("sp", dbk[:, :, :], self.KT_d[:, :, :], [], [], "dbg")
                dbv = self.nc.dram_tensor("dbg_v", [4 * self.T // 128, 128, 16, 65], BF16, kind="ExternalOutput").ap()
                c.dma("sp", dbv[:, :, :, :], self.V_d[:, :, :, :], [], [], "dbg")
        with Phase(c):
            self.mla_m4(j, xs, xs)


    def ncload(self, shape, src_ap, key="bl"):
        b = self.c.sb(shape, F32)
        self.c.dma_op("sp", lambda e: e.dma_start(out=b[:], in_=src_ap, allow_slow_non_contiguous=True), [], [b.r], key)
        return b

    def tload(self, src_rows_ap, n):
        c = self.c
        rows = c.sb([128, 128], F32)
        c.dma("sp", rows[0:n, :], src_rows_ap, [], [rows.r], "bl")
        pt = c.psum()
        tr(c, pt[:, 0:n], rows[0:n, :], self.identf[0:n, 0:n], [rows.r, self.identf.r], [pt.r])
        out = c.sb([128, n], F32)
        c.op("act", lambda a: a.copy(out=out[:], in_=pt[:, 0:n]), [pt.r], [out.r])
        return out

    def wload(self, j, c0, c1, nk=8):
        c = self.c
        w = c.sb([128, nk, c1 - c0], BF16, nres=nk)
        for k in range(nk):
            c.dma("pool", w[:, k, :], self.d["w_in_even"][j, k * 128:(k + 1) * 128, c0:c1], [], [w.res[k]], "w")
        return w

    def halo_phase(self, xs):
        c, T = self.c, self.T
        with Phase(c):
            c.dma("sp", self.hl_l.ap()[0:16, :], xs[T - 16:T, :], [], [], "hl")
        with Phase(c):
            self.allgather(self.hl_l, self.hl_g)

    def halo_x(self):
        c, d = self.c, self.d
        g4 = c.sb([3, 4, D], F32)
        c.dma("sp", g4[:], self.hl_g.ap().rearrange("(r p) n -> p r n", p=16)[13:16], [], [g4.r], "hl")
        hs = c.sb([128, 4], F32)
        c.dma("sp", hs[:], d["hsel"][:, :], [], [hs.r], "bl")
        hx = c.sb([128, D], F32)
        c.op("dve", lambda v: v.memset(hx[0:4, :], 0.0), [], [hx.r])
        v_ts(c, hx[0:3, :], g4[:, 0, :], hs[0:3, 0:1], None, ALU.mult, None, [g4.r, hs.r], [hx.r])
        for r in range(1, 4):
            v_stt(c, hx[0:3, :], g4[:, r, :], hs[0:3, r:r + 1], hx[0:3, :], ALU.mult, ALU.add, [g4.r, hs.r, hx.r],
                  [hx.r])
        return hx

    def ssd_pass(self, j, xs, final):
        c, T, d = self.c, self.T, self.d
        NT = 256
        wx = self.wload(j, 1024, 2560)
        wdt = self.wload(j, 2560, 2576)
        wz = self.wload(j, 0, 1024) if final else None
        nw = self.bload(d["mix_norm_even"][j, :], D)
        cw = self.tload(d["conv_w"][j].rearrange("t (c p) -> (t c) p", p=128), 48)
        cb = self.tload(d["conv_b"][j].rearrange("(c p) -> c p", p=128), 12)
        dtb = self.bload(d["dt_bias"][j, :], 16)
        negA = self.bload(d["a_log"][j, :], 16)
        a_act(c, negA[:], negA[:], AF.Exp, [negA.r], [negA.r])
        v_ts(c, negA[:], negA[:], -1.0, None, ALU.mult, None, [negA.r], [negA.r])
        dsk = self.bload(d["d_skip"][j, :], 16)
        snw = self.bload(d["ssd_norm"][j, :], 1024)
        triu = c.sb([128, 128], F32)
        c.dma("sp", triu[:], d["c_triu"][:, :], [], [triu.r], "bl")
        ones = c.sb([128, 128], F32)
        c.op("dve", lambda v: v.memset(ones[:], 1.0), [], [ones.r])
        Hs = c.sb([128, 16, 64], F32)
        Hsb = c.sb([128, 16, 64], BF16)
        Dtot = c.sb([128, 16], F32)
        c.op("dve", lambda v: v.memset(Hs[:], 0.0), [], [Hs.r])
        c.op("dve", lambda v: v.memset(Dtot[:], 1.0), [], [Dtot.r])
        if final:
            rm = c.sb([128, 4], F32)
            c.dma("sp", rm[:], d["rmask"][:, :], [], [rm.r], "bl")
            G = c.sb([128, 1040], F32)
            Dp = c.sb([128, 16], F32)
            for r in range(4):
                c.dma("sp", G[:, 0:1024], self.st_g.ap()[r * 128:(r + 1) * 128, 0:1024], [], [G.r], "stg")
                c.dma("sp", G[:, 1024:1040], self.st_g.ap()[r * 128:(r + 1) * 128, 1024:1040], [], [G.r], "stg")
                v_ts(c, Dp[:], G[:, 1024:1040], -1.0, rm[:, r:r + 1], ALU.add, ALU.mult, [G.r, rm.r], [Dp.r])
                v_ts(c, Dp[:], Dp[:], 1.0, None, ALU.add, None, [Dp.r], [Dp.r])
                v_tt(c, Hs[:], Hs[:], Dp[:].unsqueeze(2).to_broadcast([128, 16, 64]), ALU.mult, [Hs.r, Dp.r], [Hs.r])
                v_stt(c, Hs[:], G[:, 0:1024].rearrange("p (h d) -> p h d", h=16), rm[:, r:r + 1], Hs[:],
                      ALU.mult, ALU.add, [G.r, rm.r, Hs.r], [Hs.r])
        v_copy(c, Hsb[:], Hs[:], [Hs.r], [Hsb.r])
        xt = c.sb([128, D], F32)
        hb = c.sb([128, D], BF16)
        hT = c.sb([128, 8, NT], BF16)
        xbc = c.sb([128, 12, NT + 3], F32, nres=12)
        acc = [c.sb([128, NT], F32) for _ in range(2)]
        xc = c.sb([128, 12, NT], BF16, nres=12)
        hx = self.halo_x()
        hTh = c.sb([128, 8, 4], BF16)
        self.norm_tile_T(hx, nw, hb, hTh, 0, P=4)
        for cc in range(12):
            pc = c.psum()
            for k in range(8):
                mm(c, pc[:, 0:4], wx[:, k, cc * 128:(cc + 1) * 128], hTh[:, k, 0:4], k == 0, k == 7,
                   [wx.res[k], hTh.r], [pc.r])
            c.op("act", lambda a, pc=pc, cc=cc: a.copy(out=xbc[:, cc, 0:3], in_=pc[:, 0:3]), [pc.r], [xbc.res[cc]])
        sz = c.sb([128, 1024], F32) if final else None
        dt = c.sb([128, 16], F32)
        av = c.sb([128, 16], F32)
        acol = c.sb([128, 16], F32)
        arhs = c.sb([128, 16, 128], F32)
        arow = c.sb([128, 16, 128], F32)
        cdrow = c.sb([128, 16], F32)
        te = c.sb([128, 16], F32)
        Btok = c.sb([128, 2, 128], BF16)
        xdt = c.sb([128, 16, 64], F32)
        xdtw = c.sb([128, 16, 64], BF16)
        if final:
            xdtb = c.sb([128, 16, 64], BF16)
            xsk = c.sb([128, 16, 64], F32)
            CBm = c.sb([128, 2, 128], F32)
            erow = c.sb([128, 16, 128], BF16)
            CeT = c.sb([128, 16, 128], BF16)
            MT = c.sb([128, 16, 128], BF16)
            ysb = c.sb([128, 1024], F32)
            ssg = c.sb([128, 4], F32)
            yo = [c.sb([128, 1024], BF16) for _ in range(2)]
        arf = arhs[:].rearrange("p h l -> p (h l)")
        awf = arow[:].rearrange("p h l -> p (h l)")
        for st in range(T // NT):
            for jt in range(2):
                ti = st * 2 + jt
                c.dma("sp", xt[:], xs[ti * 128:(ti + 1) * 128, :], [], [xt.r], "ex")
                self.norm_tile_T(xt, nw, hb, hT, jt * 128)
            for cc in range(12):
                pc = c.psum()
                for k in range(8):
                    mm(c, pc[:, 0:NT], wx[:, k, cc * 128:(cc + 1) * 128], hT[:, k, :], k == 0, k == 7,
                       [wx.res[k], hT.r], [pc.r])
                c.op("act", lambda a, pc=pc, cc=cc: a.copy(out=xbc[:, cc, 3:NT + 3], in_=pc[:, 0:NT]),
                     [pc.r], [xbc.res[cc]])
                e = "dve"
                ac = acc[cc % 2]
                v_ts(c, ac[:], xbc[:, cc, 0:NT], cw[:, cc:cc + 1], None, ALU.mult, None,
                     [xbc.res[cc], cw.r], [ac.r], e=e)
                for tap in range(1, 4):
                    v_stt(c, ac[:], xbc[:, cc, tap:tap + NT], cw[:, tap * 12 + cc:tap * 12 + cc + 1], ac[:], ALU.mult, ALU.add,
                          [xbc.res[cc], cw.r, ac.r], [ac.r], e=e)
                a_act(c, xc[:, cc, :], ac[:], AF.Silu, [ac.r, cb.r], [xc.res[cc]], bias=cb[:, cc:cc + 1])
            v_copy(c, xbc[:, :, 0:3], xbc[:, :, NT:NT + 3], xbc.res, xbc.res)
            for jt in range(2):
                ti = st * 2 + jt
                ts = slice(jt * 128, (jt + 1) * 128)
                pd = c.psum()
                for k in range(8):
                    mm(c, pd[:, 0:16], hT[:, k, ts], wdt[:, k, :], k == 0, k == 7, [hT.r, wdt.res[k]], [pd.r])
                if final:
                    for n in range(2):
                        pz = c.psum()
                        for k in range(8):
                            mm(c, pz[:, :], hT[:, k, ts], wz[:, k, n * 512:(n + 1) * 512], k == 0, k == 7,
                               [hT.r, wz.res[k]], [pz.r])
                        a_act(c, sz[:, n * 512:(n + 1) * 512], pz[:, :], AF.Silu, [pz.r], [sz.r])
                v_tt(c, dt[:], pd[:, 0:16], dtb[:], ALU.add, [pd.r, dtb.r], [dt.r])
                a_act(c, dt[:], dt[:], AF.Exp, [dt.r], [dt.r])
                a_act(c, dt[:], dt[:], AF.Ln, [dt.r], [dt.r], bias=1.0)
                v_tt(c, av[:], dt[:], negA[:], ALU.mult, [dt.r, negA.r], [av.r])
                pa = c.psum()
                mm(c, pa[:, 0:16], triu[:], av[:], True, True, [triu.r, av.r], [pa.r])
                v_copy(c, acol[:], pa[:, 0:16], [pa.r], [acol.r])
                v_tt(c, arhs[:], triu[:].unsqueeze(1).to_broadcast([128, 16, 128]),
                     av[:].unsqueeze(2).to_broadcast([128, 16, 128]), ALU.mult, [triu.r, av.r], [arhs.r])
                for n in range(4):
                    pr = c.psum()
                    mm(c, pr[:, :], ones[:], arf[:, n * 512:(n + 1) * 512], True, True, [ones.r, arhs.r], [pr.r])
                    c.op("act", lambda a, pr=pr, n=n: a.copy(out=awf[:, n * 512:(n + 1) * 512], in_=pr[:, :]),
                         [pr.r], [arow.r])
                a_act(c, cdrow[:], arow[:, :, 127], AF.Exp, [arow.r], [cdrow.r])
                v_tt(c, te[:], arow[:, :, 127], acol[:], ALU.subtract, [arow.r, acol.r], [te.r])
                a_act(c, te[:], te[:], AF.Exp, [te.r], [te.r])
                pxs = c.psum()
                pxv = pxs[:].bitcast(BF16)
                for cc in range(8):
                    tr(c, pxv[:, cc * 128:(cc + 1) * 128], xc[:, cc, ts], self.ident[:], [xc.res[cc], self.ident.r],
                       [pxs.r])
                pB = c.psum()
                pBv = pB[:].bitcast(BF16)
                for g in range(2):
                    tr(c, pBv[:, g * 128:(g + 1) * 128], xc[:, 8 + g, ts], self.ident[:],
                       [xc.res[8 + g], self.ident.r], [pB.r])
                c.op("act", lambda a, pBv=pBv: a.copy(out=Btok[:], in_=pBv[:, 0:256].rearrange("p (g n) -> p g n", g=2)),
                     [pB.r], [Btok.r])
                xv = pxv[:, 0:1024].rearrange("p (h d) -> p h d", h=16)
                v_tt(c, xdt[:], xv, dt[:].unsqueeze(2).to_broadcast([128, 16, 64]), ALU.mult, [pxs.r, dt.r], [xdt.r])
                v_tt(c, xdtw[:], xdt[:], te[:].unsqueeze(2).to_broadcast([128, 16, 64]), ALU.mult, [xdt.r, te.r],
                     [xdtw.r])
                if final:
                    v_copy(c, xdtb[:], xdt[:], [xdt.r], [xdtb.r], e="pool")
                    v_tt(c, xsk[:], xv, dsk[:].unsqueeze(2).to_broadcast([128, 16, 64]), ALU.mult, [pxs.r, dsk.r],
                         [xsk.r])
                    pcb = c.psum()
                    for g in range(2):
                        mm(c, pcb[:, g * 128:(g + 1) * 128], xc[:, 8 + g, ts], xc[:, 10 + g, ts], True, True,
                           [xc.res[8 + g], xc.res[10 + g]], [pcb.r])
                    v_tt(c, CBm[:], pcb[:, 0:256].rearrange("p (g l) -> p g l", g=2),
                         triu[:].unsqueeze(1).to_broadcast([128, 2, 128]), ALU.mult, [pcb.r, triu.r], [CBm.r])
                    a_act(c, erow[:], arow[:], AF.Exp, [arow.r], [erow.r])
                    for g in range(2):
                        v_tt(c, CeT[:, 8 * g:8 * g + 8, :], erow[:, 8 * g:8 * g + 8, :],
                             xc[:, 10 + g, ts].unsqueeze(1).to_broadcast([128, 8, 128]), ALU.mult,
                             [erow.r, xc.res[10 + g]], [CeT.r], e="pool")
                    v_tt(c, arhs[:], arow[:], acol[:].unsqueeze(2).to_broadcast([128, 16, 128]), ALU.subtract,
                         [arow.r, acol.r], [arhs.r])
                    v_ts(c, arhs[:], arhs[:], 0.0, None, ALU.min, None, [arhs.r], [arhs.r])
                    a_act(c, arhs[:], arhs[:], AF.Exp, [arhs.r], [arhs.r])
                    for g in range(2):
                        v_tt(c, MT[:, 8 * g:8 * g + 8, :], arhs[:, 8 * g:8 * g + 8, :],
                             CBm[:, g, :].unsqueeze(1).to_broadcast([128, 8, 128]), ALU.mult, [arhs.r, CBm.r], [MT.r])
                    py = [c.psum(), c.psum()]
                    for h in range(16):
                        p_ = py[h // 8]
                        col = (h % 8) * 64
                        mm(c, p_[:, col:col + 64], MT[:, h, :], xdtb[:, h, :], True, False, [MT.r, xdtb.r], [p_.r])
                        mm(c, p_[:, col:col + 64], CeT[:, h, :], Hsb[:, h, :], False, True, [CeT.r, Hsb.r], [p_.r])
                    xskf = xsk[:].rearrange("p h d -> p (h d)")
                    for n in range(2):
                        v_tt(c, ysb[:, n * 512:(n + 1) * 512], py[n][:, :], xskf[:, n * 512:(n + 1) * 512], ALU.add,
                             [py[n].r, xsk.r], [ysb.r])
                    v_tt(c, ysb[:], ysb[:], sz[:], ALU.mult, [ysb.r, sz.r], [ysb.r])
                    for g in range(2):
                        a_act(c, self.junk[:, 0:512], ysb[:, g * 512:(g + 1) * 512], AF.Square, [ysb.r],
                              [self.junk.r, ssg.r], accum_out=ssg[:, g:g + 1])
                    rstd_inplace(c, ssg[:, 0:2], 512, ssg.r)
                    yb = yo[ti % 2]
                    for g in range(2):
                        v_stt(c, yb[:, g * 512:(g + 1) * 512], ysb[:, g * 512:(g + 1) * 512], ssg[:, g:g + 1],
                              snw[:, g * 512:(g + 1) * 512], ALU.mult, ALU.mult, [ysb.r, ssg.r, snw.r], [yb.r])
                    c.dma("sp", self.ycat_d[ti * 128:(ti + 1) * 128, 0:1024], yb[:], [yb.r], [], ("yo", ti % 2))
                xwf = xdtw[:].rearrange("p h d -> p (h d)")
                for g in range(2):
                    pst = c.psum()
                    mm(c, pst[:, :], Btok[:, g, :], xwf[:, g * 512:(g + 1) * 512], True, True, [Btok.r, xdtw.r], [pst.r])
                    hg = Hs[:, 8 * g:8 * g + 8, :]
                    v_tt(c, hg, hg, cdrow[:, 8 * g:8 * g + 8].unsqueeze(2).to_broadcast([128, 8, 64]), ALU.mult,
                         [Hs.r, cdrow.r, Hsb.r], [Hs.r])
                    v_tt(c, hg, hg, pst[:, :].rearrange("p (h d) -> p h d", h=8), ALU.add, [Hs.r, pst.r], [Hs.r])
                v_copy(c, Hsb[:], Hs[:], [Hs.r], [Hsb.r])
                if not final:
                    v_tt(c, Dtot[:], Dtot[:], cdrow[:], ALU.mult, [Dtot.r, cdrow.r], [Dtot.r])
        if not final:
            c.dma("sp", self.st_l.ap()[:, 0:1024], Hs[:].rearrange("p h d -> p (h d)"), [Hs.r], [], "stl")
            c.dma("sp", self.st_l.ap()[:, 1024:1040], Dtot[:], [Dtot.r], [], "stl2")

    def gla_pass(self, j, xs, final):
        c, T, d = self.c, self.T, self.d
        NT = 256
        wq = self.wload(j, 2576, 3088)
        wk = self.wload(j, 3088, 3600)
        wv = self.wload(j, 3600, 4624)
        wg = self.wload(j, 4624, 5648) if final else None
        wgl = self.wload(j, 5648, 5664)
        nw = self.bload(d["mix_norm_even"][j, :], D)
        gkb = self.bload(d["gla_gk_b"][j, :], 512)
        gnw = self.bload(d["gla_norm"][j, :], 256)
        w2 = c.sb([16, 512], BF16)
        c.dma("pool", w2[:], d["gla_gk_w2"][j, :, :], [], [w2.r], "w")
        triu = c.sb([128, 128], F32)
        c.dma("sp", triu[:], d["c_triu"][:, :], [], [triu.r], "bl")
        Sg = c.sb([128, 4, 256], F32)
        Sgb = c.sb([128, 4, 256], BF16)
        Dg = c.sb([128, 4], F32)
        c.op("dve", lambda v: v.memset(Sg[:], 0.0), [], [Sg.r])
        c.op("dve", lambda v: v.memset(Dg[:], 1.0), [], [Dg.r])
        if final:
            rm = c.sb([128, 4], F32)
            c.dma("sp", rm[:], d["rmask"][:, :], [], [rm.r], "bl")
            G = c.sb([128, 1028], F32)
            Dp = c.sb([128, 4], F32)
            for r in range(4):
                c.dma("sp", G[:, 0:1024], self.sg_g.ap()[r * 128:(r + 1) * 128, 0:1024], [], [G.r], "stg")
                c.dma("sp", G[:, 1024:1028], self.sg_g.ap()[r * 128:(r + 1) * 128, 1024:1028], [], [G.r], "stg")
                v_ts(c, Dp[:], G[:, 1024:1028], -1.0, rm[:, r:r + 1], ALU.add, ALU.mult, [G.r, rm.r], [Dp.r])
                v_ts(c, Dp[:], Dp[:], 1.0, None, ALU.add, None, [Dp.r], [Dp.r])
                v_tt(c, Sg[:], Sg[:], Dp[:].unsqueeze(2).to_broadcast([128, 4, 256]), ALU.mult, [Sg.r, Dp.r], [Sg.r])
                v_stt(c, Sg[:], G[:, 0:1024].rearrange("p (h d) -> p h d", h=4), rm[:, r:r + 1], Sg[:],
                      ALU.mult, ALU.add, [G.r, rm.r, Sg.r], [Sg.r])
        v_copy(c, Sgb[:], Sg[:], [Sg.r], [Sgb.r])
        xt = c.sb([128, D], F32)
        hb = c.sb([128, D], BF16)
        hT = c.sb([128, 8, NT], BF16)
        qTs = c.sb([128, 4, NT], F32)
        kTs = c.sb([128, 4, NT], F32)
        vsb = c.sb([128, 1024], BF16)
        sgt = c.sb([128, 1024], F32) if final else None
        glow = c.sb([128, 16], BF16)
        glT = c.sb([16, 128], BF16)
        gk = c.sb([128, 512], F32)
        eg = c.sb([128, 4, 128], F32)
        eng = c.sb([128, 4, 128], F32)
        qd = c.sb([128, 4, 128], BF16)
        kif = c.sb([128, 4, 128], F32)
        ki = c.sb([128, 4, 128], BF16)
        ke = c.sb([128, 4, 128], BF16)
        ketok = c.sb([128, 4, 128], BF16)
        if final:
            scm = c.sb([128, 4, 128], BF16)
            ssg = c.sb([128, 4], F32)
            otmp = c.sb([128, 256], F32)
            og = [c.sb([128, 1024], BF16) for _ in range(2)]
        for st in range(T // NT):
            for jt in range(2):
                ti = st * 2 + jt
                c.dma("sp", xt[:], xs[ti * 128:(ti + 1) * 128, :], [], [xt.r], "ex")
                self.norm_tile_T(xt, nw, hb, hT, jt * 128)
            for w_, dst_ in ((wq, qTs), (wk, kTs)):
                for hh in range(4):
                    pq = c.psum()
                    for k in range(8):
                        mm(c, pq[:, 0:NT], w_[:, k, hh * 128:(hh + 1) * 128], hT[:, k, :], k == 0, k == 7,
                           [w_.res[k], hT.r], [pq.r])
                    c.op("act", lambda a, pq=pq, hh=hh, dst_=dst_: a.copy(out=dst_[:, hh, :], in_=pq[:, 0:NT]),
                         [pq.r], [dst_.r])
            for jt in range(2):
                ti = st * 2 + jt
                ts = slice(jt * 128, (jt + 1) * 128)
                for n in range(2):
                    pv = c.psum()
                    for k in range(8):
                        mm(c, pv[:, :], hT[:, k, ts], wv[:, k, n * 512:(n + 1) * 512], k == 0, k == 7,
                           [hT.r, wv.res[k]], [pv.r])
                    c.op("act", lambda a, pv=pv, n=n: a.copy(out=vsb[:, n * 512:(n + 1) * 512], in_=pv[:, :]),
                         [pv.r], [vsb.r])
                    if final:
                        pg = c.psum()
                        for k in range(8):
                            mm(c, pg[:, :], hT[:, k, ts], wg[:, k, n * 512:(n + 1) * 512], k == 0, k == 7,
                               [hT.r, wg.res[k]], [pg.r])
                        a_act(c, sgt[:, n * 512:(n + 1) * 512], pg[:, :], AF.Silu, [pg.r], [sgt.r])
                pgl = c.psum()
                for k in range(8):
                    mm(c, pgl[:, 0:16], hT[:, k, ts], wgl[:, k, :], k == 0, k == 7, [hT.r, wgl.res[k]], [pgl.r])
                c.op("act", lambda a, pgl=pgl: a.copy(out=glow[:], in_=pgl[:, 0:16]), [pgl.r], [glow.r])
                ptg = c.psum()
                ptgv = ptg[:].bitcast(BF16)
                tr(c, ptgv[0:16, 0:128], glow[:], self.ident[:], [glow.r, self.ident.r], [ptg.r])
                c.op("act", lambda a, ptgv=ptgv: a.copy(out=glT[:], in_=ptgv[0:16, 0:128]), [ptg.r], [glT.r])
                pgk = c.psum()
                mm(c, pgk[:, :], glT[:], w2[:], True, True, [glT.r, w2.r], [pgk.r])
                v_tt(c, gk[:], pgk[:, :], gkb[:], ALU.add, [pgk.r, gkb.r], [gk.r])
                a_act(c, gk[:], gk[:], AF.Exp, [gk.r], [gk.r], scale=-1.0)
                a_act(c, gk[:], gk[:], AF.Ln, [gk.r], [gk.r], bias=1.0)
                v_ts(c, gk[:], gk[:], -1.0 / 16.0, None, ALU.mult, None, [gk.r], [gk.r])
                pgc = c.psum()
                for hh in range(4):
                    mm(c, pgc[:, hh * 128:(hh + 1) * 128], gk[:, hh * 128:(hh + 1) * 128], triu[:], True, True,
                       [gk.r, triu.r], [pgc.r])
                pgv = pgc[:, :].rearrange("p (h t) -> p h t", h=4)
                a_act(c, eg[:], pgv, AF.Exp, [pgc.r], [eg.r])
                a_act(c, eng[:], pgv, AF.Exp, [pgc.r], [eng.r], scale=-1.0)
                v_stt(c, qd[:], qTs[:, :, ts], 128.0 ** -0.5, eg[:], ALU.mult, ALU.mult, [qTs.r, eg.r], [qd.r])
                v_tt(c, kif[:], kTs[:, :, ts], eng[:], ALU.mult, [kTs.r, eng.r], [kif.r])
                v_copy(c, ki[:], kif[:], [kif.r], [ki.r], e="pool")
                v_tt(c, ke[:], kif[:], eg[:, :, 127:128].to_broadcast([128, 4, 128]), ALU.mult, [kif.r, eg.r], [ke.r])
                pke = c.psum()
                pkev = pke[:].bitcast(BF16)
                for hh in range(4):
                    tr(c, pkev[:, hh * 128:(hh + 1) * 128], ke[:, hh, :], self.ident[:], [ke.r, self.ident.r], [pke.r])
                c.op("act", lambda a, pkev=pkev: a.copy(out=ketok[:], in_=pkev[:, 0:512].rearrange("p (h t) -> p h t", h=4)),
                     [pke.r], [ketok.r])
                if final:
                    psc = c.psum()
                    for hh in range(4):
                        mm(c, psc[:, hh * 128:(hh + 1) * 128], ki[:, hh, :], qd[:, hh, :], True, True, [ki.r, qd.r],
                           [psc.r])
                    v_tt(c, scm[:], psc[:, :].rearrange("p (h t) -> p h t", h=4),
                         triu[:].unsqueeze(1).to_broadcast([128, 4, 128]), ALU.mult, [psc.r, triu.r], [scm.r])
                    po = [c.psum(), c.psum()]
                    for hh in range(4):
                        p_ = po[hh // 2]
                        col = (hh % 2) * 256
                        mm(c, p_[:, col:col + 256], scm[:, hh, :], vsb[:, hh * 256:(hh + 1) * 256], True, False,
                           [scm.r, vsb.r], [p_.r])
                        mm(c, p_[:, col:col + 256], qd[:, hh, :], Sgb[:, hh, :], False, True, [qd.r, Sgb.r], [p_.r])
                    for hh in range(4):
                        p_ = po[hh // 2]
                        col = (hh % 2) * 256
                        a_act(c, self.junk[:, 0:256], p_[:, col:col + 256], AF.Square, [p_.r], [self.junk.r, ssg.r],
                              accum_out=ssg[:, hh:hh + 1])
                    rstd_inplace(c, ssg[:], 256, ssg.r)
                    ob = og[ti % 2]
                    for hh in range(4):
                        p_ = po[hh // 2]
                        col = (hh % 2) * 256
                        v_stt(c, otmp[:], p_[:, col:col + 256], ssg[:, hh:hh + 1], gnw[:], ALU.mult, ALU.mult,
                              [p_.r, ssg.r, gnw.r], [otmp.r])
                        v_tt(c, ob[:, hh * 256:(hh + 1) * 256], otmp[:], sgt[:, hh * 256:(hh + 1) * 256], ALU.mult,
                             [otmp.r, sgt.r], [ob.r])
                    c.dma("sp", self.ycat_d[ti * 128:(ti + 1) * 128, 1024:2048], ob[:], [ob.r], [], ("yo", ti % 2))
                pkv = [c.psum(), c.psum()]
                for hh in range(4):
                    p_ = pkv[hh // 2]
                    col = (hh % 2) * 256
                    mm(c, p_[:, col:col + 256], ketok[:, hh, :], vsb[:, hh * 256:(hh + 1) * 256], True, True,
                       [ketok.r, vsb.r], [p_.r])
                    v_stt(c, Sg[:, hh, :], Sg[:, hh, :], eg[:, hh, 127:128], p_[:, col:col + 256], ALU.mult, ALU.add,
                          [Sg.r, eg.r, p_.r, Sgb.r], [Sg.r])
                v_copy(c, Sgb[:], Sg[:], [Sg.r], [Sgb.r])
                if not final:
                    v_tt(c, Dg[:], Dg[:], eg[:, :, 127], ALU.mult, [Dg.r, eg.r], [Dg.r])
        if not final:
            c.dma("sp", self.sg_l.ap()[:, 0:1024], Sg[:].rearrange("p h d -> p (h d)"), [Sg.r], [], "stl")
            c.dma("sp", self.sg_l.ap()[:, 1024:1028], Dg[:], [Dg.r], [], "stl2")

    def outproj_phase(self, j, xs):
        c, T, d = self.c, self.T, self.d
        wo = c.sb([128, 16, D], BF16, nres=16)
        for k in range(16):
            c.dma("pool", wo[:, k, :], d["w_out_even"][j, k * 128:(k + 1) * 128, :], [], [wo.res[k]], "w")
        yc = [c.sb([128, 2048], BF16) for _ in range(2)]
        ycT = c.sb([128, 16, 128], BF16)
        xt = [c.sb([128, D], F32) for _ in range(2)]
        xo = [c.sb([128, D], F32) for _ in range(2)]
        for ti in range(T // 128):
            b = ti % 2
            tsl = slice(ti * 128, (ti + 1) * 128)
            c.dma("sp", yc[b][:], self.ycat_d[tsl, :], [], [yc[b].r], ("opy", b))
            c.dma("sp", xt[b][:], xs[tsl, :], [], [xt[b].r], ("opx", b))
            for half in range(2):
                pt = c.psum()
                ptv = pt[:].bitcast(BF16)
                for kk in range(8):
                    k = half * 8 + kk
                    tr(c, ptv[:, kk * 128:(kk + 1) * 128], yc[b][:, k * 128:(k + 1) * 128], self.ident[:],
                       [yc[b].r, self.ident.r], [pt.r])
                c.op("act", lambda a, ptv=ptv, half=half: a.copy(
                    out=ycT[:, half * 8:half * 8 + 8, :], in_=ptv.rearrange("p (k t) -> p k t", k=8)), [pt.r], [ycT.r])
            for n in range(2):
                po = c.psum()
                for k in range(16):
                    mm(c, po[:, :], ycT[:, k, :], wo[:, k, n * 512:(n + 1) * 512], k == 0, k == 15,
                       [ycT.r, wo.res[k]], [po.r])
                v_tt(c, xo[b][:, n * 512:(n + 1) * 512], po[:, :], xt[b][:, n * 512:(n + 1) * 512], ALU.add,
                     [po.r, xt[b].r], [xo[b].r])
            c.dma("sp", xs[tsl, :], xo[b][:], [xo[b].r], [], ("opo", b))

    def even_layer(self, j, xs):
        import os
        c = self.c
        nph = int(os.environ.get("KPH", "99"))
        if nph >= 1:
            self.halo_phase(xs)
        steps = [lambda: self.ssd_pass(j, xs, False), lambda: self.gla_pass(j, xs, False),
                 lambda: (self.allgather(self.st_l, self.st_g), self.allgather(self.sg_l, self.sg_g)), lambda: self.ssd_pass(j, xs, True),
                 lambda: self.gla_pass(j, xs, True), lambda: self.outproj_phase(j, xs)]
        for i, f in enumerate(steps):
            if nph >= i + 2:
                with Phase(c):
                    f()

    def build_all(self, depth):
        c = self.c
        self.declare(depth)
        self.ycat_d = self.nc.dram_tensor("ycat_d", [self.T, 2048], BF16).ap()
        with Phase(c):
            self.setup_consts_inner()
            c.dma("sp", self.xs[:, :], self.d["x"][:, :], [], [], "xcp")
        for i in range(depth):
            if i % 2 == 0:
                self.even_layer(i // 2, self.xs)
            else:
                self.mla_layer(i // 2, self.xs)
            with Phase(c):
                dst = self.y if i == depth - 1 else self.xs
                self.ffn_phase(i, self.xs, None, dst, None, self.d["w_gate"], self.d["w_up"], self.d["w_down"],
                               self.d["ffn_norm"])
        return self.finish()

    def setup_consts_inner(self):
        c = self.c
        outer = c.stack
        c.stack = self.stack
        idf = c.sb([128, 128], F32)
        self.identf = idf
        self.ident = c.sb([128, 128], BF16)
        c.dma("sp", idf[:], self.d["c_ident"][:, :], [], [idf.r], "const")
        c.op("dve", lambda v: v.tensor_copy(out=self.ident[:], in_=idf[:]), [idf.r], [self.ident.r])
        self.junk = c.sb([128, 1024], BF16)
        self.ss = c.sb([128, 4], F32)
        c.stack = outer

    def finish(self):
        self.c.barrier()
        self.c.flush()
        self.stack.close()
        return self.nc


def _core_inputs(inputs, SEQ, depth):
    T = SEQ // 4
    NB = SEQ
    consts = {
        "c_ident": np.eye(128, dtype=np.float32),
        "c_inv": (1.0 / (np.float32(10000.0) ** (np.arange(0, 32, 2, dtype=np.float32) / np.float32(32)))).astype(np.float32),
        "c_kpos": (np.arange(NB // 128, dtype=np.float32)[None, :] * 128 + np.arange(128, dtype=np.float32)[:, None]).astype(np.float32),
        "c_triu": np.triu(np.ones((128, 128), dtype=np.float32)),
    }
    maps = []
    for core in range(8):
        b, r = core // 4, core % 4
        m = {}
        for k, v in inputs.items():
            if k == "x":
                continue
            m[k] = np.ascontiguousarray(np.asarray(v, dtype=np.float32))
        m["x"] = np.ascontiguousarray(np.asarray(inputs["x"])[b, r * T:(r + 1) * T, :], dtype=np.float32)
        m["pos"] = np.arange(r * T, (r + 1) * T, dtype=np.float32)
        rm = np.zeros((128, 4), np.float32)
        rm[:, :r] = 1.0
        hs = np.zeros((128, 4), np.float32)
        if r > 0:
            hs[:, r - 1] = 1.0
        m["rmask"], m["hsel"] = rm, hs
        m.update(consts)
        maps.append(m)
    return maps


_CACHE = {}


def kernel(**inputs):
    x = np.asarray(inputs["x"])
    B, SEQ, _ = x.shape
    depth = int(np.asarray(inputs["ffn_norm"]).shape[0])
    T = SEQ // 4
    key = (SEQ, depth)
    if key not in _CACHE:
        _CACHE[key] = Prog(T, None).build_all(depth)
    nc = _CACHE[key]
    maps = _core_inputs(inputs, SEQ, depth)
    res = run_bass_kernel_spmd(nc, maps, core_ids=list(range(8)))
    global _LAST
    _LAST = res
    out = np.empty((B, SEQ, D), dtype=np.float32)
    for core in range(8):
        b, r = core // 4, core % 4
        out[b, r * T:(r + 1) * T, :] = res.results[core]["y"]
    return out
```

```python
import numpy as np
from contextlib import ExitStack
import concourse.bass as bass
import concourse.mybir as mybir
from concourse.bass_utils import run_bass_kernel_spmd

F32 = mybir.dt.float32
BF16 = mybir.dt.bfloat16
ALU = mybir.AluOpType
AF = mybir.ActivationFunctionType
AX = mybir.AxisListType

D = 1024
FH = 2816
EPS = 1e-6
ENGS = ("pe", "act", "dve", "pool", "sp")


class Res:
    __slots__ = ("w", "r")

    def __init__(self):
        self.w = None
        self.r = {}


class Dom:
    __slots__ = ("sem", "inc", "count", "waitall")

    def __init__(self, sem, inc):
        self.sem, self.inc, self.count = sem, inc, 0
        self.waitall = False


class Buf:
    def __init__(self, t, nres=1):
        self.t = t
        self.res = [Res() for _ in range(nres)]

    def __getitem__(self, idx):
        return self.t[idx]

    @property
    def r(self):
        return self.res[0]


class Ctx:
    def __init__(self, nc, stack):
        self.nc, self.stack = nc, stack
        self.root = stack
        self.prog = {e: [] for e in ENGS}
        self.dom = {}
        for e in ("pe", "act", "dve", "pool"):
            self.dom[e] = Dom(stack.enter_context(nc.semaphore("s_" + e)), 1)
        self.waited = {e: {} for e in ENGS}
        self.dma_doms = {}
        self.nbuf = 0
        self.psum_banks = []
        self.psum_i = 0

    def sb(self, shape, dtype, nres=1, name=None):
        self.nbuf += 1
        t = self.stack.enter_context(self.nc.sbuf_tensor(name or ("b%d" % self.nbuf), list(shape), dtype))
        return Buf(t, nres)

    def init_psum(self):
        for i in range(8):
            t = self.stack.enter_context(self.nc.psum_tensor("ps%d" % i, [128, 512], F32))
            self.psum_banks.append(Buf(t))

    def psum(self):
        b = self.psum_banks[self.psum_i % 6]
        self.psum_i += 1
        return b

    def psum_acc(self, i):
        return self.psum_banks[6 + i]

    def dma_dom(self, key):
        if key not in self.dma_doms:
            sem = self.root.enter_context(self.nc.semaphore("d%d" % len(self.dma_doms)))
            self.dma_doms[key] = Dom(sem, 16)
            self.dma_doms[key].waitall = isinstance(key, str)
        return self.dma_doms[key]

    def op(self, e, fn, reads=(), writes=(), dom=None):
        deps = {}

        def add(dm, v):
            if dm.waitall:
                v = dm.count
            if deps.get(dm, 0) < v:
                deps[dm] = v

        for r in reads:
            if r.w is not None:
                add(*r.w)
        for w in writes:
            if w.w is not None:
                add(*w.w)
            for dm, v in w.r.items():
                add(dm, v)
        own = self.dom.get(e)
        for dm, v in deps.items():
            if e == "pe" and dm is own:
                continue
            if self.waited[e].get(dm, 0) >= v:
                continue
            self.waited[e][dm] = v
            self.prog[e].append(lambda eng, s=dm.sem, v=v: eng.wait_ge(s, v))
        dm = dom if dom is not None else own
        dm.count += dm.inc
        self.prog[e].append(lambda eng, s=dm.sem, i=dm.inc: fn(eng).then_inc(s, i))
        for r in reads:
            if r.r.get(dm, 0) < dm.count:
                r.r[dm] = dm.count
        for w in writes:
            w.w = (dm, dm.count)
            w.r = {}

    def dma(self, e, out, in_, reads, writes, key):
        self.dma_op(e, lambda eng: eng.dma_start(out=out, in_=in_), reads, writes, key)

    def dma_op(self, e, fn, reads, writes, key):
        dm = self.dma_dom(key)
        if dm.waitall and dm.count > 0 and self.waited[e].get(dm, 0) < dm.count:
            self.waited[e][dm] = dm.count
            self.prog[e].append(lambda eng, s=dm.sem, v=dm.count: eng.wait_ge(s, v))
        self.op(e, fn, reads, writes, dom=dm)

    def wait_all(self, e, ress):
        for r in ress:
            if r.w is not None:
                dm, v = r.w
                if self.waited[e].get(dm, 0) < v:
                    self.waited[e][dm] = v
                    self.prog[e].append(lambda eng, s=dm.sem, v=v: eng.wait_ge(s, v))

    def barrier(self):
        doms = list(self.dom.values()) + list(self.dma_doms.values())
        for e in ENGS:
            for dm in doms:
                if dm.count > 0 and self.waited[e].get(dm, 0) < dm.count:
                    self.waited[e][dm] = dm.count
                    self.prog[e].append(lambda eng, s=dm.sem, v=dm.count: eng.wait_ge(s, v))

    def flush(self):
        nc = self.nc
        prog = self.prog
        self.prog = {e: [] for e in ENGS}
        with nc.Block() as block:
            @block.tensor
            def _(eng):
                for t in prog["pe"]:
                    t(eng)

            @block.scalar
            def _(eng):
                for t in prog["act"]:
                    t(eng)

            @block.vector
            def _(eng):
                for t in prog["dve"]:
                    t(eng)

            @block.gpsimd
            def _(eng):
                for t in prog["pool"]:
                    t(eng)

            @block.sync
            def _(eng):
                for t in prog["sp"]:
                    t(eng)


class Phase:
    def __init__(self, c):
        self.c = c

    def __enter__(self):
        self.outer = self.c.stack
        self.st = ExitStack()
        self.c.stack = self.st
        return self

    def __exit__(self, *a):
        self.c.barrier()
        self.c.flush()
        self.c.stack = self.outer
        self.st.close()
        return False


def mm(c, ps_ap, lhsT, rhs, start, stop, reads, writes):
    c.op("pe", lambda pe: pe.matmul(ps_ap, lhsT, rhs, start=start, stop=stop), reads, writes)


def tr(c, ps_ap, in_ap, ident_ap, reads, writes):
    c.op("pe", lambda pe: pe.transpose(ps_ap, in_ap, ident_ap), reads, writes)


def v_tt(c, out, in0, in1, op, reads, writes, e="dve"):
    c.op(e, lambda v: v.tensor_tensor(out=out, in0=in0, in1=in1, op=op), reads, writes)


def v_ts(c, out, in0, s1, s2, op0, op1, reads, writes, e="dve"):
    if op1 is None:
        c.op(e, lambda v: v.tensor_scalar(out=out, in0=in0, scalar1=s1, scalar2=None, op0=op0), reads, writes)
    else:
        c.op(e, lambda v: v.tensor_scalar(out=out, in0=in0, scalar1=s1, scalar2=s2, op0=op0, op1=op1), reads, writes)


def v_stt(c, out, in0, scalar, in1, op0, op1, reads, writes, e="dve"):
    c.op(e, lambda v: v.scalar_tensor_tensor(out=out, in0=in0, scalar=scalar, in1=in1, op0=op0, op1=op1),
         reads, writes)


def v_copy(c, out, in_, reads, writes, e="dve"):
    c.op(e, lambda v: v.tensor_copy(out=out, in_=in_), reads, writes)


def v_red(c, out, in_, reads, writes):
    c.op("dve", lambda v: v.reduce_sum(out=out, in_=in_, axis=AX.X), reads, writes)


def v_recip(c, out, in_, reads, writes):
    c.op("dve", lambda v: v.reciprocal(out=out, in_=in_), reads, writes)


def a_act(c, out, in_, func, reads, writes, **kw):
    c.op("act", lambda a: a.activation(out=out, in_=in_, func=func, **kw), reads, writes)


def rstd_inplace(c, ss_ap, n, res):
    a_act(c, ss_ap, ss_ap, AF.Sqrt, [res], [res], scale=1.0 / n, bias=EPS)
    v_recip(c, ss_ap, ss_ap, [res], [res])


def rmsnorm(c, xt_ap, P, n, wb_ap, out_ap, junk, ss, reads, writes, wres):
    c.op("act", lambda a: a.activation(out=junk[0:P, 0:n], in_=xt_ap, func=AF.Square, accum_out=ss[0:P, 0:1]),
         reads, [junk.r, ss.r])
    c.op("act", lambda a: a.activation(out=ss[0:P, 0:1], in_=ss[0:P, 0:1], func=AF.Sqrt, scale=1.0 / n, bias=EPS),
         [ss.r], [ss.r])
    c.op("dve", lambda v: v.reciprocal(out=ss[0:P, 0:1], in_=ss[0:P, 0:1]), [ss.r], [ss.r])
    c.op("dve", lambda v: v.scalar_tensor_tensor(out=out_ap, in0=xt_ap, scalar=ss[0:P, 0:1], in1=wb_ap,
                                                 op0=ALU.mult, op1=ALU.mult),
         list(reads) + [ss.r, wres], writes)


class Prog:
    def __init__(self, T, layers, test=None):
        self.T = T
        self.layers = layers
        nc = self.nc = bass.Bass("TRN2", target_bir_lowering=False)
        self.stack = ExitStack()
        self.c = Ctx(nc, self.stack)
        self.c.init_psum()
        self.ext = {}

    def inp(self, name, shape, dtype=F32):
        t = self.nc.dram_tensor(name, list(shape), dtype, kind="ExternalInput")
        self.ext[name] = t
        return t

    def setup_consts(self):
        c = self.c
        ident_d = self.inp("c_ident", [128, 128])
        idf = c.sb([128, 128], F32)
        self.ident = c.sb([128, 128], BF16)
        c.dma("sp", idf[:], ident_d.ap()[:, :], [], [idf.r], "const")
        c.op("dve", lambda v: v.tensor_copy(out=self.ident[:], in_=idf[:]), [idf.r], [self.ident.r])
        self.junk = c.sb([128, 1024], BF16)
        self.ss = c.sb([128, 4], F32)

    def ffn_phase(self, L, src, src_res, dst, dst_res, wg_d, wu_d, wd_d, nw_d):
        c, T = self.c, self.T
        NT = 256
        wg = c.sb([128, 8, FH], BF16, nres=8)
        wu = c.sb([128, 8, FH], BF16, nres=8)
        wd = c.sb([128, 22, D], BF16, nres=22)
        nw = c.sb([128, D], F32)
        c.dma("sp", nw[:], nw_d[L, :].partition_broadcast(128), [], [nw.r], "ffn_nw")
        for k in range(8):
            c.dma("pool", wg[:, k, :], wg_d[L, k * 128:(k + 1) * 128, :], [], [wg.res[k]], "w")
            c.dma("pool", wu[:, k, :], wu_d[L, k * 128:(k + 1) * 128, :], [], [wu.res[k]], "w")
        for h in range(22):
            c.dma("pool", wd[:, h, :], wd_d[L, h * 128:(h + 1) * 128, :], [], [wd.res[h]], "w")
        xt = [c.sb([128, D], F32) for _ in range(2)]
        xo = [c.sb([128, D], F32) for _ in range(2)]
        hb = c.sb([128, D], BF16)
        hT = c.sb([128, 8, NT], BF16)
        sg = c.sb([128, NT], F32)
        hid = c.sb([128, 22, NT], BF16, nres=22)
        for st in range(T // NT):
            for j in range(2):
                ti = st * 2 + j
                c.dma("sp", xt[j][:], src[ti * 128:(ti + 1) * 128, :], [], [xt[j].r], ("ffn_x", j))
                rmsnorm(c, xt[j][:], 128, D, nw[:], hb[:], self.junk, self.ss, [xt[j].r], [hb.r], nw.r)
                pT = c.psum()
                pTv = pT[:].bitcast(BF16)
                for k in range(8):
                    tr(c, pTv[:, k * 128:(k + 1) * 128], hb[:, k * 128:(k + 1) * 128], self.ident[:],
                       [hb.r, self.ident.r], [pT.r])
                c.op("act", lambda a, j=j, pTv=pTv: a.copy(out=hT[:, :, j * 128:(j + 1) * 128],
                                                         in_=pTv.rearrange("p (k t) -> p k t", k=8)),
                     [pT.r], [hT.r])
            for h in range(22):
                pg = c.psum()
                pu = c.psum()
                for k in range(8):
                    mm(c, pg[:, 0:NT], wg[:, k, h * 128:(h + 1) * 128], hT[:, k, :], k == 0, k == 7,
                       [wg.res[k], hT.r], [pg.r])
                for k in range(8):
                    mm(c, pu[:, 0:NT], wu[:, k, h * 128:(h + 1) * 128], hT[:, k, :], k == 0, k == 7,
                       [wu.res[k], hT.r], [pu.r])
                c.op("act", lambda a, pg=pg: a.activation(out=sg[:], in_=pg[:, 0:NT], func=AF.Silu),
                     [pg.r], [sg.r])
                c.op("dve", lambda v, pu=pu, h=h: v.tensor_tensor(out=hid[:, h, :], in0=sg[:], in1=pu[:, 0:NT],
                                                                  op=ALU.mult),
                     [sg.r, pu.r], [hid.res[h]])
            for j in range(2):
                ti = st * 2 + j
                for n in range(2):
                    po = c.psum()
                    for h in range(22):
                        mm(c, po[:, :], hid[:, h, j * 128:(j + 1) * 128], wd[:, h, n * 512:(n + 1) * 512],
                           h == 0, h == 21, [hid.res[h], wd.res[h]], [po.r])
                    c.op("dve", lambda v, po=po, j=j, n=n: v.tensor_tensor(
                        out=xo[j][:, n * 512:(n + 1) * 512], in0=po[:, :], in1=xt[j][:, n * 512:(n + 1) * 512],
                        op=ALU.add), [po.r, xt[j].r], [xo[j].r])
                c.dma("sp", dst[ti * 128:(ti + 1) * 128, :], xo[j][:], [xo[j].r], [], ("ffn_o", j))


    def declare(self, depth):
        ne, no = (depth + 1) // 2, depth // 2
        T = self.T
        shapes = {
            "x": [T, D], "pos": [T], "rmask": [128, 4], "hsel": [128, 4],
            "c_ident": [128, 128], "c_inv": [16], "c_kpos": [128, 4 * T // 128], "c_triu": [128, 128],
            "mix_norm_even": [ne, D], "w_in_even": [ne, D, 5664], "conv_w": [ne, 4, 1536], "conv_b": [ne, 1536],
            "dt_bias": [ne, 16], "a_log": [ne, 16], "d_skip": [ne, 16], "ssd_norm": [ne, 1024],
            "gla_gk_w2": [ne, 16, 512], "gla_gk_b": [ne, 512], "gla_norm": [ne, 256], "w_out_even": [ne, 2048, D],
            "mix_norm_odd": [max(no, 1), D], "w_dqkv": [max(no, 1), D, 672], "q_lora_norm": [max(no, 1), 384],
            "w_uq": [max(no, 1), 384, 1536], "kv_lora_norm": [max(no, 1), 256], "w_ukv": [max(no, 1), 256, 2048],
            "q_nope_norm": [max(no, 1), 64], "q_rope_norm": [max(no, 1), 32], "k_nope_norm": [max(no, 1), 64],
            "k_rope_norm": [max(no, 1), 32], "w_o_mla": [max(no, 1), 1024, D],
            "ffn_norm": [depth, D], "w_gate": [depth, D, FH], "w_up": [depth, D, FH], "w_down": [depth, FH, D],
        }
        self.d = {k: self.inp(k, v).ap() for k, v in shapes.items()}
        self.y = self.nc.dram_tensor("y", [T, D], F32, kind="ExternalOutput").ap()
        nc = self.nc
        NB = 4 * T
        self.xs = nc.dram_tensor("xs", [T, D], F32).ap()
        self.NCH = max(1, (288 * T * 2 + 786431) // 786432)
        while T % (self.NCH * 512) != 0:
            self.NCH += 1
        self.TC = T // self.NCH
        self.lat_l = [nc.dram_tensor("lat_l%d" % q, [288, self.TC], BF16) for q in range(self.NCH)]
        self.lat_g = [nc.dram_tensor("lat_g%d" % q, [4 * 288, self.TC], BF16) for q in range(self.NCH)]
        self.qT_d = nc.dram_tensor("qT_d", [16, 96, T], BF16).ap()
        self.aT_d = nc.dram_tensor("aT_d", [16, 64, T], BF16).ap()
        self.KT_d = nc.dram_tensor("KT_d", [16, 96, NB], BF16).ap()
        self.V_d = nc.dram_tensor("V_d", [NB // 128, 128, 16, 65], BF16).ap()
        self.st_l = nc.dram_tensor("st_l", [128, 1040], F32)
        self.st_g = nc.dram_tensor("st_g", [4 * 128, 1040], F32)
        self.sg_l = nc.dram_tensor("sg_l", [128, 1040], F32)
        self.sg_g = nc.dram_tensor("sg_g", [4 * 128, 1040], F32)
        self.hl_l = nc.dram_tensor("hl_l", [16, D], F32)
        self.hl_g = nc.dram_tensor("hl_g", [4 * 16, D], F32)
        self.fn_l = nc.dram_tensor("fn_l", [16, 64], F32)
        self.fn_g = nc.dram_tensor("fn_g", [64, 64], F32)
        self.dly_a = nc.dram_tensor("dly_a", [128, 2048], F32).ap()
        self.dly_b = nc.dram_tensor("dly_b", [128, 2048], F32).ap()
        self.cc_dom = Dom(self.stack.enter_context(nc.semaphore("cc")), 1)
        self.groups = [[0, 1, 2, 3], [4, 5, 6, 7]]
        self.dres = Res()

    def bload(self, src_row_ap, n, key="bl"):
        b = self.c.sb([128, n], F32)
        self.c.dma("sp", b[:], src_row_ap.partition_broadcast(128), [], [b.r], key)
        return b

    def allgather(self, src_t, dst_t):
        c = self.c
        c.barrier()
        c.op("pool", lambda g: g.collective_compute("AllGather", ALU.bypass, replica_groups=self.groups,
                                                      ins=[src_t.ap().opt()], outs=[dst_t.ap().opt()]),
             [], [], dom=self.cc_dom)
        c.barrier()
        c.op("pool", lambda g: g.collective_compute("AllGather", ALU.bypass, replica_groups=self.groups,
                                                      ins=[self.fn_l.ap().opt()], outs=[self.fn_g.ap().opt()]),
             [], [], dom=self.cc_dom)
        c.barrier()
        for i in range(16):
            a, b = (self.dly_a, self.dly_b) if i % 2 == 0 else (self.dly_b, self.dly_a)
            c.dma("sp", b[:, :], a[:, :], [], [], "dly")
        c.barrier()

    def norm_tile_T(self, xt, nw, hb, hT, col0, P=128):
        c = self.c
        rmsnorm(c, xt[0:P, :], P, D, nw[0:P, :], hb[0:P, :], self.junk, self.ss, [xt.r], [hb.r], nw.r)
        pT = c.psum()
        pTv = pT[:].bitcast(BF16)
        for k in range(8):
            tr(c, pTv[:, k * P:(k + 1) * P], hb[0:P, k * 128:(k + 1) * 128], self.ident[0:P, 0:P],
               [hb.r, self.ident.r], [pT.r])
        c.op("act", lambda a: a.copy(out=hT[:, :, col0:col0 + P],
                                     in_=pTv[:, 0:8 * P].rearrange("p (k t) -> p k t", k=8)), [pT.r], [hT.r])

    def rope(self, x, Hn, cos_ap, sin_ap, out, ta, tb, xres, ores):
        c = self.c
        x1, x2 = x[:, :, 0:16], x[:, :, 16:32]
        cb = cos_ap.unsqueeze(1).to_broadcast([128, Hn, 16])
        sb_ = sin_ap.unsqueeze(1).to_broadcast([128, Hn, 16])
        a, b = ta[:, 0:Hn, :], tb[:, 0:Hn, :]
        v_tt(c, a, x1, cb, ALU.mult, [xres, self.cs.r, self.sn.r], [ta.r])
        v_tt(c, b, x2, sb_, ALU.mult, [xres, self.cs.r, self.sn.r], [tb.r])
        v_tt(c, out[:, :, 0:16], a, b, ALU.subtract, [ta.r, tb.r], [ores])
        v_tt(c, a, x2, cb, ALU.mult, [xres, self.cs.r, self.sn.r], [ta.r])
        v_tt(c, b, x1, sb_, ALU.mult, [xres, self.cs.r, self.sn.r], [tb.r])
        v_tt(c, out[:, :, 16:32], a, b, ALU.add, [ta.r, tb.r], [ores])

    def mla_m1(self, j, xs):
        c, T, d = self.c, self.T, self.d
        NTL = T // 128
        PI = float(np.pi)
        wdq = c.sb([128, 8, 672], BF16, nres=8)
        for k in range(8):
            c.dma("pool", wdq[:, k, :], d["w_dqkv"][j, k * 128:(k + 1) * 128, :], [], [wdq.res[k]], "w")
        wuq = c.sb([128, 3, 1536], BF16, nres=3)
        for k in range(3):
            c.dma("pool", wuq[:, k, :], d["w_uq"][j, k * 128:(k + 1) * 128, :], [], [wuq.res[k]], "w")
        nw = self.bload(d["mix_norm_odd"][j, :], D)
        qln = self.bload(d["q_lora_norm"][j, :], 384)
        kvln = self.bload(d["kv_lora_norm"][j, :], 256)
        gq = c.sb([128, 96], F32)
        c.dma("sp", gq[:, 0:64], d["q_nope_norm"][j, :].partition_broadcast(128), [], [gq.r], "bl")
        c.dma("sp", gq[:, 64:96], d["q_rope_norm"][j, :].partition_broadcast(128), [], [gq.r], "bl")
        v_ts(c, gq[:], gq[:], 96.0 ** -0.5, None, ALU.mult, None, [gq.r], [gq.r])
        gkr = self.bload(d["k_rope_norm"][j, :], 32)
        inv = self.bload(d["c_inv"], 16)
        posc = self.tload(d["pos"].rearrange("(n p) -> n p", p=128), NTL)
        ang = c.sb([128, NTL, 16], F32)
        self.cs = c.sb([128, NTL, 16], F32)
        self.sn = c.sb([128, NTL, 16], F32)
        v_tt(c, ang[:], inv[:].unsqueeze(1).to_broadcast([128, NTL, 16]),
             posc[:].unsqueeze(2).to_broadcast([128, NTL, 16]), ALU.mult, [inv.r, posc.r], [ang.r])
        kf = c.sb([128, NTL, 16], F32)
        msk = c.sb([128, NTL, 16], F32)
        MAGIC = 12582912.0
        C1 = 6.28125
        C2 = 2.0 * PI - C1
        PIC = 3.141592
        v_ts(c, kf[:], ang[:], 1.0 / (2.0 * PI), None, ALU.mult, None, [ang.r], [kf.r])
        v_ts(c, kf[:], kf[:], MAGIC, None, ALU.add, None, [kf.r], [kf.r])
        v_ts(c, kf[:], kf[:], -MAGIC, None, ALU.add, None, [kf.r], [kf.r])
        v_stt(c, ang[:], kf[:], -C1, ang[:], ALU.mult, ALU.add, [kf.r, ang.r], [ang.r])
        v_stt(c, ang[:], kf[:], -C2, ang[:], ALU.mult, ALU.add, [kf.r, ang.r], [ang.r])
        v_ts(c, self.sn[:], ang[:], PIC, -PIC, ALU.min, ALU.max, [ang.r], [self.sn.r])
        a_act(c, self.sn[:], self.sn[:], AF.Sin, [self.sn.r], [self.sn.r])
        v_ts(c, self.cs[:], ang[:], PI / 2, None, ALU.add, None, [ang.r], [self.cs.r])
        v_ts(c, msk[:], self.cs[:], PI, None, ALU.is_gt, None, [self.cs.r], [msk.r])
        v_stt(c, self.cs[:], msk[:], -2.0 * PI, self.cs[:], ALU.mult, ALU.add, [msk.r, self.cs.r], [self.cs.r])
        v_ts(c, self.cs[:], self.cs[:], PIC, -PIC, ALU.min, ALU.max, [self.cs.r], [self.cs.r])
        a_act(c, self.cs[:], self.cs[:], AF.Sin, [self.cs.r], [self.cs.r])
        xt = c.sb([128, D], F32)
        hb = c.sb([128, D], BF16)
        hT = c.sb([128, 8, 128], BF16)
        lat = c.sb([128, 672], F32)
        cqn = c.sb([128, 384], BF16)
        cqT = c.sb([128, 3, 128], BF16)
        ckvn = c.sb([128, 256], BF16)
        latT = c.sb([128, 2, 128], BF16)
        kr = c.sb([128, 1, 32], F32)
        krr = c.sb([128, 1, 32], BF16)
        krT = c.sb([32, 128], BF16)
        q = c.sb([128, 16, 96], F32)
        sq = c.sb([128, 16, 96], F32)
        ssn = c.sb([128, 16], F32)
        ssr = c.sb([128, 16], F32)
        qr = c.sb([128, 16, 32], F32)
        qf = c.sb([128, 16, 96], BF16)
        ta = c.sb([128, 16, 16], F32)
        tb = c.sb([128, 16, 16], F32)
        qT = c.sb([96, 16, 128], BF16)
        for ti in range(NTL):
            tsl = slice(ti * 128, (ti + 1) * 128)
            lq = (ti * 128) // self.TC
            lat_l = self.lat_l[lq].ap()
            lsl = slice(ti * 128 - lq * self.TC, ti * 128 - lq * self.TC + 128)
            c.dma("sp", xt[:], xs[tsl, :], [], [xt.r], "m1x")
            self.norm_tile_T(xt, nw, hb, hT, 0)
            pl0, pl1 = c.psum(), c.psum()
            for k in range(8):
                mm(c, pl0[:, :], hT[:, k, :], wdq[:, k, 0:512], k == 0, k == 7, [hT.r, wdq.res[k]], [pl0.r])
            for k in range(8):
                mm(c, pl1[:, 0:160], hT[:, k, :], wdq[:, k, 512:672], k == 0, k == 7, [hT.r, wdq.res[k]], [pl1.r])
            c.op("act", lambda a, p=pl0: a.copy(out=lat[:, 0:512], in_=p[:, :]), [pl0.r], [lat.r])
            c.op("act", lambda a, p=pl1: a.copy(out=lat[:, 512:672], in_=p[:, 0:160]), [pl1.r], [lat.r])
            rmsnorm(c, lat[:, 0:384], 128, 384, qln[:], cqn[:], self.junk, self.ss, [lat.r], [cqn.r], qln.r)
            pT = c.psum()
            pTv = pT[:].bitcast(BF16)
            for k in range(3):
                tr(c, pTv[:, k * 128:(k + 1) * 128], cqn[:, k * 128:(k + 1) * 128], self.ident[:],
                   [cqn.r, self.ident.r], [pT.r])
            c.op("act", lambda a, pTv=pTv: a.copy(out=cqT[:], in_=pTv[:, 0:384].rearrange("p (k t) -> p k t", k=3)),
                 [pT.r], [cqT.r])
            rmsnorm(c, lat[:, 384:640], 128, 256, kvln[:], ckvn[:], self.junk, self.ss, [lat.r], [ckvn.r], kvln.r)
            pT2 = c.psum()
            pT2v = pT2[:].bitcast(BF16)
            for k in range(2):
                tr(c, pT2v[:, k * 128:(k + 1) * 128], ckvn[:, k * 128:(k + 1) * 128], self.ident[:],
                   [ckvn.r, self.ident.r], [pT2.r])
            c.op("act", lambda a, p=pT2v: a.copy(out=latT[:], in_=p[:, 0:256].rearrange("p (k t) -> p k t", k=2)),
                 [pT2.r], [latT.r])
            c.dma("sp", lat_l[0:256, lsl].rearrange("(k p) t -> p k t", p=128), latT[:], [latT.r], [], "m1l")
            rmsnorm(c, lat[:, 640:672], 128, 32, gkr[:], kr[:, 0, :], self.junk, self.ss, [lat.r], [kr.r], gkr.r)
            self.rope(kr, 1, self.cs[:, ti, :], self.sn[:, ti, :], krr, ta, tb, kr.r, krr.r)
            pT3 = c.psum()
            pT3v = pT3[:].bitcast(BF16)
            tr(c, pT3v[0:32, 0:128], krr[:, 0, :], self.ident[:], [krr.r, self.ident.r], [pT3.r])
            c.op("act", lambda a, p=pT3v: a.copy(out=krT[:], in_=p[0:32, 0:128]), [pT3.r], [krT.r])
            c.dma("sp", lat_l[256:288, lsl], krT[:], [krT.r], [], "m1k")
            qfl = q[:].rearrange("p h d -> p (h d)")
            for n in range(3):
                pq = c.psum()
                for k in range(3):
                    mm(c, pq[:, :], cqT[:, k, :], wuq[:, k, n * 512:(n + 1) * 512], k == 0, k == 2,
                       [cqT.r, wuq.res[k]], [pq.r])
                c.op("act", lambda a, p=pq, n=n: a.copy(out=qfl[:, n * 512:(n + 1) * 512], in_=p[:, :]),
                     [pq.r], [q.r])
            a_act(c, sq[:], q[:], AF.Square, [q.r], [sq.r])
            v_red(c, ssn[:], sq[:, :, 0:64], [sq.r], [ssn.r])
            v_red(c, ssr[:], sq[:, :, 64:96], [sq.r], [ssr.r])
            rstd_inplace(c, ssn[:], 64, ssn.r)
            rstd_inplace(c, ssr[:], 32, ssr.r)
            v_tt(c, sq[:, :, 0:64], q[:, :, 0:64], ssn[:].unsqueeze(2).to_broadcast([128, 16, 64]), ALU.mult,
                 [q.r, ssn.r], [sq.r])
            v_tt(c, qf[:, :, 0:64], sq[:, :, 0:64], gq[:, 0:64].unsqueeze(1).to_broadcast([128, 16, 64]), ALU.mult,
                 [sq.r, gq.r], [qf.r])
            v_tt(c, sq[:, :, 64:96], q[:, :, 64:96], ssr[:].unsqueeze(2).to_broadcast([128, 16, 32]), ALU.mult,
                 [q.r, ssr.r], [sq.r])
            v_tt(c, qr[:], sq[:, :, 64:96], gq[:, 64:96].unsqueeze(1).to_broadcast([128, 16, 32]), ALU.mult,
                 [sq.r, gq.r], [qr.r])
            self.rope(qr, 16, self.cs[:, ti, :], self.sn[:, ti, :], qf[:, :, 64:96], ta, tb, qr.r, qf.r)
            for half in range(2):
                pt = c.psum()
                ptv = pt[:].bitcast(BF16)
                for hh in range(8):
                    tr(c, ptv[0:96, hh * 128:(hh + 1) * 128], qf[:, half * 8 + hh, :], self.ident[:],
                       [qf.r, self.ident.r], [pt.r])
                c.op("act", lambda a, p=ptv, half=half: a.copy(
                    out=qT[:, half * 8:half * 8 + 8, :], in_=p[0:96, :].rearrange("p (h t) -> p h t", h=8)),
                    [pt.r], [qT.r])
            c.dma("sp", self.qT_d[:, :, tsl].rearrange("h p t -> p h t"), qT[:], [qT.r], [], "m1q")

    def mla_m2(self, j):
        c, T, d = self.c, self.T, self.d
        NB = 4 * T
        wukv = c.sb([128, 2, 2048], BF16, nres=2)
        for k in range(2):
            c.dma("pool", wukv[:, k, :], d["w_ukv"][j, k * 128:(k + 1) * 128, :], [], [wukv.res[k]], "w")
        gk = self.bload(d["k_nope_norm"][j, :], 64)
        ckvT = [c.sb([128, 2, 512], BF16) for _ in range(2)]
        KTs = [c.sb([96, 16, 512], BF16) for _ in range(2)]
        vaug = [c.sb([128, 4, 16, 65], BF16) for _ in range(2)]
        for b in range(2):
            c.op("dve", lambda v, b=b: v.memset(vaug[b][:, :, :, 64:65], 1.0), [], [vaug[b].r])
        sq = c.sb([128, 4, 64], F32)
        ssk = c.sb([128, 16], F32)
        tmp = c.sb([128, 4, 64], F32)
        kn = c.sb([128, 16, 64], BF16)
        for st in range(NB // 512):
            b = st % 2
            r = (st * 512) // T
            t0 = st * 512 - r * T
            lq = t0 // self.TC
            lat_g = self.lat_g[lq].ap()
            t0 = t0 - lq * self.TC
            base = r * 288
            c.dma("sp", ckvT[b][:], lat_g[base:base + 256, t0:t0 + 512].rearrange("(k p) t -> p k t", p=128),
                  [], [ckvT[b].r], ("m2c", b))
            for h in range(16):
                c.dma("sp", KTs[b][64:96, h, :], lat_g[base + 256:base + 288, t0:t0 + 512], [], [KTs[b].r],
                      ("m2r", b))
            for blk in range(4):
                bs = slice(blk * 128, (blk + 1) * 128)
                pk = [c.psum() for _ in range(4)]
                for n in range(4):
                    for k in range(2):
                        mm(c, pk[n][:, :], ckvT[b][:, k, bs], wukv[:, k, n * 512:(n + 1) * 512], k == 0, k == 1,
                           [ckvT[b].r, wukv.res[k]], [pk[n].r])
                for n in range(4):
                    pv = pk[n][:, :].rearrange("p (h d) -> p h d", h=4)
                    a_act(c, sq[:], pv[:, :, 0:64], AF.Square, [pk[n].r], [sq.r])
                    v_red(c, ssk[:, 4 * n:4 * n + 4], sq[:], [sq.r], [ssk.r])
                    c.op("act", lambda a, pv=pv, n=n, blk=blk, b=b: a.copy(
                        out=vaug[b][:, blk, 4 * n:4 * n + 4, 0:64], in_=pv[:, :, 64:128]), [pk[n].r], [vaug[b].r])
                rstd_inplace(c, ssk[:], 64, ssk.r)
                for n in range(4):
                    pv = pk[n][:, :].rearrange("p (h d) -> p h d", h=4)
                    v_tt(c, tmp[:], pv[:, :, 0:64], ssk[:, 4 * n:4 * n + 4].unsqueeze(2).to_broadcast([128, 4, 64]),
                         ALU.mult, [pk[n].r, ssk.r], [tmp.r])
                    v_tt(c, kn[:, 4 * n:4 * n + 4, :], tmp[:], gk[:].unsqueeze(1).to_broadcast([128, 4, 64]),
                         ALU.mult, [tmp.r, gk.r], [kn.r])
                for half in range(2):
                    pt = c.psum()
                    ptv = pt[:].bitcast(BF16)
                    for hh in range(8):
                        tr(c, ptv[0:64, hh * 128:(hh + 1) * 128], kn[:, half * 8 + hh, :], self.ident[:],
                           [kn.r, self.ident.r], [pt.r])
                    c.op("act", lambda a, p=ptv, half=half, b=b, bs=bs: a.copy(
                        out=KTs[b][0:64, half * 8:half * 8 + 8, bs],
                        in_=p[0:64, :].rearrange("p (h t) -> p h t", h=8)), [pt.r], [KTs[b].r])
            c.dma("sp", self.KT_d[:, :, st * 512:(st + 1) * 512].rearrange("h p t -> p h t"), KTs[b][:],
                  [KTs[b].r], [], ("m2k", b))
            c.dma("sp", self.V_d[st * 4:(st + 1) * 4].rearrange("n p h c -> p n h c"), vaug[b][:],
                  [vaug[b].r], [], ("m2v", b))

    def mla_m3(self, j):
        c, T, d = self.c, self.T, self.d
        NB = 4 * T
        NBLK = NB // 128
        NG = T // 512
        KT = [c.sb([96, NB], BF16) for _ in range(2)]
        Vh = [c.sb([128, NBLK, 65], BF16) for _ in range(2)]
        qrow = self.bload(d["pos"], T)
        kcols = c.sb([128, NBLK], F32)
        c.dma("sp", kcols[:], d["c_kpos"][:, :], [], [kcols.r], "bl")
        ones = c.sb([128, 64], F32)
        c.op("dve", lambda v: v.memset(ones[:], 1.0), [], [ones.r])
        qT = [c.sb([96, 512], BF16) for _ in range(2)]
        pTs = [c.sb([128, 512], BF16) for _ in range(6)]
        rs = c.sb([128, 512], F32)
        rb = c.sb([64, 512], F32)
        oT = [c.sb([64, 512], BF16) for _ in range(2)]
        it = 0
        ig = 0
        for h in range(16):
            b = h % 2
            c.dma("sp", KT[b][:], self.KT_d[h], [], [KT[b].r], ("m3k", b))
            c.dma("sp", Vh[b][:], self.V_d[:, :, h, :].rearrange("n p c -> p n c"), [], [Vh[b].r], ("m3v", b))
            for g in range(NG):
                qb = qT[ig % 2]
                ob = oT[ig % 2]
                gs = slice(g * 512, (g + 1) * 512)
                c.dma("sp", qb[:], self.qT_d[h, :, gs], [], [qb.r], ("m3q", ig % 2))
                kmax = (3 * T + (g + 1) * 512) // 128
                po = c.psum_acc(ig % 2)
                LA = 3
                pT_of = {}
                for step in range(kmax + LA):
                    if step < kmax:
                        kb = step
                        ps = c.psum()
                        mm(c, ps[:, :], KT[b][:, kb * 128:(kb + 1) * 128], qb[:], True, True, [KT[b].r, qb.r], [ps.r])
                        pT = pTs[it % len(pTs)]
                        it += 1
                        pT_of[kb] = pT
                        a_act(c, pT[:], ps[:, :], AF.Exp, [ps.r], [pT.r])
                        if kb * 128 + 127 > g * 512:
                            v_stt(c, pT[:], qrow[:, gs], kcols[:, kb:kb + 1], pT[:], ALU.is_ge, ALU.mult,
                                  [pT.r, qrow.r, kcols.r], [pT.r])
                    if step >= LA:
                        kb = step - LA
                        pT = pT_of.pop(kb)
                        mm(c, po[0:65, :], Vh[b][:, kb, :], pT[:], kb == 0, kb == kmax - 1, [Vh[b].r, pT.r], [po.r])
                c.op("act", lambda a, po=po: a.copy(out=rs[64:65, :], in_=po[64:65, :]), [po.r], [rs.r])
                v_recip(c, rs[64:65, :], rs[64:65, :], [rs.r], [rs.r])
                pb = c.psum()
                mm(c, pb[0:64, :], ones[64:65, 0:64], rs[64:65, :], True, True, [ones.r, rs.r], [pb.r])
                c.op("act", lambda a, pb=pb: a.copy(out=rb[:], in_=pb[0:64, :]), [pb.r], [rb.r])
                v_tt(c, ob[:], po[0:64, :], rb[:], ALU.mult, [po.r, rb.r], [ob.r])
                c.dma("sp", self.aT_d[h, :, gs], ob[:], [ob.r], [], ("m3o", ig % 2))
                ig += 1

    def mla_m4(self, j, xs, dst):
        c, T, d = self.c, self.T, self.d
        wo = c.sb([64, 16, D], BF16, nres=16)
        for h in range(16):
            c.dma("pool", wo[:, h, :], d["w_o_mla"][j, h * 64:(h + 1) * 64, :], [], [wo.res[h]], "w")
        aT = [c.sb([64, 16, 128], BF16) for _ in range(2)]
        xt = [c.sb([128, D], F32) for _ in range(2)]
        xo = [c.sb([128, D], F32) for _ in range(2)]
        for ti in range(T // 128):
            b = ti % 2
            tsl = slice(ti * 128, (ti + 1) * 128)
            c.dma("sp", aT[b][:], self.aT_d[:, :, tsl].rearrange("h p t -> p h t"), [], [aT[b].r], ("m4a", b))
            c.dma("sp", xt[b][:], xs[tsl, :], [], [xt[b].r], ("m4x", b))
            for n in range(2):
                po = c.psum()
                for h in range(16):
                    mm(c, po[:, :], aT[b][:, h, :], wo[:, h, n * 512:(n + 1) * 512], h == 0, h == 15,
                       [aT[b].r, wo.res[h]], [po.r])
                v_tt(c, xo[b][:, n * 512:(n + 1) * 512], po[:, :], xt[b][:, n * 512:(n + 1) * 512], ALU.add,
                     [po.r, xt[b].r], [xo[b].r])
            c.dma("sp", dst[tsl, :], xo[b][:], [xo[b].r], [], ("m4o", b))

    def mla_layer(self, j, xs):
        c = self.c
        with Phase(c):
            self.mla_m1(j, xs)
        with Phase(c):
            for q in range(self.NCH):
                self.allgather(self.lat_l[q], self.lat_g[q])
        import os
        with Phase(c):
            if os.environ.get("KDBG") == "2":
                dl0 = self.nc.dram_tensor("dbg_l0", [4 * 288, self.TC], BF16, kind="ExternalOutput").ap()
                c.dma("sp", dl0[:, :], self.lat_g[0].ap()[:, :], [], [], "dbg")
            self.mla_m2(j)
        with Phase(c):
            self.mla_m3(j)
        import os
        if os.environ.get("KDBG") == "2":
            with Phase(c):
                dbg = self.nc.dram_tensor("dbg_a", [16, 64, self.T], BF16, kind="ExternalOutput").ap()
                c.dma("sp", dbg[:, :, :], self.aT_d[:, :, :], [], [], "dbg")
                dbq = self.nc.dram_tensor("dbg_q", [16, 96, self.T], BF16, kind="ExternalOutput").ap()
                c.dma("sp", dbq[:, :, :], self.qT_d[:, :, :], [], [], "dbg")
                dbk = self.nc.dram_tensor("dbg_k", [16, 96, 4 * self.T], BF16, kind="ExternalOutput").ap()
                c.dma("sp", dbk[:, :, :], self.KT_d[:, :, :], [], [], "dbg")
                dbv = self.nc.dram_tensor("dbg_v", [4 * self.T // 128, 128, 16, 65], BF16, kind="ExternalOutput").ap()
                c.dma("sp", dbv[:, :, :, :], self.V_d[:, :, :, :], [], [], "dbg")
        with Phase(c):
            self.mla_m4(j, xs, xs)


    def ncload(self, shape, src_ap, key="bl"):
        b = self.c.sb(shape, F32)
        self.c.dma_op("sp", lambda e: e.dma_start(out=b[:], in_=src_ap, allow_slow_non_contiguous=True), [], [b.r], key)
        return b

    def tload(self, src_rows_ap, n):
        c = self.c
        rows = c.sb([128, 128], F32)
        c.dma("sp", rows[0:n, :], src_rows_ap, [], [rows.r], "bl")
        pt = c.psum()
        tr(c, pt[:, 0:n], rows[0:n, :], self.identf[0:n, 0:n], [rows.r, self.identf.r], [pt.r])
        out = c.sb([128, n], F32)
        c.op("act", lambda a: a.copy(out=out[:], in_=pt[:, 0:n]), [pt.r], [out.r])
        return out

    def wload(self, j, c0, c1, nk=8):
        c = self.c
        w = c.sb([128, nk, c1 - c0], BF16, nres=nk)
        for k in range(nk):
            c.dma("pool", w[:, k, :], self.d["w_in_even"][j, k * 128:(k + 1) * 128, c0:c1], [], [w.res[k]], "w")
        return w

    def halo_phase(self, xs):
        c, T = self.c, self.T
        with Phase(c):
            c.dma("sp", self.hl_l.ap()[0:16, :], xs[T - 16:T, :], [], [], "hl")
        with Phase(c):
            self.allgather(self.hl_l, self.hl_g)

    def halo_x(self):
        c, d = self.c, self.d
        g4 = c.sb([3, 4, D], F32)
        c.dma("sp", g4[:], self.hl_g.ap().rearrange("(r p) n -> p r n", p=16)[13:16], [], [g4.r], "hl")
        hs = c.sb([128, 4], F32)
        c.dma("sp", hs[:], d["hsel"][:, :], [], [hs.r], "bl")
        hx = c.sb([128, D], F32)
        c.op("dve", lambda v: v.memset(hx[0:4, :], 0.0), [], [hx.r])
        v_ts(c, hx[0:3, :], g4[:, 0, :], hs[0:3, 0:1], None, ALU.mult, None, [g4.r, hs.r], [hx.r])
        for r in range(1, 4):
            v_stt(c, hx[0:3, :], g4[:, r, :], hs[0:3, r:r + 1], hx[0:3, :], ALU.mult, ALU.add, [g4.r, hs.r, hx.r],
                  [hx.r])
        return hx

    def ssd_pass(self, j, xs, final):
        c, T, d = self.c, self.T, self.d
        NT = 256
        wx = self.wload(j, 1024, 2560)
        wdt = self.wload(j, 2560, 2576)
        wz = self.wload(j, 0, 1024) if final else None
        nw = self.bload(d["mix_norm_even"][j, :], D)
        cw = self.tload(d["conv_w"][j].rearrange("t (c p) -> (t c) p", p=128), 48)
        cb = self.tload(d["conv_b"][j].rearrange("(c p) -> c p", p=128), 12)
        dtb = self.bload(d["dt_bias"][j, :], 16)
        negA = self.bload(d["a_log"][j, :], 16)
        a_act(c, negA[:], negA[:], AF.Exp, [negA.r], [negA.r])
        v_ts(c, negA[:], negA[:], -1.0, None, ALU.mult, None, [negA.r], [negA.r])
        dsk = self.bload(d["d_skip"][j, :], 16)
        snw = self.bload(d["ssd_norm"][j, :], 1024)
        triu = c.sb([128, 128], F32)
        c.dma("sp", triu[:], d["c_triu"][:, :], [], [triu.r], "bl")
        ones = c.sb([128, 128], F32)
        c.op("dve", lambda v: v.memset(ones[:], 1.0), [], [ones.r])
        Hs = c.sb([128, 16, 64], F32)
        Hsb = c.sb([128, 16, 64], BF16)
        Dtot = c.sb([128, 16], F32)
        c.op("dve", lambda v: v.memset(Hs[:], 0.0), [], [Hs.r])
        c.op("dve", lambda v: v.memset(Dtot[:], 1.0), [], [Dtot.r])
        if final:
            rm = c.sb([128, 4], F32)
            c.dma("sp", rm[:], d["rmask"][:, :], [], [rm.r], "bl")
            G = c.sb([128, 1040], F32)
            Dp = c.sb([128, 16], F32)
            for r in range(4):
                c.dma("sp", G[:, 0:1024], self.st_g.ap()[r * 128:(r + 1) * 128, 0:1024], [], [G.r], "stg")
                c.dma("sp", G[:, 1024:1040], self.st_g.ap()[r * 128:(r + 1) * 128, 1024:1040], [], [G.r], "stg")
                v_ts(c, Dp[:], G[:, 1024:1040], -1.0, rm[:, r:r + 1], ALU.add, ALU.mult, [G.r, rm.r], [Dp.r])
                v_ts(c, Dp[:], Dp[:], 1.0, None, ALU.add, None, [Dp.r], [Dp.r])
                v_tt(c, Hs[:], Hs[:], Dp[:].unsqueeze(2).to_broadcast([128, 16, 64]), ALU.mult, [Hs.r, Dp.r], [Hs.r])
                v_stt(c, Hs[:], G[:, 0:1024].rearrange("p (h d) -> p h d", h=16), rm[:, r:r + 1], Hs[:],
                      ALU.mult, ALU.add, [G.r, rm.r, Hs.r], [Hs.r])
        v_copy(c, Hsb[:], Hs[:], [Hs.r], [Hsb.r])
        xt = c.sb([128, D], F32)
        hb = c.sb([128, D], BF16)
        hT = c.sb([128, 8, NT], BF16)
        xbc = c.sb([128, 12, NT + 3], F32, nres=12)
        acc = [c.sb([128, NT], F32) for _ in range(2)]
        xc = c.sb([128, 12, NT], BF16, nres=12)
        hx = self.halo_x()
        hTh = c.sb([128, 8, 4], BF16)
        self.norm_tile_T(hx, nw, hb, hTh, 0, P=4)
        for cc in range(12):
            pc = c.psum()
            for k in range(8):
                mm(c, pc[:, 0:4], wx[:, k, cc * 128:(cc + 1) * 128], hTh[:, k, 0:4], k == 0, k == 7,
                   [wx.res[k], hTh.r], [pc.r])
            c.op("act", lambda a, pc=pc, cc=cc: a.copy(out=xbc[:, cc, 0:3], in_=pc[:, 0:3]), [pc.r], [xbc.res[cc]])
        sz = c.sb([128, 1024], F32) if final else None
        dt = c.sb([128, 16], F32)
        av = c.sb([128, 16], F32)
        acol = c.sb([128, 16], F32)
        arhs = c.sb([128, 16, 128], F32)
        arow = c.sb([128, 16, 128], F32)
        cdrow = c.sb([128, 16], F32)
        te = c.sb([128, 16], F32)
        Btok = c.sb([128, 2, 128], BF16)
        xdt = c.sb([128, 16, 64], F32)
        xdtw = c.sb([128, 16, 64], BF16)
        if final:
            xdtb = c.sb([128, 16, 64], BF16)
            xsk = c.sb([128, 16, 64], F32)
            CBm = c.sb([128, 2, 128], F32)
            erow = c.sb([128, 16, 128], BF16)
            CeT = c.sb([128, 16, 128], BF16)
            MT = c.sb([128, 16, 128], BF16)
            ysb = c.sb([128, 1024], F32)
            ssg = c.sb([128, 4], F32)
            yo = [c.sb([128, 1024], BF16) for _ in range(2)]
        arf = arhs[:].rearrange("p h l -> p (h l)")
        awf = arow[:].rearrange("p h l -> p (h l)")
        for st in range(T // NT):
            for jt in range(2):
                ti = st * 2 + jt
                c.dma("sp", xt[:], xs[ti * 128:(ti + 1) * 128, :], [], [xt.r], "ex")
                self.norm_tile_T(xt, nw, hb, hT, jt * 128)
            for cc in range(12):
                pc = c.psum()
                for k in range(8):
                    mm(c, pc[:, 0:NT], wx[:, k, cc * 128:(cc + 1) * 128], hT[:, k, :], k == 0, k == 7,
                       [wx.res[k], hT.r], [pc.r])
                c.op("act", lambda a, pc=pc, cc=cc: a.copy(out=xbc[:, cc, 3:NT + 3], in_=pc[:, 0:NT]),
                     [pc.r], [xbc.res[cc]])
                e = "dve"
                ac = acc[cc % 2]
                v_ts(c, ac[:], xbc[:, cc, 0:NT], cw[:, cc:cc + 1], None, ALU.mult, None,
                     [xbc.res[cc], cw.r], [ac.r], e=e)
                for tap in range(1, 4):
                    v_stt(c, ac[:], xbc[:, cc, tap:tap + NT], cw[:, tap * 12 + cc:tap * 12 + cc + 1], ac[:], ALU.mult, ALU.add,
                          [xbc.res[cc], cw.r, ac.r], [ac.r], e=e)
                a_act(c, xc[:, cc, :], ac[:], AF.Silu, [ac.r, cb.r], [xc.res[cc]], bias=cb[:, cc:cc + 1])
            v_copy(c, xbc[:, :, 0:3], xbc[:, :, NT:NT + 3], xbc.res, xbc.res)
            for jt in range(2):
                ti = st * 2 + jt
                ts = slice(jt * 128, (jt + 1) * 128)
                pd = c.psum()
                for k in range(8):
                    mm(c, pd[:, 0:16], hT[:, k, ts], wdt[:, k, :], k == 0, k == 7, [hT.r, wdt.res[k]], [pd.r])
                if final:
                    for n in range(2):
                        pz = c.psum()
                        for k in range(8):
                            mm(c, pz[:, :], hT[:, k, ts], wz[:, k, n * 512:(n + 1) * 512], k == 0, k == 7,
                               [hT.r, wz.res[k]], [pz.r])
                        a_act(c, sz[:, n * 512:(n + 1) * 512], pz[:, :], AF.Silu, [pz.r], [sz.r])
                v_tt(c, dt[:], pd[:, 0:16], dtb[:], ALU.add, [pd.r, dtb.r], [dt.r])
                a_act(c, dt[:], dt[:], AF.Exp, [dt.r], [dt.r])
                a_act(c, dt[:], dt[:], AF.Ln, [dt.r], [dt.r], bias=1.0)
                v_tt(c, av[:], dt[:], negA[:], ALU.mult, [dt.r, negA.r], [av.r])
                pa = c.psum()
                mm(c, pa[:, 0:16], triu[:], av[:], True, True, [triu.r, av.r], [pa.r])
                v_copy(c, acol[:], pa[:, 0:16], [pa.r], [acol.r])
                v_tt(c, arhs[:], triu[:].unsqueeze(1).to_broadcast([128, 16, 128]),
                     av[:].unsqueeze(2).to_broadcast([128, 16, 128]), ALU.mult, [triu.r, av.r], [arhs.r])
                for n in range(4):
                    pr = c.psum()
                    mm(c, pr[:, :], ones[:], arf[:, n * 512:(n + 1) * 512], True, True, [ones.r, arhs.r], [pr.r])
                    c.op("act", lambda a, pr=pr, n=n: a.copy(out=awf[:, n * 512:(n + 1) * 512], in_=pr[:, :]),
                         [pr.r], [arow.r])
                a_act(c, cdrow[:], arow[:, :, 127], AF.Exp, [arow.r], [cdrow.r])
                v_tt(c, te[:], arow[:, :, 127], acol[:], ALU.subtract, [arow.r, acol.r], [te.r])
                a_act(c, te[:], te[:], AF.Exp, [te.r], [te.r])
                pxs = c.psum()
                pxv = pxs[:].bitcast(BF16)
                for cc in range(8):
                    tr(c, pxv[:, cc * 128:(cc + 1) * 128], xc[:, cc, ts], self.ident[:], [xc.res[cc], self.ident.r],
                       [pxs.r])
                pB = c.psum()
                pBv = pB[:].bitcast(BF16)
                for g in range(2):
                    tr(c, pBv[:, g * 128:(g + 1) * 128], xc[:, 8 + g, ts], self.ident[:],
                       [xc.res[8 + g], self.ident.r], [pB.r])
                c.op("act", lambda a, pBv=pBv: a.copy(out=Btok[:], in_=pBv[:, 0:256].rearrange("p (g n) -> p g n", g=2)),
                     [pB.r], [Btok.r])
                xv = pxv[:, 0:1024].rearrange("p (h d) -> p h d", h=16)
                v_tt(c, xdt[:], xv, dt[:].unsqueeze(2).to_broadcast([128, 16, 64]), ALU.mult, [pxs.r, dt.r], [xdt.r])
                v_tt(c, xdtw[:], xdt[:], te[:].unsqueeze(2).to_broadcast([128, 16, 64]), ALU.mult, [xdt.r, te.r],
                     [xdtw.r])
                if final:
                    v_copy(c, xdtb[:], xdt[:], [xdt.r], [xdtb.r], e="pool")
                    v_tt(c, xsk[:], xv, dsk[:].unsqueeze(2).to_broadcast([128, 16, 64]), ALU.mult, [pxs.r, dsk.r],
                         [xsk.r])
                    pcb = c.psum()
                    for g in range(2):
                        mm(c, pcb[:, g * 128:(g + 1) * 128], xc[:, 8 + g, ts], xc[:, 10 + g, ts], True, True,
                           [xc.res[8 + g], xc.res[10 + g]], [pcb.r])
                    v_tt(c, CBm[:], pcb[:, 0:256].rearrange("p (g l) -> p g l", g=2),
                         triu[:].unsqueeze(1).to_broadcast([128, 2, 128]), ALU.mult, [pcb.r, triu.r], [CBm.r])
                    a_act(c, erow[:], arow[:], AF.Exp, [arow.r], [erow.r])
                    for g in range(2):
                        v_tt(c, CeT[:, 8 * g:8 * g + 8, :], erow[:, 8 * g:8 * g + 8, :],
                             xc[:, 10 + g, ts].unsqueeze(1).to_broadcast([128, 8, 128]), ALU.mult,
                             [erow.r, xc.res[10 + g]], [CeT.r], e="pool")
                    v_tt(c, arhs[:], arow[:], acol[:].unsqueeze(2).to_broadcast([128, 16, 128]), ALU.subtract,
                         [arow.r, acol.r], [arhs.r])
                    v_ts(c, arhs[:], arhs[:], 0.0, None, ALU.min, None, [arhs.r], [arhs.r])
                    a_act(c, arhs[:], arhs[:], AF.Exp, [arhs.r], [arhs.r])
                    for g in range(2):
                        v_tt(c, MT[:, 8 * g:8 * g + 8, :], arhs[:, 8 * g:8 * g + 8, :],
                             CBm[:, g, :].unsqueeze(1).to_broadcast([128, 8, 128]), ALU.mult, [arhs.r, CBm.r], [MT.r])
                    py = [c.psum(), c.psum()]
                    for h in range(16):
                        p_ = py[h // 8]
                        col = (h % 8) * 64
                        mm(c, p_[:, col:col + 64], MT[:, h, :], xdtb[:, h, :], True, False, [MT.r, xdtb.r], [p_.r])
                        mm(c, p_[:, col:col + 64], CeT[:, h, :], Hsb[:, h, :], False, True, [CeT.r, Hsb.r], [p_.r])
                    xskf = xsk[:].rearrange("p h d -> p (h d)")
                    for n in range(2):
                        v_tt(c, ysb[:, n * 512:(n + 1) * 512], py[n][:, :], xskf[:, n * 512:(n + 1) * 512], ALU.add,
                             [py[n].r, xsk.r], [ysb.r])
                    v_tt(c, ysb[:], ysb[:], sz[:], ALU.mult, [ysb.r, sz.r], [ysb.r])
                    for g in range(2):
                        a_act(c, self.junk[:, 0:512], ysb[:, g * 512:(g + 1) * 512], AF.Square, [ysb.r],
                              [self.junk.r, ssg.r], accum_out=ssg[:, g:g + 1])
                    rstd_inplace(c, ssg[:, 0:2], 512, ssg.r)
                    yb = yo[ti % 2]
                    for g in range(2):
                        v_stt(c, yb[:, g * 512:(g + 1) * 512], ysb[:, g * 512:(g + 1) * 512], ssg[:, g:g + 1],
                              snw[:, g * 512:(g + 1) * 512], ALU.mult, ALU.mult, [ysb.r, ssg.r, snw.r], [yb.r])
                    c.dma("sp", self.ycat_d[ti * 128:(ti + 1) * 128, 0:1024], yb[:], [yb.r], [], ("yo", ti % 2))
                xwf = xdtw[:].rearrange("p h d -> p (h d)")
                for g in range(2):
                    pst = c.psum()
                    mm(c, pst[:, :], Btok[:, g, :], xwf[:, g * 512:(g + 1) * 512], True, True, [Btok.r, xdtw.r], [pst.r])
                    hg = Hs[:, 8 * g:8 * g + 8, :]
                    v_tt(c, hg, hg, cdrow[:, 8 * g:8 * g + 8].unsqueeze(2).to_broadcast([128, 8, 64]), ALU.mult,
                         [Hs.r, cdrow.r, Hsb.r], [Hs.r])
                    v_tt(c, hg, hg, pst[:, :].rearrange("p (h d) -> p h d", h=8), ALU.add, [Hs.r, pst.r], [Hs.r])
                v_copy(c, Hsb[:], Hs[:], [Hs.r], [Hsb.r])
                if not final:
                    v_tt(c, Dtot[:], Dtot[:], cdrow[:], ALU.mult, [Dtot.r, cdrow.r], [Dtot.r])
        if not final:
            c.dma("sp", self.st_l.ap()[:, 0:1024], Hs[:].rearrange("p h d -> p (h d)"), [Hs.r], [], "stl")
            c.dma("sp", self.st_l.ap()[:, 1024:1040], Dtot[:], [Dtot.r], [], "stl2")

    def gla_pass(self, j, xs, final):
        c, T, d = self.c, self.T, self.d
        NT = 256
        wq = self.wload(j, 2576, 3088)
        wk = self.wload(j, 3088, 3600)
        wv = self.wload(j, 3600, 4624)
        wg = self.wload(j, 4624, 5648) if final else None
        wgl = self.wload(j, 5648, 5664)
        nw = self.bload(d["mix_norm_even"][j, :], D)
        gkb = self.bload(d["gla_gk_b"][j, :], 512)
        gnw = self.bload(d["gla_norm"][j, :], 256)
        w2 = c.sb([16, 512], BF16)
        c.dma("pool", w2[:], d["gla_gk_w2"][j, :, :], [], [w2.r], "w")
        triu = c.sb([128, 128], F32)
        c.dma("sp", triu[:], d["c_triu"][:, :], [], [triu.r], "bl")
        Sg = c.sb([128, 4, 256], F32)
        Sgb = c.sb([128, 4, 256], BF16)
        Dg = c.sb([128, 4], F32)
        c.op("dve", lambda v: v.memset(Sg[:], 0.0), [], [Sg.r])
        c.op("dve", lambda v: v.memset(Dg[:], 1.0), [], [Dg.r])
        if final:
            rm = c.sb([128, 4], F32)
            c.dma("sp", rm[:], d["rmask"][:, :], [], [rm.r], "bl")
            G = c.sb([128, 1028], F32)
            Dp = c.sb([128, 4], F32)
            for r in range(4):
                c.dma("sp", G[:, 0:1024], self.sg_g.ap()[r * 128:(r + 1) * 128, 0:1024], [], [G.r], "stg")
                c.dma("sp", G[:, 1024:1028], self.sg_g.ap()[r * 128:(r + 1) * 128, 1024:1028], [], [G.r], "stg")
                v_ts(c, Dp[:], G[:, 1024:1028], -1.0, rm[:, r:r + 1], ALU.add, ALU.mult, [G.r, rm.r], [Dp.r])
                v_ts(c, Dp[:], Dp[:], 1.0, None, ALU.add, None, [Dp.r], [Dp.r])
                v_tt(c, Sg[:], Sg[:], Dp[:].unsqueeze(2).to_broadcast([128, 4, 256]), ALU.mult, [Sg.r, Dp.r], [Sg.r])
                v_stt(c, Sg[:], G[:, 0:1024].rearrange("p (h d) -> p h d", h=4), rm[:, r:r + 1], Sg[:],
                      ALU.mult, ALU.add, [G.r, rm.r, Sg.r], [Sg.r])
        v_copy(c, Sgb[:], Sg[:], [Sg.r], [Sgb.r])
        xt = c.sb([128, D], F32)
        hb = c.sb([128, D], BF16)
        hT = c.sb([128, 8, NT], BF16)
        qTs = c.sb([128, 4, NT], F32)
        kTs = c.sb([128, 4, NT], F32)
        vsb = c.sb([128, 1024], BF16)
        sgt = c.sb([128, 1024], F32) if final else None
        glow = c.sb([128, 16], BF16)
        glT = c.sb([16, 128], BF16)
        gk = c.sb([128, 512], F32)
        eg = c.sb([128, 4, 128], F32)
        eng = c.sb([128, 4, 128], F32)
        qd = c.sb([128, 4, 128], BF16)
        kif = c.sb([128, 4, 128], F32)
        ki = c.sb([128, 4, 128], BF16)
        ke = c.sb([128, 4, 128], BF16)
        ketok = c.sb([128, 4, 128], BF16)
        if final:
            scm = c.sb([128, 4, 128], BF16)
            ssg = c.sb([128, 4], F32)
            otmp = c.sb([128, 256], F32)
            og = [c.sb([128, 1024], BF16) for _ in range(2)]
        for st in range(T // NT):
            for jt in range(2):
                ti = st * 2 + jt
                c.dma("sp", xt[:], xs[ti * 128:(ti + 1) * 128, :], [], [xt.r], "ex")
                self.norm_tile_T(xt, nw, hb, hT, jt * 128)
            for w_, dst_ in ((wq, qTs), (wk, kTs)):
                for hh in range(4):
                    pq = c.psum()
                    for k in range(8):
                        mm(c, pq[:, 0:NT], w_[:, k, hh * 128:(hh + 1) * 128], hT[:, k, :], k == 0, k == 7,
                           [w_.res[k], hT.r], [pq.r])
                    c.op("act", lambda a, pq=pq, hh=hh, dst_=dst_: a.copy(out=dst_[:, hh, :], in_=pq[:, 0:NT]),
                         [pq.r], [dst_.r])
            for jt in range(2):
                ti = st * 2 + jt
                ts = slice(jt * 128, (jt + 1) * 128)
                for n in range(2):
                    pv = c.psum()
                    for k in range(8):
                        mm(c, pv[:, :], hT[:, k, ts], wv[:, k, n * 512:(n + 1) * 512], k == 0, k == 7,
                           [hT.r, wv.res[k]], [pv.r])
                    c.op("act", lambda a, pv=pv, n=n: a.copy(out=vsb[:, n * 512:(n + 1) * 512], in_=pv[:, :]),
                         [pv.r], [vsb.r])
                    if final:
                        pg = c.psum()
                        for k in range(8):
                            mm(c, pg[:, :], hT[:, k, ts], wg[:, k, n * 512:(n + 1) * 512], k == 0, k == 7,
                               [hT.r, wg.res[k]], [pg.r])
                        a_act(c, sgt[:, n * 512:(n + 1) * 512], pg[:, :], AF.Silu, [pg.r], [sgt.r])
                pgl = c.psum()
                for k in range(8):
                    mm(c, pgl[:, 0:16], hT[:, k, ts], wgl[:, k, :], k == 0, k == 7, [hT.r, wgl.res[k]], [pgl.r])
                c.op("act", lambda a, pgl=pgl: a.copy(out=glow[:], in_=pgl[:, 0:16]), [pgl.r], [glow.r])
                ptg = c.psum()
                ptgv = ptg[:].bitcast(BF16)
                tr(c, ptgv[0:16, 0:128], glow[:], self.ident[:], [glow.r, self.ident.r], [ptg.r])
                c.op("act", lambda a, ptgv=ptgv: a.copy(out=glT[:], in_=ptgv[0:16, 0:128]), [ptg.r], [glT.r])
                pgk = c.psum()
                mm(c, pgk[:, :], glT[:], w2[:], True, True, [glT.r, w2.r], [pgk.r])
                v_tt(c, gk[:], pgk[:, :], gkb[:], ALU.add, [pgk.r, gkb.r], [gk.r])
                a_act(c, gk[:], gk[:], AF.Exp, [gk.r], [gk.r], scale=-1.0)
                a_act(c, gk[:], gk[:], AF.Ln, [gk.r], [gk.r], bias=1.0)
                v_ts(c, gk[:], gk[:], -1.0 / 16.0, None, ALU.mult, None, [gk.r], [gk.r])
                pgc = c.psum()
                for hh in range(4):
                    mm(c, pgc[:, hh * 128:(hh + 1) * 128], gk[:, hh * 128:(hh + 1) * 128], triu[:], True, True,
                       [gk.r, triu.r], [pgc.r])
                pgv = pgc[:, :].rearrange("p (h t) -> p h t", h=4)
                a_act(c, eg[:], pgv, AF.Exp, [pgc.r], [eg.r])
                a_act(c, eng[:], pgv, AF.Exp, [pgc.r], [eng.r], scale=-1.0)
                v_stt(c, qd[:], qTs[:, :, ts], 128.0 ** -0.5, eg[:], ALU.mult, ALU.mult, [qTs.r, eg.r], [qd.r])
                v_tt(c, kif[:], kTs[:, :, ts], eng[:], ALU.mult, [kTs.r, eng.r], [kif.r])
                v_copy(c, ki[:], kif[:], [kif.r], [ki.r], e="pool")
                v_tt(c, ke[:], kif[:], eg[:, :, 127:128].to_broadcast([128, 4, 128]), ALU.mult, [kif.r, eg.r], [ke.r])
                pke = c.psum()
                pkev = pke[:].bitcast(BF16)
                for hh in range(4):
                    tr(c, pkev[:, hh * 128:(hh + 1) * 128], ke[:, hh, :], self.ident[:], [ke.r, self.ident.r], [pke.r])
                c.op("act", lambda a, pkev=pkev: a.copy(out=ketok[:], in_=pkev[:, 0:512].rearrange("p (h t) -> p h t", h=4)),
                     [pke.r], [ketok.r])
                if final:
                    psc = c.psum()
                    for hh in range(4):
                        mm(c, psc[:, hh * 128:(hh + 1) * 128], ki[:, hh, :], qd[:, hh, :], True, True, [ki.r, qd.r],
                           [psc.r])
                    v_tt(c, scm[:], psc[:, :].rearrange("p (h t) -> p h t", h=4),
                         triu[:].unsqueeze(1).to_broadcast([128, 4, 128]), ALU.mult, [psc.r, triu.r], [scm.r])
                    po = [c.psum(), c.psum()]
                    for hh in range(4):
                        p_ = po[hh // 2]
                        col = (hh % 2) * 256
                        mm(c, p_[:, col:col + 256], scm[:, hh, :], vsb[:, hh * 256:(hh + 1) * 256], True, False,
                           [scm.r, vsb.r], [p_.r])
                        mm(c, p_[:, col:col + 256], qd[:, hh, :], Sgb[:, hh, :], False, True, [qd.r, Sgb.r], [p_.r])
                    for hh in range(4):
                        p_ = po[hh // 2]
                        col = (hh % 2) * 256
                        a_act(c, self.junk[:, 0:256], p_[:, col:col + 256], AF.Square, [p_.r], [self.junk.r, ssg.r],
                              accum_out=ssg[:, hh:hh + 1])
                    rstd_inplace(c, ssg[:], 256, ssg.r)
                    ob = og[ti % 2]
                    for hh in range(4):
                        p_ = po[hh // 2]
                        col = (hh % 2) * 256
                        v_stt(c, otmp[:], p_[:, col:col + 256], ssg[:, hh:hh + 1], gnw[:], ALU.mult, ALU.mult,
                              [p_.r, ssg.r, gnw.r], [otmp.r])
                        v_tt(c, ob[:, hh * 256:(hh + 1) * 256], otmp[:], sgt[:, hh * 256:(hh + 1) * 256], ALU.mult,
                             [otmp.r, sgt.r], [ob.r])
                    c.dma("sp", self.ycat_d[ti * 128:(ti + 1) * 128, 1024:2048], ob[:], [ob.r], [], ("yo", ti % 2))
                pkv = [c.psum(), c.psum()]
                for hh in range(4):
                    p_ = pkv[hh // 2]
                    col = (hh % 2) * 256
                    mm(c, p_[:, col:col + 256], ketok[:, hh, :], vsb[:, hh * 256:(hh + 1) * 256], True, True,
                       [ketok.r, vsb.r], [p_.r])
                    v_stt(c, Sg[:, hh, :], Sg[:, hh, :], eg[:, hh, 127:128], p_[:, col:col + 256], ALU.mult, ALU.add,
                          [Sg.r, eg.r, p_.r, Sgb.r], [Sg.r])
                v_copy(c, Sgb[:], Sg[:], [Sg.r], [Sgb.r])
                if not final:
                    v_tt(c, Dg[:], Dg[:], eg[:, :, 127], ALU.mult, [Dg.r, eg.r], [Dg.r])
        if not final:
            c.dma("sp", self.sg_l.ap()[:, 0:1024], Sg[:].rearrange("p h d -> p (h d)"), [Sg.r], [], "stl")
            c.dma("sp", self.sg_l.ap()[:, 1024:1028], Dg[:], [Dg.r], [], "stl2")

    def outproj_phase(self, j, xs):
        c, T, d = self.c, self.T, self.d
        wo = c.sb([128, 16, D], BF16, nres=16)
        for k in range(16):
            c.dma("pool", wo[:, k, :], d["w_out_even"][j, k * 128:(k + 1) * 128, :], [], [wo.res[k]], "w")
        yc = [c.sb([128, 2048], BF16) for _ in range(2)]
        ycT = c.sb([128, 16, 128], BF16)
        xt = [c.sb([128, D], F32) for _ in range(2)]
        xo = [c.sb([128, D], F32) for _ in range(2)]
        for ti in range(T // 128):
            b = ti % 2
            tsl = slice(ti * 128, (ti + 1) * 128)
            c.dma("sp", yc[b][:], self.ycat_d[tsl, :], [], [yc[b].r], ("opy", b))
            c.dma("sp", xt[b][:], xs[tsl, :], [], [xt[b].r], ("opx", b))
            for half in range(2):
                pt = c.psum()
                ptv = pt[:].bitcast(BF16)
                for kk in range(8):
                    k = half * 8 + kk
                    tr(c, ptv[:, kk * 128:(kk + 1) * 128], yc[b][:, k * 128:(k + 1) * 128], self.ident[:],
                       [yc[b].r, self.ident.r], [pt.r])
                c.op("act", lambda a, ptv=ptv, half=half: a.copy(
                    out=ycT[:, half * 8:half * 8 + 8, :], in_=ptv.rearrange("p (k t) -> p k t", k=8)), [pt.r], [ycT.r])
            for n in range(2):
                po = c.psum()
                for k in range(16):
                    mm(c, po[:, :], ycT[:, k, :], wo[:, k, n * 512:(n + 1) * 512], k == 0, k == 15,
                       [ycT.r, wo.res[k]], [po.r])
                v_tt(c, xo[b][:, n * 512:(n + 1) * 512], po[:, :], xt[b][:, n * 512:(n + 1) * 512], ALU.add,
                     [po.r, xt[b].r], [xo[b].r])
            c.dma("sp", xs[tsl, :], xo[b][:], [xo[b].r], [], ("opo", b))

    def even_layer(self, j, xs):
        import os
        c = self.c
        nph = int(os.environ.get("KPH", "99"))
        if nph >= 1:
            self.halo_phase(xs)
        steps = [lambda: self.ssd_pass(j, xs, False), lambda: self.gla_pass(j, xs, False),
                 lambda: (self.allgather(self.st_l, self.st_g), self.allgather(self.sg_l, self.sg_g)), lambda: self.ssd_pass(j, xs, True),
                 lambda: self.gla_pass(j, xs, True), lambda: self.outproj_phase(j, xs)]
        for i, f in enumerate(steps):
            if nph >= i + 2:
                with Phase(c):
                    f()

    def build_all(self, depth):
        c = self.c
        self.declare(depth)
        self.ycat_d = self.nc.dram_tensor("ycat_d", [self.T, 2048], BF16).ap()
        with Phase(c):
            self.setup_consts_inner()
            c.dma("sp", self.xs[:, :], self.d["x"][:, :], [], [], "xcp")
            z = c.sb([128, 2048], F32)
            c.op("dve", lambda v: v.memset(z[:], 0.0), [], [z.r])
            c.dma("sp", self.dly_a[:, :], z[:], [z.r], [], "xcp")
            c.dma("sp", self.dly_b[:, :], z[:], [z.r], [], "xcp")
            c.dma("sp", self.fn_l.ap()[:, :], z[0:16, 0:64], [z.r], [], "xcp")
            c.dma("sp", self.sg_l.ap()[:, :], z[:, 0:1040], [z.r], [], "xcp")
            c.dma("sp", self.st_l.ap()[:, :], z[:, 0:1040], [z.r], [], "xcp")
        for i in range(depth):
            if i % 2 == 0:
                self.even_layer(i // 2, self.xs)
            else:
                self.mla_layer(i // 2, self.xs)
            with Phase(c):
                dst = self.y if i == depth - 1 else self.xs
                self.ffn_phase(i, self.xs, None, dst, None, self.d["w_gate"], self.d["w_up"], self.d["w_down"],
                               self.d["ffn_norm"])
        return self.finish()

    def setup_consts_inner(self):
        c = self.c
        outer = c.stack
        c.stack = self.stack
        idf = c.sb([128, 128], F32)
        self.identf = idf
        self.ident = c.sb([128, 128], BF16)
        c.dma("sp", idf[:], self.d["c_ident"][:, :], [], [idf.r], "const")
        c.op("dve", lambda v: v.tensor_copy(out=self.ident[:], in_=idf[:]), [idf.r], [self.ident.r])
        self.junk = c.sb([128, 1024], BF16)
        self.ss = c.sb([128, 4], F32)
        c.stack = outer

    def finish(self):
        self.c.barrier()
        self.c.flush()
        self.stack.close()
        return self.nc


def _core_inputs(inputs, SEQ, depth):
    T = SEQ // 4
    NB = SEQ
    consts = {
        "c_ident": np.eye(128, dtype=np.float32),
        "c_inv": (1.0 / (np.float32(10000.0) ** (np.arange(0, 32, 2, dtype=np.float32) / np.float32(32)))).astype(np.float32),
        "c_kpos": (np.arange(NB // 128, dtype=np.float32)[None, :] * 128 + np.arange(128, dtype=np.float32)[:, None]).astype(np.float32),
        "c_triu": np.triu(np.ones((128, 128), dtype=np.float32)),
    }
    maps = []
    for core in range(8):
        b, r = core // 4, core % 4
        m = {}
        for k, v in inputs.items():
            if k == "x":
                continue
            m[k] = np.ascontiguousarray(np.asarray(v, dtype=np.float32))
        m["x"] = np.ascontiguousarray(np.asarray(inputs["x"])[b, r * T:(r + 1) * T, :], dtype=np.float32)
        m["pos"] = np.arange(r * T, (r + 1) * T, dtype=np.float32)
        rm = np.zeros((128, 4), np.float32)
        rm[:, :r] = 1.0
        hs = np.zeros((128, 4), np.float32)
        if r > 0:
            hs[:, r - 1] = 1.0
        m["rmask"], m["hsel"] = rm, hs
        m.update(consts)
        maps.append(m)
    return maps


_CACHE = {}


def kernel(**inputs):
    x = np.asarray(inputs["x"])
    B, SEQ, _ = x.shape
    depth = int(np.asarray(inputs["ffn_norm"]).shape[0])
    T = SEQ // 4
    key = (SEQ, depth)
    if key not in _CACHE:
        _CACHE[key] = Prog(T, None).build_all(depth)
    nc = _CACHE[key]
    maps = _core_inputs(inputs, SEQ, depth)
    res = run_bass_kernel_spmd(nc, maps, core_ids=list(range(8)))
    global _LAST
    _LAST = res
    out = np.empty((B, SEQ, D), dtype=np.float32)
    for core in range(8):
        b, r = core // 4, core % 4
        out[b, r * T:(r + 1) * T, :] = res.results[core]["y"]
    return out
```
